# Optimizing a Trainium2 kernel written in Bass

```python
import jax, jax.numpy as jnp
from jax import lax
import numpy as np

D_MODEL = 1024
BATCH = 8
SEQ = 2048
DEPTH = 2
DEC_BATCH = 128
DEC_SEQ = 8
PAST_LEN = 16384
PAGE_SIZE = 128

N_META = 16
RW_HEADS = 8
RW_HD = 64
D_RW = RW_HEADS * RW_HD
RW_W_LORA = 64
RW_A_LORA = 64
RW_G_LORA = 128
RW_SHIFT_W = 3 * D_RW + RW_W_LORA + RW_A_LORA + RW_G_LORA
RW_GN_EPS = 64e-5
ML_HEADS = 4
ML_HD = 128
D_ML = ML_HEADS * ML_HD
CONV_W = 4
ML_CHUNK = 64
ML_IN_W = 4 * D_ML + 2 * ML_HEADS
ML_GN_EPS = 1e-5
N_IN = RW_SHIFT_W + ML_IN_W + 2 * D_MODEL
D_FF = 4 * D_MODEL
RMS_EPS = 1e-6

kernel_name = "rwkv7_mlstm_gated_hybrid_step"

F32 = jnp.float32


def _split(a, sizes):
    idx = [int(i) for i in np.cumsum(sizes)[:-1]]
    return jnp.split(a, idx, axis=-1)


def _rms_norm(x, g):
    xf = x.astype(F32)
    y = xf * lax.rsqrt(jnp.mean(xf * xf, axis=-1, keepdims=True) + RMS_EPS)
    return (y * g.astype(F32)).astype(x.dtype)


def _head_norm(x, g, eps):
    mu = jnp.mean(x, axis=-1, keepdims=True)
    xc = x - mu
    var = jnp.mean(xc * xc, axis=-1, keepdims=True)
    return xc * lax.rsqrt(var + eps) * g.astype(F32)


def _rwkv7_step(S, inp):
    r, w, k, v, kk, a = inp
    sa = jnp.einsum('bhvk,bhk->bhv', S, -kk)
    S = S * w[:, :, None, :] + sa[..., None] * (kk * a)[:, :, None, :] + v[..., None] * k[:, :, None, :]
    return S, jnp.einsum('bhvk,bhk->bhv', S, r)


def _rwkv7_branch(z, shift_buf, S0, mu, w0, w_up, a0, a_up, g_up, k_k, k_a, r_k, gn_g, gn_b):
    B, T, _ = z.shape
    z = z.astype(F32)
    prev = jnp.concatenate([shift_buf[:, None].astype(F32), z[:, :-1]], axis=1)
    zs = z + (prev - z) * mu.astype(F32)
    r, k, v, wl, al, gl = _split(zs, [D_RW, D_RW, D_RW, RW_W_LORA, RW_A_LORA, RW_G_LORA])
    w = -jax.nn.softplus(-(w0.astype(F32) + jnp.tanh(wl) @ w_up.astype(F32))) - 0.5
    decay = jnp.exp(-jnp.exp(w))
    a = jax.nn.sigmoid(a0.astype(F32) + al @ a_up.astype(F32))
    g = jax.nn.sigmoid(gl) @ g_up.astype(F32)
    heads = lambda t: t.reshape(B, T, RW_HEADS, RW_HD)
    kk = heads(k * k_k.astype(F32))
    kk = kk * lax.rsqrt(jnp.maximum(jnp.sum(kk * kk, axis=-1, keepdims=True), 1e-24))
    k = k * (1.0 + (a - 1.0) * k_a.astype(F32))
    r, decay, k, v, a = heads(r), heads(decay), heads(k), heads(v), heads(a)
    tm = lambda t: jnp.swapaxes(t, 0, 1)
    S_T, y = lax.scan(_rwkv7_step, S0.astype(F32), (tm(r), tm(decay), tm(k), tm(v), tm(kk), tm(a)))
    y = tm(y)
    y = _head_norm(y, gn_g, RW_GN_EPS) + gn_b.astype(F32)
    y = y + jnp.sum(r * k * r_k.astype(F32), axis=-1, keepdims=True) * v
    y = y.reshape(B, T, D_RW) * g
    return y, z[:, -1], S_T


def _mlstm_chunk(carry, inp):
    C, n, m = carry
    q, k, v, ig, lf = inp
    L = q.shape[1]
    bt = jnp.swapaxes(jnp.cumsum(lf, axis=1), 1, 2)
    igt = jnp.swapaxes(ig, 1, 2)
    causal = jnp.tril(jnp.ones((L, L), dtype=bool))
    dlog = jnp.where(causal, bt[..., :, None] - bt[..., None, :] + igt[..., None, :], -jnp.inf)
    inter = bt + m[..., None]
    m_t = jnp.maximum(inter, jnp.max(dlog, axis=-1))
    A = jnp.exp(dlog - m_t[..., None])
    sc = jnp.exp(inter - m_t)
    aqk = A * jnp.einsum('blhd,bshd->bhls', q, k)
    num = jnp.einsum('bhls,bshd->blhd', aqk, v) + jnp.swapaxes(sc, 1, 2)[..., None] * jnp.einsum('bhvk,blhk->blhv', C, q)
    den = jnp.sum(aqk, axis=-1) + sc * jnp.einsum('bhk,blhk->bhl', n, q)
    denom = jnp.maximum(jnp.abs(den), jnp.exp(-m_t))
    h = num / jnp.swapaxes(denom, 1, 2)[..., None]
    m_new = m_t[..., -1]
    wc = jnp.exp(bt[..., -1:] - bt + igt - m_new[..., None])
    dec = jnp.exp(bt[..., -1] + m - m_new)
    C_new = dec[..., None, None] * C + jnp.einsum('bhs,bshv,bshk->bhvk', wc, v, k)
    n_new = dec[..., None] * n + jnp.einsum('bhs,bshk->bhk', wc, k)
    return (C_new, n_new, m_new), h


def _mlstm_run(state, seqs, chunk):
    B, T = seqs[0].shape[:2]
    nc = T // chunk
    blk = lambda a: jnp.swapaxes(a.reshape((B, nc, chunk) + a.shape[2:]), 0, 1)
    state, h = lax.scan(_mlstm_chunk, state, tuple(blk(a) for a in seqs))
    h = jnp.swapaxes(h, 0, 1).reshape((B, T) + h.shape[3:])
    return state, h


def _mlstm_branch(z, conv_buf, C0, n0, m0, conv_w, conv_b, i_bias, f_bias, gn_g, n_lead, chunk):
    B, T, _ = z.shape
    z = z.astype(F32)
    qk_raw, v, o, ig, fg = _split(z, [2 * D_ML, D_ML, D_ML, ML_HEADS, ML_HEADS])
    xp = jnp.concatenate([conv_buf.astype(F32), qk_raw], axis=1)
    cw = conv_w.astype(F32)
    qk = conv_b.astype(F32) + xp[:, 0:T] * cw[0]
    for j in range(1, CONV_W):
        qk = qk + xp[:, j:j + T] * cw[j]
    qk = jax.nn.silu(qk)
    q, k = jnp.split(qk, 2, axis=-1)
    heads = lambda t: t.reshape(B, T, ML_HEADS, ML_HD)
    q, k, v = heads(q), heads(k) * (ML_HD ** -0.5), heads(v)
    ig = ig + i_bias.astype(F32)
    lf = jax.nn.log_sigmoid(fg + f_bias.astype(F32))
    state = (C0.astype(F32), n0.astype(F32), m0.astype(F32))
    seqs = (q, k, v, ig, lf)
    if n_lead:
        state, h_lead = _mlstm_run(state, tuple(s[:, :n_lead] for s in seqs), n_lead)
        state, h_rest = _mlstm_run(state, tuple(s[:, n_lead:] for s in seqs), chunk)
        h = jnp.concatenate([h_lead, h_rest], axis=1)
    else:
        state, h = _mlstm_run(state, seqs, chunk)
    h = _head_norm(h, gn_g, ML_GN_EPS).reshape(B, T, D_ML) * jax.nn.sigmoid(o)
    return h, xp[:, -(CONV_W - 1):], state


def _layer(x, st, lp, n_lead, chunk):
    S0, sh0, C0, n0, m0, cb0 = st
    u = _rms_norm(x, lp['pre1'])
    z = u @ lp['w_in']
    z_rw, z_ml, z_gate = _split(z, [RW_SHIFT_W, ML_IN_W, 2 * D_MODEL])
    ya, sh1, S1 = _rwkv7_branch(z_rw, sh0, S0, lp['rw_mu'], lp['rw_w0'], lp['rw_w_up'], lp['rw_a0'],
                                lp['rw_a_up'], lp['rw_g_up'], lp['rw_k_k'], lp['rw_k_a'], lp['rw_r_k'],
                                lp['rw_gn_g'], lp['rw_gn_b'])
    yb, cb1, (C1, n1, m1) = _mlstm_branch(z_ml, cb0, C0, n0, m0, lp['ml_conv_w'], lp['ml_conv_b'],
                                          lp['ml_i_bias'], lp['ml_f_bias'], lp['ml_gn_g'], n_lead, chunk)
    ga, gb = jnp.split(jax.nn.sigmoid(z_gate.astype(F32)), 2, axis=-1)
    merged = ga * (ya @ lp['p_a'].astype(F32)) + gb * (yb @ lp['p_b'].astype(F32))
    h = x + _rms_norm(merged.astype(x.dtype) @ lp['w_out'], lp['post1'])
    u2 = _rms_norm(h, lp['pre2'])
    f = jnp.square(jax.nn.relu(u2 @ lp['w_ff_up'])) @ lp['w_ff_down']
    out = h + _rms_norm(f, lp['post2'])
    dt = x.dtype
    return out, (S1.astype(dt), sh1.astype(dt), C1.astype(dt), n1.astype(dt), m1.astype(dt), cb1.astype(dt))


def setup_inputs(seed: int = 0) -> dict:
    key = jax.random.key(seed)
    ks = jax.random.split(key, 40)
    nrm = lambda i, shape, s: jax.random.normal(ks[i], shape, F32) * s
    L = DEPTH
    d = {}
    d['x_prompt'] = nrm(0, (BATCH, SEQ, D_MODEL), 1.0)
    d['x_sample'] = nrm(1, (DEC_BATCH, DEC_SEQ, D_MODEL), 1.0)
    d['state_rwkv_S'] = nrm(2, (L, DEC_BATCH, RW_HEADS, RW_HD, RW_HD), 0.1)
    d['state_rwkv_shift'] = nrm(3, (L, DEC_BATCH, RW_SHIFT_W), 1.0)
    d['state_mlstm_C'] = nrm(4, (L, DEC_BATCH, ML_HEADS, ML_HD, ML_HD), 0.3)
    d['state_mlstm_n'] = nrm(5, (L, DEC_BATCH, ML_HEADS, ML_HD), 0.3)
    d['state_mlstm_m'] = nrm(6, (L, DEC_BATCH, ML_HEADS), 0.5)
    d['state_mlstm_conv'] = nrm(7, (L, DEC_BATCH, CONV_W - 1, 2 * D_ML), 1.0)
    d['meta_tokens'] = nrm(8, (N_META, D_MODEL), 1.0)
    d['w_in'] = nrm(9, (L, D_MODEL, N_IN), D_MODEL ** -0.5)
    d['rw_mu'] = jax.random.uniform(ks[10], (L, RW_SHIFT_W), F32)
    d['rw_w0'] = jax.random.uniform(ks[11], (L, D_RW), F32, -6.0, 1.0)
    d['rw_w_up'] = nrm(12, (L, RW_W_LORA, D_RW), 0.5 * RW_W_LORA ** -0.5)
    d['rw_a0'] = nrm(13, (L, D_RW), 0.5)
    d['rw_a_up'] = nrm(14, (L, RW_A_LORA, D_RW), 0.5 * RW_A_LORA ** -0.5)
    d['rw_g_up'] = nrm(15, (L, RW_G_LORA, D_RW), RW_G_LORA ** -0.5)
    d['rw_k_k'] = 0.85 + nrm(16, (L, D_RW), 0.02)
    d['rw_k_a'] = 1.0 + nrm(17, (L, D_RW), 0.02)
    d['rw_r_k'] = nrm(18, (L, RW_HEADS, RW_HD), 0.1)
    d['rw_gn_g'] = 1.0 + nrm(19, (L, RW_HEADS, RW_HD), 0.02)
    d['rw_gn_b'] = nrm(20, (L, RW_HEADS, RW_HD), 0.02)
    d['ml_conv_w'] = nrm(21, (L, CONV_W, 2 * D_ML), CONV_W ** -0.5)
    d['ml_conv_b'] = nrm(22, (L, 2 * D_ML), 0.01)
    d['ml_i_bias'] = nrm(23, (L, ML_HEADS), 0.1)
    d['ml_f_bias'] = jnp.linspace(3.0, 6.0, ML_HEADS, dtype=F32)[None] + nrm(24, (L, ML_HEADS), 0.1)
    d['ml_gn_g'] = 1.0 + nrm(25, (L, ML_HEADS, ML_HD), 0.02)
    d['p_a'] = nrm(26, (L, D_RW, D_MODEL), D_RW ** -0.5)
    d['p_b'] = nrm(27, (L, D_ML, D_MODEL), D_ML ** -0.5)
    d['w_out'] = nrm(28, (L, D_MODEL, D_MODEL), D_MODEL ** -0.5)
    d['pre1'] = 1.0 + nrm(29, (L, D_MODEL), 0.02)
    d['post1'] = 1.0 + nrm(30, (L, D_MODEL), 0.02)
    d['pre2'] = 1.0 + nrm(31, (L, D_MODEL), 0.02)
    d['post2'] = 1.0 + nrm(32, (L, D_MODEL), 0.02)
    d['w_ff_up'] = nrm(33, (L, D_MODEL, D_FF), D_MODEL ** -0.5)
    d['w_ff_down'] = nrm(34, (L, D_FF, D_MODEL), D_FF ** -0.5)
    return d


def reference(x_prompt, x_sample, state_rwkv_S, state_rwkv_shift, state_mlstm_C, state_mlstm_n,
              state_mlstm_m, state_mlstm_conv, meta_tokens, w_in, rw_mu, rw_w0, rw_w_up, rw_a0, rw_a_up,
              rw_g_up, rw_k_k, rw_k_a, rw_r_k, rw_gn_g, rw_gn_b, ml_conv_w, ml_conv_b, ml_i_bias, ml_f_bias,
              ml_gn_g, p_a, p_b, w_out, pre1, post1, pre2, post2, w_ff_up, w_ff_down):
    B = x_prompt.shape[0]
    dt = x_prompt.dtype
    xp = jnp.concatenate([jnp.broadcast_to(meta_tokens[None].astype(dt), (B, N_META, D_MODEL)), x_prompt], axis=1)
    xs = x_sample
    zero_state = (jnp.zeros((B, RW_HEADS, RW_HD, RW_HD), F32), jnp.zeros((B, RW_SHIFT_W), F32),
                  jnp.zeros((B, ML_HEADS, ML_HD, ML_HD), F32), jnp.zeros((B, ML_HEADS, ML_HD), F32),
                  jnp.zeros((B, ML_HEADS), F32), jnp.zeros((B, CONV_W - 1, 2 * D_ML), F32))
    p_states, s_states = [], []
    for l in range(DEPTH):
        lp = dict(w_in=w_in[l], rw_mu=rw_mu[l], rw_w0=rw_w0[l], rw_w_up=rw_w_up[l], rw_a0=rw_a0[l],
                  rw_a_up=rw_a_up[l], rw_g_up=rw_g_up[l], rw_k_k=rw_k_k[l], rw_k_a=rw_k_a[l], rw_r_k=rw_r_k[l],
                  rw_gn_g=rw_gn_g[l], rw_gn_b=rw_gn_b[l], ml_conv_w=ml_conv_w[l], ml_conv_b=ml_conv_b[l],
                  ml_i_bias=ml_i_bias[l], ml_f_bias=ml_f_bias[l], ml_gn_g=ml_gn_g[l], p_a=p_a[l], p_b=p_b[l],
                  w_out=w_out[l], pre1=pre1[l], post1=post1[l], pre2=pre2[l], post2=post2[l],
                  w_ff_up=w_ff_up[l], w_ff_down=w_ff_down[l])
        xp, st_p = _layer(xp, zero_state, lp, N_META, ML_CHUNK)
        st_in = (state_rwkv_S[l], state_rwkv_shift[l], state_mlstm_C[l], state_mlstm_n[l],
                 state_mlstm_m[l], state_mlstm_conv[l])
        xs, st_s = _layer(xs, st_in, lp, 0, xs.shape[1])
        p_states.append(st_p)
        s_states.append(st_s)
    stk = lambda lst, i: jnp.stack([s[i] for s in lst])
    y_prompt = xp[:, N_META:]
    return (y_prompt, xs,
            stk(p_states, 0), stk(p_states, 1), stk(p_states, 2), stk(p_states, 3), stk(p_states, 4), stk(p_states, 5),
            stk(s_states, 0), stk(s_states, 1), stk(s_states, 2), stk(s_states, 3), stk(s_states, 4), stk(s_states, 5))
```

```python
import numpy as np
from contextlib import ExitStack
import concourse.bass as bass
import concourse.mybir as mybir
from concourse.bass_utils import run_bass_kernel_spmd

F32 = mybir.dt.float32
BF16 = mybir.dt.bfloat16
AF = mybir.ActivationFunctionType
ALU = mybir.AluOpType

NDS = 40
NWS = 6
NGS = 8
RDT = BF16
F32R = mybir.dt.float32r
USE_F32R = True

D = 1024
NCH = 8
NIN = 5896
SEQ = 2048
NMETA = 16
DEPTH = 2
NS = 16
TSL = 8
SLABW = 1032
SLABS_IN = [(0, 1024), (1024, 1792), (1792, 2816), (2816, 3848), (3848, 4872), (4872, 5896)]
RMS_EPS = 1e-6
RW_GN_EPS = 64e-5
ML_GN_EPS = 1e-5
NPRM = 118
PR = {}
_o = 0
for _n, _c in [('pre1', 8), ('post1', 8), ('pre2', 8), ('post2', 8), ('mu', 14), ('w0', 4), ('a0', 4), ('kk', 4),
               ('ka', 4), ('rk', 4), ('gng', 4), ('gnb', 4), ('cw', 32), ('cb', 8), ('mlg', 4)]:
    PR[_n] = _o
    _o += _c
assert _o == NPRM
CONST_NAMES = ['ident', 'ones', 'blk', 'mlt64', 'mgt64', 'mle64', 'mneg64', 'mlt16', 'mgt16', 'mle16', 'mneg16',
               'mlt8', 'mgt8', 'mle8', 'mneg8']


class Prog:
    ENGS = ['pe', 'act', 'dve', 'pool', 'sp']

    def __init__(self, nc, block, es):
        self.nc = nc
        self.bfn = {'pe': block.tensor, 'act': block.scalar, 'dve': block.vector,
                    'pool': block.gpsimd, 'sp': block.sync}
        self.semh = {}
        for e in ['pe', 'act', 'dve', 'pool']:
            self.semh[e] = es.enter_context(nc.semaphore("sem_" + e))
        for i in range(NDS):
            self.semh['d%d' % i] = es.enter_context(nc.semaphore("dsem%d" % i))
        for i in range(NWS):
            self.semh['w%d' % i] = es.enter_context(nc.semaphore("wsem%d" % i))
        for i in range(NGS):
            self.semh['g%d' % i] = es.enter_context(nc.semaphore("gsem%d" % i))
        self.gnext = 0
        self.cnt = {k: 0 for k in self.semh}
        self.dnext = 0
        self.wnext = 0
        self.buf = {e: [] for e in self.ENGS}
        self.lastw = {}
        self.readers = {}
        self.seen = {e: {} for e in self.ENGS}
        self.nins = 0
        self.clock = {}
        self.tokseq = {}
        self.gseq = 0
        self.nwaits = 0

    KEYMAP = {}

    @staticmethod
    def _k(x):
        if isinstance(x, (str, tuple)):
            return x
        return Prog.KEYMAP.get(x.name, x.name)

    def _deps(self, eng, reads, writes):
        need = {}
        for k in reads:
            w = self.lastw.get(k)
            if w is not None and need.get(w[0], 0) < w[1]:
                need[w[0]] = w[1]
        for k in writes:
            w = self.lastw.get(k)
            if w is not None and need.get(w[0], 0) < w[1]:
                need[w[0]] = w[1]
            for sid, v in self.readers.get(k, {}).items():
                if need.get(sid, 0) < v:
                    need[sid] = v
        out = []
        seen = self.seen[eng]
        items = sorted(need.items(), key=lambda kv: -self.tokseq.get(kv, 0))
        for sid, v in items:
            if seen.get(sid, 0) >= v:
                continue
            seen[sid] = v
            out.append((sid, v))
            ck = self.clock.get((sid, v))
            if ck:
                for s2, v2 in ck.items():
                    if seen.get(s2, 0) < v2:
                        seen[s2] = v2
        out.reverse()
        return out

    def _commit(self, tok, reads, writes, eng=None):
        self.gseq += 1
        self.tokseq[tok] = self.gseq
        if eng is not None:
            self.clock[tok] = dict(self.seen[eng])
        for k in reads:
            d = self.readers.setdefault(k, {})
            if d.get(tok[0], 0) < tok[1]:
                d[tok[0]] = tok[1]
        for k in writes:
            self.lastw[k] = tok
            self.readers[k] = {}

    def op(self, eng, fn, reads=(), writes=(), attach=None, multi=False):
        reads = [self._k(x) for x in reads]
        writes = [self._k(x) for x in writes]
        waits = self._deps(eng, reads, writes)
        self.nwaits += len(waits)
        self.cnt[eng] += 1
        tok = (eng, self.cnt[eng])
        self._commit(tok, reads, writes, eng)
        semh = self.semh
        sem = semh[eng]
        if attach is None:
            attach = eng in ('act', 'dve', 'pe')

        def run(e):
            if attach and waits:
                for sid, v in waits[:-1]:
                    e.wait_ge(semh[sid], v)
                if multi:
                    done = [False]

                    def hook(ins):
                        if not done[0]:
                            ins._wait_ge(semh[waits[-1][0]], waits[-1][1])
                            done[0] = True
                        return ins
                    fn(e, hook).then_inc(sem, 1)
                else:
                    ins = fn(e)
                    ins._wait_ge(semh[waits[-1][0]], waits[-1][1])
                    ins.then_inc(sem, 1)
            elif multi:
                for sid, v in waits:
                    e.wait_ge(semh[sid], v)
                fn(e, lambda ins: ins).then_inc(sem, 1)
            else:
                for sid, v in waits:
                    e.wait_ge(semh[sid], v)
                fn(e).then_inc(sem, 1)
        self.buf[eng].append(run)
        self.nins += 1

    def dma(self, q, out, in_, reads=(), writes=(), slab=False, **kw):
        reads = [self._k(x) for x in reads]
        writes = [self._k(x) for x in writes]
        if slab:
            sid = 'w%d' % self.wnext
            self.wnext = (self.wnext + 1) % NWS
        elif q == 'pool':
            sid = 'g%d' % self.gnext
            self.gnext = (self.gnext + 1) % NGS
        else:
            sid = 'd%d' % self.dnext
            self.dnext = (self.dnext + 1) % NDS
        waits = self._deps(q, reads, writes)
        prev = self.cnt[sid]
        if prev > 0 and self.seen[q].get(sid, 0) < prev:
            self.seen[q][sid] = prev
            waits.append((sid, prev))
        self.cnt[sid] = prev + 16
        tok = (sid, prev + 16)
        self._commit(tok, reads, writes, q)
        semh = self.semh

        def run(e):
            for s, v in waits:
                e.wait_ge(semh[s], v)
            e.dma_start(out=out, in_=in_, **kw).then_inc(semh[sid], 16)
        self.buf[q].append(run)
        self.nins += 1

    def barrier(self, engines=('act', 'dve', 'sp'), with_w=False):
        for e in engines:
            waits = []
            for sid, v in self.cnt.items():
                if v == 0 or (sid.startswith('w') and not with_w):
                    continue
                if self.seen[e].get(sid, 0) >= v:
                    continue
                self.seen[e][sid] = v
                waits.append((sid, v))
            if waits:
                semh = self.semh

                def run(en, waits=waits):
                    for s, v in waits:
                        en.wait_ge(semh[s], v)
                self.buf[e].append(run)

    def flush(self):
        for e in self.ENGS:
            fns = self.buf[e]
            if not fns:
                continue
            self.buf[e] = []

            def body(en, fns=fns):
                for f in fns:
                    f(en)
            self.bfn[e](body)


class Chunk:
    pass


class StopBuild(Exception):
    pass


def build(debug=False, stop=None):
    nc = bass.Bass("TRN2", target_bir_lowering=False)
    stage_i = [0]
    I = {}

    def din(name, shape):
        I[name] = nc.dram_tensor(name, list(shape), F32, kind="ExternalInput").ap()
        return I[name]

    O = {}

    def dout(name, shape):
        O[name] = nc.dram_tensor(name, list(shape), F32, kind="ExternalOutput").ap()
        return O[name]

    din('xp', (SEQ, D)); din('xs', (NS * TSL, D))
    din('sS', (DEPTH, NS, 8, 64, 64)); din('ssh', (DEPTH, NS, 1792)); din('sC', (DEPTH, NS, 4, 128, 128))
    din('sn', (DEPTH, NS, 4, 128)); din('sm', (DEPTH, NS * 4)); din('scv', (DEPTH, NS * 3, D))
    din('meta', (NMETA, D)); din('w_in', (DEPTH, D, NIN)); din('w_up', (DEPTH, 64, 512)); din('a_up', (DEPTH, 64, 512))
    din('g_up', (DEPTH, 128, 512)); din('p_a', (DEPTH, 512, D)); din('p_b', (DEPTH, 512, D)); din('w_out', (DEPTH, D, D))
    din('w_ff_up', (DEPTH, D, 4 * D)); din('w_ff_down', (DEPTH, 4 * D, D))
    din('prm', (DEPTH, NPRM, 128)); din('gbias', (1, 16)); din('consts', (128, len(CONST_NAMES) * 128))

    dout('yp', (SEQ, D)); dout('ys', (NS * TSL, D))
    dout('pS', (DEPTH, 8, 64, 64)); dout('psh', (DEPTH, 14, 128)); dout('pC', (DEPTH, 4, 128, 128)); dout('pn', (DEPTH, 4, 128))
    dout('pm', (DEPTH, 4)); dout('pcv', (DEPTH, 3, D))
    dout('oS', (DEPTH, NS, 8, 64, 64)); dout('osh', (DEPTH, NS, 1792)); dout('oC', (DEPTH, NS, 4, 128, 128))
    dout('on', (DEPTH, NS * 4, 128)); dout('om', (DEPTH, NS * 4)); dout('ocv', (DEPTH, NS * 3, D))
    dbg = {}

    with ExitStack() as es:
        used = {}

        def T(name, shape, dt=F32, st=None):
            n = used.get(name, 0)
            used[name] = n + 1
            un = name if n == 0 else "%s_r%d" % (name, n)
            Prog.KEYMAP[un] = name
            return (st if st is not None else es).enter_context(nc.sbuf_tensor(un, list(shape), dt))

        cst = T('cst', [128, len(CONST_NAMES), 128])
        C = {n: cst[:, i, :] for i, n in enumerate(CONST_NAMES)}
        ones_bf = T('ones_bf', [128, 128], BF16)
        blk_bf = T('blk_bf', [128, 128], BF16)
        ident_bf = T('ident_bf', [128, 128], BF16)
        pc = T('pc', [128, DEPTH, 128])
        npc = T('npc', [128, DEPTH, 8])
        gb = T('gb', [128, 16])
        ngb = T('ngb', [128, 16])
        lw_wa = T('lw_wa', [128, DEPTH, 512], BF16)
        lw_g = T('lw_g', [128, DEPTH, 512], BF16)
        slabs = [T('slab%d' % i, [128, NCH, SLABW], BF16) for i in range(2)]
        xT = T('xT', [128, NCH, 512])
        uT = T('uT', [128, NCH, 512], BF16)
        yaT = T('yaT', [128, 4, 512], BF16)
        ybT = T('ybT', [128, 4, 512], BF16)
        Sbd = [[T('Sbd%d_%d' % (l, hp), [128, 128]) for hp in range(4)] for l in range(DEPTH)]
        CTp = [T('CT%d' % l, [128, 4, 132]) for l in range(DEPTH)]
        mP = T('mP', [128, DEPTH, 4])
        shc = T('shc', [128, DEPTH, 14])
        cvc = T('cvc', [128, DEPTH, 8, 3])
        stg = [T('stg%d' % i, [128, D]) for i in range(2)]
        pss = [es.enter_context(nc.psum_tensor('psb%d' % i, [128, 512], F32)) for i in range(8)]

        block = es.enter_context(nc.Block())
        P = Prog(nc, block, es)

        def stage(name):
            stage_i[0] += 1
            if stop is not None and stage_i[0] >= stop:
                print("STOP at stage", stage_i[0], name)
                P.barrier(engines=('pe', 'act', 'dve', 'pool', 'sp'), with_w=True)
                P.flush()
                raise StopBuild()

        st = {'big': 0, 'small': 0, 'slab': 0, 'stg': 0, 'alt': 0}

        def psbig():
            i = st['big']; st['big'] = (i + 1) % 8
            return pss[i], ('psB', i)

        def pssmall():
            return psbig()

        def kn(x):
            return Prog._k(x)

        def TT(eng, out, in0, in1, op, r, w):
            P.op(eng, lambda e: e.tensor_tensor(out=out, in0=in0, in1=in1, op=op), r, w)

        def TS(eng, out, in0, s1, op0, r, w, s2=None, op1=None):
            if op1 is None:
                P.op(eng, lambda e: e.tensor_scalar(out=out, in0=in0, scalar1=s1, scalar2=None, op0=op0), r, w)
            else:
                P.op(eng, lambda e: e.tensor_scalar(out=out, in0=in0, scalar1=s1, scalar2=s2, op0=op0, op1=op1), r, w)

        def STT(eng, out, in0, scalar, in1, op0, op1, r, w, accum_out=None):
            if accum_out is None:
                P.op(eng, lambda e: e.scalar_tensor_tensor(out=out, in0=in0, scalar=scalar, in1=in1, op0=op0, op1=op1), r, w)
            else:
                P.op(eng, lambda e: e.scalar_tensor_tensor(out=out, in0=in0, scalar=scalar, in1=in1, op0=op0, op1=op1,
                                                           accum_out=accum_out), r, w)

        def ACT(out, in_, func, r, w, bias=0.0, scale=1.0):
            P.op('act', lambda e: e.activation(out=out, in_=in_, func=func, bias=bias, scale=scale), r, w)

        def CP(eng, out, in_, r, w):
            if eng == 'act':
                P.op('act', lambda e: e.copy(out=out, in_=in_), r, w)
            else:
                P.op(eng, lambda e: e.tensor_copy(out=out, in_=in_), r, w)

        def CPalt(out, in_, r, w):
            st['alt'] ^= 1
            CP('act' if st['alt'] else 'dve', out, in_, r, w)

        def MSET(eng, ap, val, w):
            P.op(eng, lambda e: e.memset(ap, val), (), w)

        def SCAN(out, d0, d1, init, op0, op1, r, w):
            P.op('dve', lambda e: e.tensor_tensor_scan(out=out, data0=d0, data1=d1, initial=init, op0=op0, op1=op1), r, w)

        def RECIP(out, in_, r, w):
            P.op('dve', lambda e: e.reciprocal(out=out, in_=in_), r, w)

        def MM(out, pairs, r, w):
            def fn(e, hook):
                n = len(pairs)
                ins = None
                for i, (l_, r_) in enumerate(pairs):
                    ins = hook(e.matmul(out, lhsT=l_, rhs=r_, start=(i == 0), stop=(i == n - 1)))
                return ins
            P.op('pe', fn, r, w, multi=True)

        def MMB(trip, r, w):
            def fn(e, hook):
                ins = None
                for (o_, l_, r_) in trip:
                    ins = hook(e.matmul(o_, lhsT=l_, rhs=r_, start=True, stop=True))
                return ins
            P.op('pe', fn, r, w, multi=True)

        def MMG(groups_, r, w):
            def fn(e, hook):
                ins = None
                for (o_, pairs) in groups_:
                    n = len(pairs)
                    for i, (l_, r_) in enumerate(pairs):
                        ins = hook(e.matmul(o_, lhsT=l_, rhs=r_, start=(i == 0), stop=(i == n - 1)))
                return ins
            P.op('pe', fn, r, w, multi=True)

        def TRB(pairs, r, w):
            def fn(e, hook):
                ins = None
                for (o_, i_) in pairs:
                    ins = hook(e.transpose(out=o_, in_=i_, identity=C['ident'][:, :]))
                return ins
            P.op('pe', fn, list(r) + ['cst'], w, multi=True)

        def TR(out, in_, n_in_part, r, w):
            P.op('pe', lambda e: e.transpose(out=out, in_=in_, identity=C['ident'][0:n_in_part, 0:n_in_part]), r + ['cst'], w)

        def dump(name, ap, shape, r):
            if not debug:
                return
            dbg[name] = dout('dbg_' + name, shape)
            P.dma('pool', dbg[name], ap, reads=r)

        def load_slab(src2d, r0, nrow_chunks, c0, c1):
            i = st['slab']; st['slab'] = (i + 1) % 2
            sl = slabs[i]
            src = src2d[r0:r0 + nrow_chunks * 128, c0:c1].rearrange("(c p) n -> p c n", p=128)
            P.dma('pool', sl[:, 0:nrow_chunks, 0:c1 - c0], src, writes=[sl], slab=True)
            return sl

        def to_fm(dst_fn, rows_ap, R, F, dst_keys):
            i = st['stg']; st['stg'] ^= 1
            sg = stg[i]
            P.dma('sp', sg[0:R, 0:F], rows_ap, writes=[sg])
            for j in range(F // 128):
                ps, pk = pssmall()
                TR(ps[:, 0:R], sg[0:R, j * 128:(j + 1) * 128], R, [sg], [pk])
                CPalt(dst_fn(j), ps[:, 0:R], [pk], dst_keys)

        def from_fm(rows_ap, src_fn, R, F, src_keys):
            i = st['stg']; st['stg'] ^= 1
            sg = stg[i]
            for j in range(F // 128):
                ps, pk = pssmall()
                TR(ps[0:R, 0:128], src_fn(j), 128, src_keys, [pk])
                CPalt(sg[0:R, j * 128:(j + 1) * 128], ps[0:R, 0:128], [pk], [sg])
            P.dma('sp', rows_ap, sg[0:R, 0:F], reads=[sg])

        def x_to_fm(rows_ap, R, t0):
            i = st['stg']; st['stg'] ^= 1
            sg = stg[i]
            P.dma('sp', sg[0:R, 0:D], rows_ap, writes=[sg])
            for j0 in (0, 4):
                ps, pk = psbig()
                P.op('pe', lambda e, hook, ps=ps, j0=j0: [hook(e.transpose(out=ps[:, k * R:(k + 1) * R], in_=sg[0:R, (j0 + k) * 128:(j0 + k + 1) * 128],
                                                                      identity=C['ident'][0:R, 0:R])) for k in range(4)][-1], [sg, 'cst'], [pk], multi=True)
                CPalt(xT[:, j0:j0 + 4, t0:t0 + R], ps[:, 0:4 * R].rearrange("p (k r) -> p k r", r=R), [pk], [xT])

        def x_from_fm(rows_ap, R, t0):
            i = st['stg']; st['stg'] ^= 1
            sg = stg[i]
            for j0 in (0, 4):
                ps, pk = psbig()
                P.op('pe', lambda e, hook, ps=ps, j0=j0: [hook(e.transpose(out=ps[0:R, k * 128:(k + 1) * 128], in_=xT[:, j0 + k, t0:t0 + R],
                                                                      identity=C['ident'][:, :])) for k in range(4)][-1], [xT, 'cst'], [pk], multi=True)
                CPalt(sg[0:R, j0 * 128:(j0 + 4) * 128], ps[0:R, 0:512], [pk], [sg])
            P.dma('sp', rows_ap, sg[0:R, 0:D], reads=[sg])

        P.dma('sp', cst[:].rearrange("p a b -> p (a b)"), I['consts'], writes=[cst])
        CP('dve', ones_bf[:], C['ones'], [cst], [ones_bf])
        CP('dve', blk_bf[:], C['blk'], [cst], [blk_bf])
        CP('dve', ident_bf[:], C['ident'], [cst], [ident_bf])
        for l in range(DEPTH):
            to_fm(lambda j, l=l: pc[:, l, 0:NPRM], I['prm'][l], NPRM, 128, [pc])
        P.dma('sp', gb[:], I['gbias'].partition_broadcast(128), writes=[gb])
        TS('dve', ngb[:], gb[:], -1.0, ALU.mult, [gb], [ngb])
        for l in range(DEPTH):
            TS('dve', npc[:, l, 0:8], pc[:, l, PR['w0']:PR['w0'] + 8], -1.0, ALU.mult, [pc], [npc])
            P.dma('pool', lw_wa[0:64, l, :], I['w_up'][l], writes=[lw_wa])
            P.dma('pool', lw_wa[64:128, l, :], I['a_up'][l], writes=[lw_wa])
            P.dma('pool', lw_g[:, l, :], I['g_up'][l], writes=[lw_g])
            for hp in range(4):
                MSET('dve', Sbd[l][hp][:], 0.0, [Sbd[l][hp]])
        MSET('dve', mP[:], 0.0, [('mP', h) for h in range(4)])
        for l in range(DEPTH):
            MSET('dve', CTp[l][:], 0.0, [CTp[l]])
        MSET('dve', shc[:], 0.0, [shc])
        MSET('dve', cvc[:], 0.0, [cvc])

        def pcol(l, name, j):
            return pc[:, l, PR[name] + j:PR[name] + j + 1]

        def rmsnorm_u(src, NT, l, gname, sc):
            sq = T('nsq', [128, NCH, NT], BF16, sc)
            rstd = T('nrstd', [128, NT], F32, sc)
            P.op('act', lambda e: e.activation(out=sq[:], in_=src[:, :, 0:NT], func=AF.Square), [src], [sq])
            ps, pk = psbig()
            MM(ps[:, 0:NT], [(ones_bf[:], sq[:, c, :]) for c in range(NCH)], [ones_bf, sq], [pk])
            ACT(rstd[:], ps[:, 0:NT], AF.Ln, [pk], [rstd], bias=RMS_EPS, scale=1.0 / D)
            ACT(rstd[:], rstd[:], AF.Exp, [rstd], [rstd], scale=-0.5)
            for c in range(NCH):
                STT('dve', uT[:, c, 0:NT], src[:, c, 0:NT], pcol(l, gname, c), rstd[:], ALU.mult, ALU.mult,
                    [src, rstd, pc], [('uT', c)])

        def rmsnorm_add(acc, NT, l, gname, sc):
            sq = T('asq', [128, NCH, NT], BF16, sc)
            rstd = T('arstd', [128, NT], F32, sc)
            P.op('act', lambda e: e.activation(out=sq[:], in_=acc[:, :, 0:NT], func=AF.Square), [acc], [sq])
            ps, pk = psbig()
            MM(ps[:, 0:NT], [(ones_bf[:], sq[:, c, :]) for c in range(NCH)], [ones_bf, sq], [pk])
            ACT(rstd[:], ps[:, 0:NT], AF.Ln, [pk], [rstd], bias=RMS_EPS, scale=1.0 / D)
            ACT(rstd[:], rstd[:], AF.Exp, [rstd], [rstd], scale=-0.5)
            for c in range(NCH):
                STT('dve', acc[:, c, 0:NT], acc[:, c, 0:NT], pcol(l, gname, c), rstd[:], ALU.mult, ALU.mult,
                    [acc, rstd, pc], [acc])
            P.op('dve', lambda e: e.tensor_tensor(out=xT[:, :, 0:NT], in0=xT[:, :, 0:NT], in1=acc[:, :, 0:NT], op=ALU.add),
                 [acc, xT], [xT])

        UT_ALL = [('uT', c) for c in range(NCH)]

        def dense_tile(sl, col0, ncols, NT, nch=NCH, rhs=None, rkeys=None):
            ps, pk = psbig()
            if rhs is None:
                rhs = uT; rkeys = UT_ALL
            MM(ps[0:ncols, 0:NT], [(sl[:, c, col0:col0 + ncols], rhs[:, c, 0:NT]) for c in range(nch)], [sl] + list(rkeys), [pk])
            return ps, pk

        def rwkv_phase(l, NT, sgroups, groups, chunks, blk_i):
            with ExitStack() as sc:
                zs = T('zs', [128, 12, NT], F32, sc)
                lora = T('lora', [128, NT], BF16, sc)
                sgl = T('sgl', [128, NT], BF16, sc)
                zraw = [T('zraw0', [128, NT + 32], F32, sc)] * 2
                dtmp = [T('dtmp0', [128, NT], F32, sc)] * 2
                if blk_i == 0:
                    shin = T('shin', [128, 14, NS], F32, sc)
                    shout = T('shout', [128, 14, NS], F32, sc)
                    to_fm(lambda j: shin[:, j, :], I['ssh'][l][:, 0:1024], NS, 1024, [shin])
                    to_fm(lambda j: shin[:, 8 + j, :], I['ssh'][l][:, 1024:1792], NS, 768, [shin])
                    Sst = T('Sst', [128, 32, 128], F32, sc)
                    Sbs = T('Sbs', [128, NS * 4, 128], F32, sc)
                    MSET('dve', Sst[:], 0.0, [('Sst', i) for i in range(8)])
                    src = I['sS'][l].rearrange("s (hp h2) v k -> h2 v s hp k", h2=2)
                    Sst4 = Sst[:].rearrange("p (s hp) k -> p s hp k", hp=4)
                    for half in range(2):
                        for h2 in range(2):
                            P.dma('sp', Sst4[h2 * 64:(h2 + 1) * 64, :, :, h2 * 64:(h2 + 1) * 64], src[h2][:, half * 8:(half + 1) * 8],
                                  writes=[('Sst', s8) for s8 in range(8)])
                        for s8 in range(8):
                            u0 = (half * 8 + s8) * 4
                            ps, pk = psbig()
                            TRB([(ps[:, i * 128:(i + 1) * 128], Sst[:, s8 * 4 + i, :]) for i in range(4)], [('Sst', s8)], [pk])
                            CPalt(Sbs[:, u0:u0 + 4, :], ps[:, 0:512].rearrange("p (u k) -> p u k", k=128), [pk], [('Sbs', u0 + i) for i in range(4)])
                slA = load_slab(I['w_in'][l], 0, NCH, SLABS_IN[0][0], SLABS_IN[0][1])
                slB = load_slab(I['w_in'][l], 0, NCH, SLABS_IN[1][0], SLABS_IN[1][1])
                for j in range(14):
                    sl = slA if j < 8 else slB
                    ps, pk = dense_tile(sl, (j % 8 if j < 8 else j - 8) * 128, 128, NT)
                    zr = zraw[j % 2]; dt_ = dtmp[j % 2]
                    off_r = 0
                    for (off, nseg, L, kind) in sgroups:
                        zv = zr[:, off_r:off_r + nseg * (L + 1)].rearrange("p (s t) -> p s t", t=L + 1)
                        pv = ps[:, off:off + nseg * L].rearrange("p (s t) -> p s t", t=L)
                        CP('act', zv[:, :, 1:L + 1], pv, [pk], [zr])
                        if kind == 'p':
                            CP('dve', zv[:, 0, 0:1], shc[:, l, j:j + 1], [shc], [zr])
                        else:
                            CP('dve', zv[:, :, 0], shin[:, j, :], [shin], [zr])
                        dv = dt_[:, off:off + nseg * L].rearrange("p (s t) -> p s t", t=L)
                        TT('dve', dv, zv[:, :, 0:L], pv, ALU.subtract, [zr, pk], [dt_])
                        if kind == 'p':
                            CP('dve', shc[:, l, j:j + 1], zv[:, 0, L:L + 1], [zr], [shc])
                        else:
                            CP('dve', shout[:, j, :], zv[:, :, L], [zr], [shout])
                        off_r += nseg * (L + 1)
                    if j < 12:
                        STT('dve', zs[:, j, :], dt_[:, 0:NT], pcol(l, 'mu', j), ps[:, 0:NT], ALU.mult, ALU.add,
                            [dt_, pk, pc], [('zs', j)])
                    elif j == 12:
                        tmpz = dtmp[j % 2]
                        STT('dve', tmpz[:, 0:NT], dt_[:, 0:NT], pcol(l, 'mu', j), ps[:, 0:NT], ALU.mult, ALU.add,
                            [dt_, pk, pc], [dt_])
                        ACT(lora[0:64, :], tmpz[0:64, 0:NT], AF.Tanh, [dt_], [lora])
                        CP('act', lora[64:128, :], tmpz[64:128, 0:NT], [dt_], [lora])
                    else:
                        tmpz = dtmp[j % 2]
                        STT('dve', tmpz[:, 0:NT], dt_[:, 0:NT], pcol(l, 'mu', j), ps[:, 0:NT], ALU.mult, ALU.add,
                            [dt_, pk, pc], [dt_])
                        ACT(sgl[:], tmpz[:, 0:NT], AF.Sigmoid, [dt_], [sgl])
                if blk_i == 0:
                    from_fm(O['osh'][l][:, 0:1024], lambda j: shout[:, j, :], NS, 1024, [shout])
                    from_fm(O['osh'][l][:, 1024:1792], lambda j: shout[:, 8 + j, :], NS, 768, [shout])
                if debug and l == 0 and blk_i == 0:
                    dump('zs', zs[:].rearrange("p a b -> p (a b)"), [128, 12 * NT], [('zs', j) for j in range(12)])

                a_t = T('a_t', [128, NT], F32, sc); kap = T('kap', [128, NT], F32, sc)
                kmod = T('kmod', [128, NT], F32, sc); b_t = T('b_t', [128, NT], F32, sc); ew = T('ew', [128, NT], F32, sc)
                cl = T('cl', [128, NT], F32, sc); E3 = a_t
                sqk = T('sqk', [128, NT], BF16, sc)
                tmpa = dtmp[0]
                npad = sum(nseg * 2 * L for (off, nseg, L, kind) in groups)
                E1p = [T('E1_%d' % i, [128, NT], F32, sc) for i in range(2)]
                E2p = [T('E2_%d' % i, [128, NT], F32, sc) for i in range(2)]
                rtp = [T('rt_%d' % i, [128, NT], F32, sc) for i in range(2)]
                bonp = [T('bon_%d' % i, [128, NT], F32, sc) for i in range(2)]
                g_tp = [T('g_t_%d' % i, [128, NT], F32, sc) for i in range(2)]
                rtbp = [T('rtb_%d' % i, [128, NT], RDT, sc) for i in range(2)]
                padp = [{n: T('pad%s_%d' % (n, i), [128, npad], RDT, sc) for n in 'kbqv'} for i in range(2)]
                for i in range(2):
                    for n in 'kbqv':
                        MSET('dve', padp[i][n][:], 0.0, [padp[i][n]])
                RW = [128, 512]
                wk = {n: T('w' + n, RW, RDT, sc) for n in ['bh', 'kh', 'PT', 'Vbd', 'Bbd', 'Kbd', 'Kh', 'U0', 'Ktm', 'PVs', 'Zb']}
                for n in ['A0', 'A0T', 'A1', 'A1T', 'Z']:
                    wk[n] = T('w' + n, RW, F32, sc)
                wk['QbT'] = T('wQbT', [128, 256], RDT, sc)
                wk['QkT'] = T('wQkT', [128, 256], RDT, sc)
                wk['gI'] = T('wgI', RW, F32, sc)
                kps = [{'Mx': T('kMx0', RW, F32, sc), 'D0': T('kD00', RW, F32, sc),
                        'Rh': T('kRh0', [128, 256], F32, sc), 'Y0': T('kY00', [128, 256], F32, sc)}] * 2
                batches = []
                cur = []
                for ch in chunks:
                    if cur and (cur[0].L != ch.L or len(cur) >= 4):
                        batches.append(cur); cur = []
                    cur.append(ch)
                if cur:
                    batches.append(cur)
                ui = [0]
                bi = [0]
                def pre_gen(hp):
                    E1 = E1p[hp % 2]; E2 = E2p[hp % 2]; rt = rtp[hp % 2]; bon = bonp[hp % 2]; g_t = g_tp[hp % 2]; rtb = rtbp[hp % 2]
                    padk = padp[hp % 2]['k']; padb = padp[hp % 2]['b']; padq = padp[hp % 2]['q']; padv = padp[hp % 2]['v']
                    kk_t = bon
                    r_ = zs[:, hp, :]; k_ = zs[:, 4 + hp, :]; v_ = zs[:, 8 + hp, :]
                    rk_, kk_, vk_ = ('zs', hp), ('zs', 4 + hp), ('zs', 8 + hp)
                    hc = slice(hp * 128, (hp + 1) * 128)
                    ps, pk = psbig()
                    MM(ps[:, 0:NT], [(lw_wa[0:64, l, hc], lora[0:64, :])], [lw_wa, lora], [pk])
                    ACT(ew[:], ps[:, 0:NT], AF.Exp, [pk, npc], [ew], bias=npc[:, l, hp:hp + 1], scale=-1.0)
                    ACT(ew[:], ew[:], AF.Ln, [ew], [ew], bias=1.0)
                    ACT(ew[:], ew[:], AF.Exp, [ew], [ew], bias=-0.5, scale=-1.0)
                    yield
                    ps, pk = psbig()
                    MM(ps[:, 0:NT], [(lw_wa[64:128, l, hc], lora[64:128, :])], [lw_wa, lora], [pk])
                    ACT(a_t[:], ps[:, 0:NT], AF.Exp, [pk, npc], [a_t], bias=npc[:, l, 4 + hp:5 + hp], scale=-1.0)
                    ACT(a_t[:], a_t[:], AF.Ln, [a_t], [a_t], bias=1.0)
                    ACT(a_t[:], a_t[:], AF.Exp, [a_t], [a_t], scale=-1.0)
                    ps, pk = psbig()
                    MM(ps[:, 0:NT], [(lw_g[:, l, hc], sgl[:])], [lw_g, sgl], [pk])
                    CP('act', g_t[:], ps[:, 0:NT], [pk], [g_t])
                    yield
                    TS('dve', kk_t[:], k_, pcol(l, 'kk', hp), ALU.mult, [kk_, pc], [kk_t])
                    ACT(sqk[:], kk_t[:], AF.Square, [kk_t], [sqk])
                    ps, pk = psbig()
                    MM(ps[:, 0:NT], [(blk_bf[:], sqk[:])], [blk_bf, sqk], [pk])
                    yield
                    TS('dve', tmpa[:], ps[:, 0:NT], 1e-18, ALU.max, [pk], [tmpa])
                    ACT(tmpa[:], tmpa[:], AF.Ln, [tmpa], [tmpa])
                    ACT(tmpa[:], tmpa[:], AF.Exp, [tmpa], [tmpa], scale=-0.5)
                    TT('dve', kap[:], kk_t[:], tmpa[:], ALU.mult, [kk_t, tmpa], [kap])
                    yield
                    TS('dve', tmpa[:], a_t[:], -1.0, ALU.add, [a_t, pc], [tmpa], s2=pcol(l, 'ka', hp), op1=ALU.mult)
                    STT('dve', kmod[:], tmpa[:], 1.0, k_, ALU.add, ALU.mult, [tmpa, kk_], [kmod])
                    TT('dve', b_t[:], kap[:], a_t[:], ALU.mult, [kap, a_t], [b_t])
                    yield
                    STT('dve', tmpa[:], r_, pcol(l, 'rk', hp), kmod[:], ALU.mult, ALU.mult, [rk_, pc, kmod], [tmpa])
                    ps, pk = psbig()
                    MM(ps[:, 0:NT], [(C['blk'], tmpa[:])], ['cst', tmpa], [pk])
                    TT('dve', bon[:], ps[:, 0:NT], v_, ALU.mult, [pk, vk_], [bon])
                    yield
                    for ch in chunks:
                        SCAN(cl[:, ch.off:ch.off + ch.L], C['ones'][:, 0:ch.L], ew[:, ch.off:ch.off + ch.L], 0.0, ALU.mult, ALU.subtract,
                             ['cst', ew], [cl])
                    yield
                    ACT(E1[:], cl[:], AF.Exp, [cl], [E1])
                    TT('dve', tmpa[:], cl[:], ew[:], ALU.add, [cl, ew], [tmpa])
                    ACT(E2[:], tmpa[:], AF.Exp, [tmpa], [E2])
                    ACT(E3[:], cl[:], AF.Exp, [cl], [E3], scale=-1.0)
                    yield
                    TT('dve', rt[:], r_, E1[:], ALU.mult, [rk_, E1], [rt])
                    CP('act', rtb[:], rt[:], [rt], [rtb])
                    yield
                    po = 0
                    for (off, nseg, L, kind) in groups:
                        for (pd, src, sk, E_, ek) in [(padk, kap[:], kap, E2, E2), (padb, b_t[:], b_t, E3, E3),
                                                       (padq, kmod[:], kmod, E3, E3), (padv, v_, vk_, None, None)]:
                            pv = pd[:, po:po + nseg * 2 * L].rearrange("p (s t) -> p s t", t=2 * L)
                            for h2 in range(2):
                                prt = slice(h2 * 64, (h2 + 1) * 64)
                                sv = src[prt, off:off + nseg * L].rearrange("p (s t) -> p s t", t=L)
                                if E_ is None:
                                    CP('dve', pv[prt, :, h2 * L:(h2 + 1) * L], sv, [sk], [pd])
                                else:
                                    ev = E_[prt, off:off + nseg * L].rearrange("p (s t) -> p s t", t=L)
                                    TT('dve', pv[prt, :, h2 * L:(h2 + 1) * L], sv, ev, ALU.mult, [sk, ek], [pd])
                            yield
                        po += nseg * 2 * L
                def stage(hp, tick):
                    E1 = E1p[hp % 2]; yT = E2p[hp % 2]; rt = rtp[hp % 2]; rtb = rtbp[hp % 2]
                    padk = padp[hp % 2]['k']; padb = padp[hp % 2]['b']; padq = padp[hp % 2]['q']; padv = padp[hp % 2]['v']
                    pending = []
                    IDT = ident_bf if RDT == BF16 else C['ident']
                    IDK = kn(ident_bf) if RDT == BF16 else 'cst'
                    for bt_ in batches:
                        L = bt_[0].L; L2 = 2 * L; nb = len(bt_); po0 = bt_[0].padoff; o0 = bt_[0].off
                        Wd = nb * L2; Wl = nb * L; W8 = nb * 128
                        kp_ = kps[bi[0] % 2]; bi[0] += 1
                        J = {64: 5, 16: 3, 8: 2}[L]

                        def v3(ap, w):
                            return ap.rearrange("p (n t) -> p n t", t=w)
                        mlt = C['mlt%d' % L][0:L2, 0:L2].unsqueeze(1).to_broadcast([L2, nb, L2])
                        mgt = C['mgt%d' % L][0:L2, 0:L2].unsqueeze(1).to_broadcast([L2, nb, L2])
                        mle = C['mle%d' % L][0:L2, 0:L].unsqueeze(1).to_broadcast([L2, nb, L])
                        idb = C['ident'][0:L2, 0:L2].unsqueeze(1).to_broadcast([L2, nb, L2])
                        E1L = v3(E1[:, o0:o0 + Wl], L)[:, :, L - 1:L]
                        TT('dve', v3(wk['bh'][:, 0:Wd], L2), v3(padb[:, po0:po0 + Wd], L2), E1L.to_broadcast([128, nb, L2]), ALU.mult, [padb, E1], [wk['bh']])
                        TT('dve', v3(wk['kh'][:, 0:Wd], L2), v3(padq[:, po0:po0 + Wd], L2), E1L.to_broadcast([128, nb, L2]), ALU.mult, [padq, E1], [wk['kh']])
                        TT('dve', v3(wk['gI'][:, 0:W8], 128), C['ident'].unsqueeze(1).to_broadcast([128, nb, 128]), E1L.to_broadcast([128, nb, 128]),
                           ALU.mult, ['cst', E1], [wk['gI']])

                        def sl2(t, i):
                            return t[:, po0 + i * L2:po0 + (i + 1) * L2]

                        def bl(t, i):
                            return t[0:L2, i * L2:(i + 1) * L2]

                        def b8(t, i):
                            return t[0:L2, i * 128:(i + 1) * 128]
                        RB = range(nb)

                        def fr(ap):
                            return ap.bitcast(F32R) if USE_F32R else ap
                        ps, pk = psbig()
                        MMB([(bl(ps, i), sl2(padb, i), sl2(padk, i)) for i in RB], [padb, padk], [pk])
                        STT('dve', fr(v3(wk['A0T'][0:L2, 0:Wd], L2)), v3(ps[0:L2, 0:Wd], L2), -1.0, mlt, ALU.mult, ALU.mult, [pk, 'cst'], [wk['A0T']])
                        ps, pk = psbig()
                        MMB([(bl(ps, i), sl2(padk, i), sl2(padb, i)) for i in RB], [padb, padk], [pk])
                        STT('dve', fr(v3(wk['A0'][0:L2, 0:Wd], L2)), v3(ps[0:L2, 0:Wd], L2), -1.0, mgt, ALU.mult, ALU.mult, [pk, 'cst'], [wk['A0']])
                        ps, pk = psbig()
                        MMB([(bl(ps, i), sl2(padq, i), sl2(padk, i)) for i in RB], [padq, padk], [pk])
                        TT('dve', v3(wk['PT'][0:L2, 0:Wd], L2), v3(ps[0:L2, 0:Wd], L2), mlt, ALU.mult, [pk, 'cst'], [wk['PT']])
                        ps, pk = psbig()
                        MMB([(ps[0:L2, i * L:(i + 1) * L], sl2(padb, i), rtb[:, o0 + i * L:o0 + (i + 1) * L]) for i in RB], [padb, rtb], [pk])
                        TT('dve', v3(wk['QbT'][0:L2, 0:Wl], L), v3(ps[0:L2, 0:Wl], L), mle, ALU.mult, [pk, 'cst'], [wk['QbT']])
                        ps, pk = psbig()
                        MMB([(ps[0:L2, i * L:(i + 1) * L], sl2(padq, i), rtb[:, o0 + i * L:o0 + (i + 1) * L]) for i in RB], [padq, rtb], [pk])
                        TT('dve', v3(wk['QkT'][0:L2, 0:Wl], L), v3(ps[0:L2, 0:Wl], L), mle, ALU.mult, [pk, 'cst'], [wk['QkT']])
                        tick()
                        for (dst, srct, off_) in [(wk['Vbd'], padv, po0), (wk['Bbd'], wk['bh'], 0), (wk['Kbd'], wk['kh'], 0), (wk['Ktm'], padk, po0)]:
                            ps, pk = psbig()
                            MMB([(b8(ps, i), srct[:, off_ + i * L2:off_ + (i + 1) * L2], IDT[:, :]) for i in RB], [srct, IDK], [pk])
                            CPalt(dst[0:L2, 0:W8], ps[0:L2, 0:W8], [pk], [dst])
                        ps, pk = psbig()
                        MMB([(b8(ps, i), bl(wk['PT'], i), b8(wk['Vbd'], i)) for i in RB], [wk['PT'], wk['Vbd']], [pk])
                        CPalt(wk['PVs'][0:L2, 0:W8], ps[0:L2, 0:W8], [pk], [wk['PVs']])
                        tick()
                        Z = wk['Z']
                        TT('dve', fr(v3(Z[0:L2, 0:Wd], L2)), v3(wk['A0T'][0:L2, 0:Wd], L2), idb, ALU.add, [wk['A0T'], 'cst'], [Z])
                        Ap, ApT = wk['A0'], wk['A0T']
                        An, AnT = wk['A1'], wk['A1T']
                        for jj in range(1, J + 1):
                            ps, pk = psbig()
                            MMB([(bl(ps, i), fr(bl(ApT, i)), fr(bl(Ap, i))) for i in RB], [Ap, ApT], [pk])
                            CP('act', fr(An[0:L2, 0:Wd]), ps[0:L2, 0:Wd], [pk], [An])
                            if jj < J:
                                ps, pk = psbig()
                                MMB([(bl(ps, i), fr(bl(Ap, i)), fr(bl(ApT, i))) for i in RB], [Ap, ApT], [pk])
                                CP('act', fr(AnT[0:L2, 0:Wd]), ps[0:L2, 0:Wd], [pk], [AnT])
                            ps, pk = psbig()
                            MMB([(bl(ps, i), fr(bl(An, i)), fr(bl(Z, i))) for i in RB], [An, Z], [pk])
                            TT('dve', fr(Z[0:L2, 0:Wd]), Z[0:L2, 0:Wd], ps[0:L2, 0:Wd], ALU.add, [Z, pk], [Z])
                            Ap, ApT, An, AnT = An, AnT, Ap, ApT
                            if pending:
                                pending.pop(0)()
                            tick()
                        tick()
                        Zb = wk['Zb']
                        CP('act', Zb[0:L2, 0:Wd], Z[0:L2, 0:Wd], [Z], [Zb])
                        ps, pk = psbig()
                        MMB([(b8(ps, i), bl(Zb, i), b8(wk['Ktm'], i)) for i in RB], [Zb, wk['Ktm']], [pk])
                        CP('act', wk['Kh'][0:L2, 0:W8], ps[0:L2, 0:W8], [pk], [wk['Kh']])
                        ps, pk = psbig()
                        MMB([(b8(ps, i), bl(Zb, i), b8(wk['PVs'], i)) for i in RB], [Zb, wk['PVs']], [pk])
                        P.op('act', lambda e, o_=wk['U0'][0:L2, 0:W8], i_=ps[0:L2, 0:W8]: e.mul(out=o_, in_=i_, mul=-1.0), [pk], [kn(wk['U0'])])
                        tick()
                        while pending:
                            pending.pop(0)()
                        ps, pk = psbig()
                        MMB([(ps[:, i * L:(i + 1) * L], b8(wk['Kh'], i), wk['QbT'][0:L2, i * L:(i + 1) * L]) for i in RB], [wk['Kh'], wk['QbT']], [pk])
                        TT('dve', kp_['Rh'][:, 0:Wl], rt[:, o0:o0 + Wl], ps[:, 0:Wl], ALU.subtract, [rt, pk], [kp_['Rh']])
                        ps, pk = psbig()
                        MMG([(ps[:, i * L:(i + 1) * L], [(b8(wk['U0'], i), wk['QbT'][0:L2, i * L:(i + 1) * L]),
                                                        (b8(wk['Vbd'], i), wk['QkT'][0:L2, i * L:(i + 1) * L])]) for i in RB],
                            [wk['U0'], wk['QbT'], wk['Vbd'], wk['QkT']], [pk])
                        CP('act', kp_['Y0'][:, 0:Wl], ps[:, 0:Wl], [pk], [kp_['Y0']])
                        ps, pk = psbig()
                        MMG([(ps[:, i * 128:(i + 1) * 128], [(b8(wk['Bbd'], i), b8(wk['U0'], i)), (b8(wk['Kbd'], i), b8(wk['Vbd'], i))]) for i in RB],
                            [wk['Bbd'], wk['U0'], wk['Kbd'], wk['Vbd']], [pk])
                        CP('act', kp_['D0'][:, 0:W8], ps[:, 0:W8], [pk], [kp_['D0']])
                        ps, pk = psbig()
                        MMB([(ps[:, i * 128:(i + 1) * 128], b8(wk['Kh'], i), b8(wk['Bbd'], i)) for i in RB], [wk['Kh'], wk['Bbd']], [pk])
                        TT('dve', kp_['Mx'][:, 0:W8], wk['gI'][:, 0:W8], ps[:, 0:W8], ALU.subtract, [wk['gI'], pk], [kp_['Mx']])
                        while pending:
                            pending.pop(0)()

                        def step2(i, ch, kp_=kp_, L=L):
                            o = ch.off
                            if ch.kind == 'p':
                                S_ap = Sbd[l][hp][:]; S_key = kn(Sbd[l][hp])
                            else:
                                S_ap = Sbs[:, ch.seq * 4 + hp, :]; S_key = ('Sbs', ch.seq * 4 + hp)
                            psY, pkY = psbig()
                            MM(psY[:, 0:L], [(S_ap, kp_['Rh'][:, i * L:(i + 1) * L])], [S_key, kp_['Rh']], [pkY])
                            psS, pkS = psbig()
                            MM(psS[:, 0:128], [(kp_['Mx'][:, i * 128:(i + 1) * 128], S_ap)], [S_key, kp_['Mx']], [pkS])
                            TT('dve', S_ap, psS[:, 0:128], kp_['D0'][:, i * 128:(i + 1) * 128], ALU.add, [pkS, kp_['D0']], [S_key])
                            TT('dve', yT[:, o:o + L], psY[:, 0:L], kp_['Y0'][:, i * L:(i + 1) * L], ALU.add, [pkY, kp_['Y0']], [yT])
                        for i, ch in enumerate(bt_):
                            pending.append(lambda i=i, ch=ch, f=step2: f(i, ch))
                    while pending:
                        pending.pop(0)()
                def post(hp):
                    yT = E2p[hp % 2]; bon = bonp[hp % 2]; g_t = g_tp[hp % 2]
                    ps, pk = psbig()
                    MM(ps[:, 0:NT], [(C['blk'], yT[:])], ['cst', yT], [pk])
                    STT('dve', yT[:], ps[:, 0:NT], -1.0 / 64, yT[:], ALU.mult, ALU.add, [pk, yT], [yT])
                    ACT(tmpa[:], yT[:], AF.Square, [yT], [tmpa])
                    ps, pk = psbig()
                    MM(ps[:, 0:NT], [(C['blk'], tmpa[:])], ['cst', tmpa], [pk])
                    ACT(tmpa[:], ps[:, 0:NT], AF.Ln, [pk], [tmpa], bias=RW_GN_EPS, scale=1.0 / 64)
                    ACT(tmpa[:], tmpa[:], AF.Exp, [tmpa], [tmpa], scale=-0.5)
                    TT('dve', yT[:], yT[:], tmpa[:], ALU.mult, [yT, tmpa], [yT])
                    STT('dve', yT[:], yT[:], pcol(l, 'gng', hp), bon[:], ALU.mult, ALU.add, [yT, pc, bon], [yT])
                    STT('dve', yaT[:, hp, 0:NT], yT[:], pcol(l, 'gnb', hp), g_t[:], ALU.add, ALU.mult, [yT, pc, g_t], [('yaT', hp)])
                def advance(g, n=1):
                    if g is None:
                        return
                    for _ in range(n):
                        try:
                            next(g)
                        except StopIteration:
                            return
                for _ in pre_gen(0):
                    pass
                for hp in range(4):
                    nxt = pre_gen(hp + 1) if hp < 3 else None
                    stage(hp, lambda: advance(nxt, 1))
                    if nxt is not None:
                        for _ in nxt:
                            pass
                    post(hp)
                if blk_i == 0:
                    dst = O['oS'][l].rearrange("s (hp h2) v k -> h2 v s hp k", h2=2)
                    for half in range(2):
                        for s8 in range(8):
                            s_ = half * 8 + s8; u0 = s_ * 4
                            ps, pk = psbig()
                            TRB([(ps[:, i * 128:(i + 1) * 128], Sbs[:, u0 + i, :]) for i in range(4)], [('Sbs', u0 + i) for i in range(4)], [pk])
                            CPalt(Sst[:, s8 * 4:s8 * 4 + 4, :], ps[:, 0:512].rearrange("p (u k) -> p u k", k=128), [pk], [('Sst', s8)])
                        for h2 in range(2):
                            P.dma('sp', dst[h2][:, half * 8:(half + 1) * 8], Sst4[h2 * 64:(h2 + 1) * 64, :, :, h2 * 64:(h2 + 1) * 64],
                                  reads=[('Sst', s8) for s8 in range(8)])
                if debug and l == 0 and blk_i == 0:
                    dump('yaT', yaT[:, :, 0:NT], [128, 4, NT], [('yaT', h) for h in range(4)])
                P.barrier()
                P.flush()

        def mlstm_phase(l, NT, groups, chunks, blk_i):
            with ExitStack() as sc:
                nck = len(chunks)
                npd = sum(nseg * (L + 3) for (off, nseg, L, kind) in groups)
                qkp = T('qkp', [128, 8, npd], F32, sc)
                cacc = T('cacc', [128, NT], F32, sc)
                qb = T('qb', [128, 4, NT], BF16, sc); kb = T('kb', [128, 4, NT], BF16, sc); kf = T('kf', [128, 4, NT], F32, sc)
                sgo = T('sgo', [128, 4, NT], BF16, sc)
                vtm = T('vtm', [128, nck, 4, 132], BF16, sc)
                gsb = T('gsb', [8, NT], F32, sc)
                h4 = T('h4', [128, 4, NT], F32, sc)
                igb = T('igb', [128, NT], F32, sc); sp_ = T('sp_', [128, NT], F32, sc)
                cc4 = T('cc4', [128, 4, NT], F32, sc); G4 = T('G4', [128, 4, NT], F32, sc); em4 = T('em4', [128, 4, NT], F32, sc)
                sc4 = T('sc4', [128, 4, NT], F32, sc); tmpb = T('tmpb', [128, NT], F32, sc)
                if blk_i == 0:
                    cvin = T('cvin', [128, 8, NS * 3], F32, sc); cvout = T('cvout', [128, 8, NS * 3], F32, sc)
                    to_fm(lambda j: cvin[:, j, :], I['scv'][l], NS * 3, D, [cvin])
                    Cst = T('Cst', [128, 32, 128], F32, sc)
                    CTs = T('CTs', [128, NS * 4, 132], F32, sc)
                    m_s = T('m_s', [128, NS * 4], F32, sc); mo_s = T('mo_s', [128, NS * 4], F32, sc)
                    nst = T('nst', [128, NS * 4], F32, sc)
                    CTbC = Cst[:].bitcast(BF16).rearrange("p a (b c) -> p (a b) c", b=2)
                    CTbn = T('CTbn', [128, NS * 4], BF16, sc)
                    for g4 in range(4):
                        g2 = g4 % 2
                        P.dma('sp', Cst[:, g2 * 16:(g2 + 1) * 16, :].rearrange("p (s h) k -> p s h k", h=4),
                              I['sC'][l, g4 * 4:(g4 + 1) * 4].rearrange("s h v k -> v s h k"), writes=[('Cst', g2)])
                        for q4 in range(4):
                            u0 = g4 * 16 + q4 * 4
                            ps, pk = psbig()
                            TRB([(ps[:, i * 128:(i + 1) * 128], Cst[:, g2 * 16 + q4 * 4 + i, :]) for i in range(4)], [('Cst', g2)], [pk])
                            CPalt(CTs[:, u0:u0 + 4, 0:128], ps[:, 0:512].rearrange("p (u k) -> p u k", k=128), [pk], [('CTs', u0 // 4)])
                    to_fm(lambda j: nst[:, :], I['sn'][l].rearrange("s h k -> (s h) k"), NS * 4, 128, [nst])
                    CP('dve', CTs[:, :, 128], nst[:], [nst] + [('CTs', u) for u in range(NS)], [('CTs', u) for u in range(NS)])
                    P.dma('sp', m_s[:], I['sm'][l:l + 1, :].partition_broadcast(128), writes=[m_s])
                    CP('act', CTbC, CTs[:, :, 0:128], [('CTs', u) for u in range(NS)], [('Cst', 0), ('Cst', 1)])
                    CP('act', CTbn[:], CTs[:, :, 128], [('CTs', u) for u in range(NS)], [CTbn])
                MSET('dve', vtm[:, :, :, 128:129], 1.0, [vtm])
                slC = load_slab(I['w_in'][l], 0, NCH, SLABS_IN[2][0], SLABS_IN[2][1])
                slD = load_slab(I['w_in'][l], 0, NCH, SLABS_IN[3][0], SLABS_IN[3][1])
                for j in range(8):
                    ps, pk = dense_tile(slC, j * 128, 128, NT)
                    po = 0
                    for (off, nseg, L, kind) in groups:
                        qv = qkp[:, j, po:po + nseg * (L + 3)].rearrange("p (s t) -> p s t", t=L + 3)
                        pv = ps[:, off:off + nseg * L].rearrange("p (s t) -> p s t", t=L)
                        CP('act', qv[:, :, 3:L + 3], pv, [pk], [('qkp', j)])
                        if kind == 'p':
                            CP('dve', qv[:, 0, 0:3], cvc[:, l, j, :], [cvc], [('qkp', j)])
                            CP('dve', cvc[:, l, j, :], qv[:, 0, L:L + 3], [('qkp', j)], [cvc])
                        else:
                            CP('dve', qv[:, :, 0:3], cvin[:, j, :].rearrange("p (s t) -> p s t", t=3), [cvin], [('qkp', j)])
                            CP('dve', cvout[:, j, :].rearrange("p (s t) -> p s t", t=3), qv[:, :, L:L + 3], [('qkp', j)], [cvout])
                        av = cacc[:, off:off + nseg * L].rearrange("p (s t) -> p s t", t=L)
                        TS('dve', av, qv[:, :, 0:L], pcol(l, 'cw', 0 * 8 + j), ALU.mult, [('qkp', j), pc], [cacc],
                           s2=pcol(l, 'cb', j), op1=ALU.add)
                        for tp in range(1, 4):
                            STT('dve', av, qv[:, :, tp:tp + L], pcol(l, 'cw', tp * 8 + j), av, ALU.mult, ALU.add, [('qkp', j), pc, cacc], [cacc])
                        po += nseg * (L + 3)
                    if j < 4:
                        ACT(qb[:, j, :], cacc[:], AF.Silu, [cacc], [('qb', j)])
                    else:
                        ACT(kf[:, j - 4, :], cacc[:], AF.Silu, [cacc], [('kf', j - 4)])
                        TS('dve', kf[:, j - 4, :], kf[:, j - 4, :], 128.0 ** -0.5, ALU.mult, [('kf', j - 4)], [('kf', j - 4)])
                        CP('dve', kb[:, j - 4, :], kf[:, j - 4, :], [('kf', j - 4)], [('kb', j - 4)])
                if blk_i == 0:
                    from_fm(O['ocv'][l], lambda j: cvout[:, j, :], NS * 3, D, [cvout])
                for ci, ch in enumerate(chunks):
                    ps, pk = psbig()
                    MM(ps[0:ch.L, 0:512], [(uT[:, c, ch.off:ch.off + ch.L], slD[:, c, 0:512]) for c in range(NCH)], [slD] + UT_ALL, [pk])
                    CPalt(vtm[0:ch.L, ci, :, 0:128], ps[0:ch.L, 0:512].rearrange("p (h v) -> p h v", v=128), [pk], [vtm])
                for j in range(4):
                    ps, pk = dense_tile(slD, 512 + j * 128, 128, NT)
                    ACT(sgo[:, j, :], ps[:, 0:NT], AF.Sigmoid, [pk], [('sgo', j)])
                ps, pk = psbig()
                MM(ps[0:8, 0:NT], [(slD[:, c, 1024:1032], uT[:, c, 0:NT]) for c in range(NCH)], [slD] + UT_ALL, [pk])
                CP('act', gsb[:], ps[0:8, 0:NT], [pk], [gsb])
                if debug and l == 0 and blk_i == 0:
                    dump('qb', qb[:], [128, 4, NT], [('qb', j) for j in range(4)])
                ctm = [{n: T('m%s%d' % (n, i), [128, w_], dt_, sc) for (n, w_, dt_) in
                        [('t3', 512, F32), ('AT', 512, F32), ('qt', 256, F32),
                         ('aqk', 512, BF16), ('qtb', 512, BF16), ('cco', 64, F32), ('wco', 64, F32), ('wtmp', 64, F32)]} for i in range(2)]
                for d_ in ctm:
                    d_['den'] = d_['t3']
                for h in range(4):
                    ps, pk = psbig()
                    MM(ps[:, 0:NT], [(C['ident'][0:8, h:h + 1].to_broadcast([8, 128]), gsb[:])], ['cst', gsb], [pk])
                    TS('dve', igb[:], ps[:, 0:NT], gb[:, l * 8 + h:l * 8 + h + 1], ALU.add, [pk, gb], [igb])
                    ps, pk = psbig()
                    MM(ps[:, 0:NT], [(C['ident'][0:8, 4 + h:5 + h].to_broadcast([8, 128]), gsb[:])], ['cst', gsb], [pk])
                    ACT(sp_[:], ps[:, 0:NT], AF.Exp, [pk, ngb], [sp_], bias=ngb[:, l * 8 + 4 + h:l * 8 + 5 + h], scale=-1.0)
                    ACT(sp_[:], sp_[:], AF.Ln, [sp_], [sp_], bias=1.0)
                    for ch in chunks:
                        SCAN(em4[:, h, ch.off:ch.off + ch.L], C['ones'][:, 0:ch.L], sp_[:, ch.off:ch.off + ch.L], 0.0, ALU.mult, ALU.subtract,
                             ['cst', sp_], [('em4', h)])
                    TT('dve', cc4[:, h, :], igb[:], em4[:, h, :], ALU.subtract, [igb, ('em4', h)], [('cc4', h)])
                for ci, ch in enumerate(chunks):
                    for h in range(4):
                        L = ch.L; o = ch.off
                        if ch.kind == 'p':
                            m_in = mP[:, l, h:h + 1]; m_key = ('mP', h); m_out = m_in; mo_key = ('mP', h)
                        else:
                            u = ch.seq * 4 + h
                            m_in = m_s[:, u:u + 1]; m_key = 'm_s'; m_out = mo_s[:, u:u + 1]; mo_key = ('mo_s', u)
                        SCAN(G4[:, h, o:o + L], cc4[:, h, o:o + L], cc4[:, h, o:o + L], m_in, ALU.max, ALU.max, [('cc4', h), m_key], [('G4', h)])
                        ACT(sc4[:, h, o:o + L], G4[:, h, o:o + L], AF.Exp, [('G4', h), m_key], [('sc4', h)], bias=m_in, scale=-1.0)
                        TT('dve', m_out, em4[:, h, o + L - 1:o + L], G4[:, h, o + L - 1:o + L], ALU.add, [('em4', h), ('G4', h)], [mo_key])
                for h in range(4):
                    TT('dve', tmpb[:], em4[:, h, :], G4[:, h, :], ALU.add, [('em4', h), ('G4', h)], [tmpb])
                    ACT(em4[:, h, :], tmpb[:], AF.Exp, [tmpb], [('em4', h)], scale=-1.0)
                H4 = lambda n: [(n, h) for h in range(4)]

                mgroups = []
                for ci, ch in enumerate(chunks):
                    if mgroups and ch.kind == 's' and chunks[mgroups[-1][0]].kind == 's' and mgroups[-1][1] < 16:
                        mgroups[-1][1] += 1
                    else:
                        mgroups.append([ci, 1])

                def g_ctx(gi):
                    ci0, ns = mgroups[gi]
                    ch0 = chunks[ci0]
                    return ci0, ns, ch0.L, ch0.off, ctm[gi % 2]

                def CT_of(ch):
                    if ch.kind == 'p':
                        return CTp[l][:, :, :], kn(CTp[l])
                    return CTs[:, ch.seq * 4:(ch.seq + 1) * 4, :], ('CTs', ch.seq)

                def ml_A(gi):
                    ci0, ns, L, o0, tm = g_ctx(gi)
                    W = ns * L; W4 = 4 * W

                    def src4(t):
                        return t.rearrange("p h (n t) -> p h n t", t=L)

                    def f4(ap):
                        return ap.rearrange("p (h n t) -> p h n t", h=4, n=ns)
                    mnegb = C['mneg%d' % L][0:L, 0:L].unsqueeze(1).unsqueeze(1).to_broadcast([L, 4, ns, L])
                    idb = C['ident'][0:L, 0:L].unsqueeze(1).unsqueeze(1).to_broadcast([L, 4, ns, L])
                    t3 = f4(tm['t3'][0:L, 0:W4]); cco = tm['cco']; wco = tm['wco']
                    cc3 = cco[0:L, 0:4 * ns].rearrange("p (h n) -> p h n", n=ns)
                    TT('dve', t3, src4(cc4[0:L, :, o0:o0 + W]), idb, ALU.mult, H4('cc4') + ['cst'], [tm['t3']])
                    P.op('dve', lambda e, o_=cc3, i_=t3: e.tensor_reduce(out=o_, in_=i_, axis=mybir.AxisListType.X, op=ALU.add),
                         [kn(tm['t3'])], [kn(cco)])
                    TT('dve', t3, mnegb, src4(G4[0:L, :, o0:o0 + W]), ALU.subtract, H4('G4') + ['cst'], [tm['t3']])
                    TT('dve', t3, t3, cc3.unsqueeze(3).to_broadcast([L, 4, ns, L]), ALU.add, [tm['t3'], cco], [tm['t3']])
                    ACT(tm['AT'][0:L, 0:W4], tm['t3'][0:L, 0:W4], AF.Exp, [tm['t3']], [tm['AT']])
                    ps, pk = psbig()
                    MMB([(ps[0:L, (h * ns + n) * L:(h * ns + n + 1) * L], kb[:, h, o0 + n * L:o0 + (n + 1) * L], qb[:, h, o0 + n * L:o0 + (n + 1) * L])
                         for h in range(4) for n in range(ns)], H4('kb') + H4('qb'), [pk])
                    TT('dve', tm['aqk'][0:L, 0:W4], tm['AT'][0:L, 0:W4], ps[0:L, 0:W4], ALU.mult, [tm['AT'], pk], [tm['aqk']])
                    qtt = tm['qtb'] if chunks[ci0].kind == 's' else tm['qt']
                    TT('dve', f4(qtt[:, 0:W4]), src4(qb[:, :, o0:o0 + W]), src4(sc4[:, :, o0:o0 + W]), ALU.mult, H4('qb') + H4('sc4'), [qtt])
                    TT('dve', tm['wtmp'][0:L, 0:4 * ns].rearrange("p (h n) -> p h n", n=ns), cc3, src4(G4[0:L, :, o0:o0 + W])[:, :, :, L - 1],
                       ALU.subtract, [cco] + H4('G4'), [tm['wtmp']])
                    ACT(wco[0:L, 0:4 * ns], tm['wtmp'][0:L, 0:4 * ns], AF.Exp, [tm['wtmp']], [wco])

                def ml_B(gi):
                    ci0, ns, L, o0, tm = g_ctx(gi)
                    W = ns * L; W4 = 4 * W
                    grpN = []; grpD = []; ckeys = []
                    smp = chunks[ci0].kind == 's'
                    qtt = tm['qtb'] if smp else tm['qt']
                    for h in range(4):
                        for n in range(ns):
                            ch_ = chunks[ci0 + n]
                            if smp:
                                u_ = ch_.seq * 4 + h
                                Cm = CTbC[:, u_, :]; ncol_ = CTbn[:, u_:u_ + 1]
                                for k_ in (('Cst', 0), ('Cst', 1), kn(CTbn)):
                                    if k_ not in ckeys:
                                        ckeys.append(k_)
                            else:
                                CTv, CT_key = CT_of(ch_)
                                Cm = CTv[:, h, 0:128]; ncol_ = CTv[:, h, 128:129]
                                if CT_key not in ckeys:
                                    ckeys.append(CT_key)
                            c0 = (h * ns + n) * L
                            grpN.append((None, c0, [(vtm[0:L, ci0 + n, h, 0:128], tm['aqk'][0:L, c0:c0 + L]), (Cm, qtt[:, c0:c0 + L])]))
                            grpD.append((None, c0, [(ones_bf[0:L, :], tm['aqk'][0:L, c0:c0 + L]),
                                                    (ncol_.to_broadcast([128, 128]), qtt[:, c0:c0 + L])]))
                    psN, pkN = psbig()
                    MMG([(psN[:, c0:c0 + L], prs) for (_, c0, prs) in grpN], [vtm, tm['aqk'], qtt] + ckeys, [pkN])
                    psD, pkD = psbig()
                    MMG([(psD[:, c0:c0 + L], prs) for (_, c0, prs) in grpD], [ones_bf, tm['aqk'], qtt] + ckeys, [pkD])
                    ACT(tm['den'][:, 0:W4], psD[:, 0:W4], AF.Abs, [pkD], [tm['den']])
                    den4 = tm['den'][:, 0:W4].rearrange("p (h n t) -> p h n t", h=4, n=ns)
                    TT('dve', den4, den4, em4[:, :, o0:o0 + W].rearrange("p h (n t) -> p h n t", t=L), ALU.max, [tm['den']] + H4('em4'), [tm['den']])
                    ACT(tm['den'][:, 0:W4], tm['den'][:, 0:W4], AF.Ln, [tm['den']], [tm['den']])
                    ACT(tm['den'][:, 0:W4], tm['den'][:, 0:W4], AF.Exp, [tm['den']], [tm['den']], scale=-1.0)
                    TT('dve', h4[:, :, o0:o0 + W].rearrange("p h (n t) -> p h n t", t=L), psN[:, 0:W4].rearrange("p (h n t) -> p h n t", h=4, n=ns),
                       den4, ALU.mult, [pkN, tm['den']], H4('h4'))

                kwt = [T('kwt%d' % i, [128, 512], BF16, sc) for i in range(2)]

                def ml_K(gi, n):
                    ci0, ns, L, o0, tm = g_ctx(gi)
                    ci = ci0 + n; o = chunks[ci].off
                    kw = kwt[ci % 2]
                    wc3 = tm['wco'][0:L, 0:4 * ns].rearrange("p (h n) -> p h n", n=ns)[:, :, n:n + 1]
                    ps, pk = psbig()
                    TRB([(ps[0:L, h * 128:(h + 1) * 128], kf[:, h, o:o + L]) for h in range(4)], H4('kf'), [pk])
                    TT('dve', kw[0:L, 0:512].rearrange("p (h c) -> p h c", c=128), ps[0:L, 0:512].rearrange("p (h c) -> p h c", c=128),
                       wc3.to_broadcast([L, 4, 128]), ALU.mult, [pk, tm['wco']], [kw])

                def ml_C(gi, n):
                    ci0, ns, L, o0, tm = g_ctx(gi)
                    ci = ci0 + n; ch = chunks[ci]; o = ch.off
                    CTv, CT_key = CT_of(ch)
                    kw = kwt[ci % 2]
                    psC, pkC = psbig()
                    MMB([(psC[:, h * 128:(h + 1) * 128], kw[0:L, h * 128:(h + 1) * 128], vtm[0:L, ci, h, 0:128]) for h in range(4)], [kw, vtm], [pkC])
                    psn, pkn = psbig()
                    MMB([(psn[:, h:h + 1], kw[0:L, h * 128:(h + 1) * 128], vtm[0:L, ci, h, 128:129]) for h in range(4)], [kw, vtm], [pkn])
                    TT('dve', CTv[:, :, 0:129], CTv[:, :, 0:129], sc4[:, :, o + L - 1:o + L].to_broadcast([128, 4, 129]), ALU.mult,
                       [CT_key] + H4('sc4'), [CT_key])
                    TT('dve', CTv[:, :, 0:128], CTv[:, :, 0:128], psC[:, 0:512].rearrange("p (h c) -> p h c", c=128), ALU.add, [CT_key, pkC], [CT_key])
                    TT('dve', CTv[:, :, 128], CTv[:, :, 128], psn[:, 0:4], ALU.add, [CT_key, pkn], [CT_key])

                ml_A(0)
                ml_K(0, 0)
                for gi in range(len(mgroups)):
                    if gi + 1 < len(mgroups):
                        ml_A(gi + 1)
                    ml_B(gi)
                    ns_ = mgroups[gi][1]
                    for n in range(ns_):
                        if n + 1 < ns_:
                            ml_K(gi, n + 1)
                        elif gi + 1 < len(mgroups):
                            ml_K(gi + 1, 0)
                        ml_C(gi, n)
                tq = [tmpb, igb, sp_, cacc]
                pks = []
                for h in range(4):
                    ps, pk = psbig()
                    MM(ps[:, 0:NT], [(C['ones'], h4[:, h, :])], ['cst', ('h4', h)], [pk])
                    pks.append((ps, pk))
                for h in range(4):
                    ps, pk = pks[h]
                    STT('dve', h4[:, h, :], ps[:, 0:NT], -1.0 / 128, h4[:, h, :], ALU.mult, ALU.add, [pk, ('h4', h)], [('h4', h)])
                for h in range(4):
                    ACT(tq[h][:], h4[:, h, :], AF.Square, [('h4', h)], [tq[h]])
                pks = []
                for h in range(4):
                    ps, pk = psbig()
                    MM(ps[:, 0:NT], [(C['ones'], tq[h][:])], ['cst', tq[h]], [pk])
                    pks.append((ps, pk))
                for h in range(4):
                    ps, pk = pks[h]
                    ACT(tq[h][:], ps[:, 0:NT], AF.Ln, [pk], [tq[h]], bias=ML_GN_EPS, scale=1.0 / 128)
                for h in range(4):
                    ACT(tq[h][:], tq[h][:], AF.Exp, [tq[h]], [tq[h]], scale=-0.5)
                for h in range(4):
                    STT('dve', h4[:, h, :], h4[:, h, :], pcol(l, 'mlg', h), tq[h][:], ALU.mult, ALU.mult, [('h4', h), pc, tq[h]], [('h4', h)])
                for h in range(4):
                    TT('dve', ybT[:, h, 0:NT], h4[:, h, :], sgo[:, h, :], ALU.mult, [('h4', h), ('sgo', h)], [('ybT', h)])
                if blk_i == 0:
                    for g4 in range(4):
                        g2 = g4 % 2
                        for q4 in range(4):
                            u0 = g4 * 16 + q4 * 4
                            ps, pk = psbig()
                            TRB([(ps[:, i * 128:(i + 1) * 128], CTs[:, u0 + i, 0:128]) for i in range(4)], [('CTs', u0 // 4)], [pk])
                            CPalt(Cst[:, g2 * 16 + q4 * 4:g2 * 16 + q4 * 4 + 4, :], ps[:, 0:512].rearrange("p (u k) -> p u k", k=128), [pk], [('Cst', g2)])
                        P.dma('sp', O['oC'][l, g4 * 4:(g4 + 1) * 4].rearrange("s h v k -> v s h k"),
                              Cst[:, g2 * 16:(g2 + 1) * 16, :].rearrange("p (s h) k -> p s h k", h=4), reads=[('Cst', g2)])
                    CP('dve', nst[:], CTs[:, :, 128], [('CTs', u) for u in range(NS)], [nst])
                    from_fm(O['on'][l], lambda j: nst[:, :], NS * 4, 128, [nst])
                    P.dma('sp', O['om'][l:l + 1, :], mo_s[0:1, :], reads=[('mo_s', u) for u in range(NS * 4)])
                if debug and l == 0 and blk_i == 0:
                    dump('ybT', ybT[:, :, 0:NT], [128, 4, NT], [('ybT', h) for h in range(4)])
                P.barrier()
                P.flush()

        def tail_phase(l, NT):
            with ExitStack() as sc:
                gab = T('gab', [128, 16, NT], F32, sc)
                mrg = T('mrg', [128, NCH, NT], BF16, sc)
                acc = T('acc', [128, NCH, NT], F32, sc)
                tmpm = [T('tmpm%d' % i, [128, NT], F32, sc) for i in range(2)]
                aT = T('aT', [128, NCH, NT], BF16, sc)
                for si in (4, 5):
                    sl = load_slab(I['w_in'][l], 0, NCH, SLABS_IN[si][0], SLABS_IN[si][1])
                    for j in range(8):
                        ps, pk = dense_tile(sl, j * 128, 128, NT)
                        ACT(gab[:, (si - 4) * 8 + j, :], ps[:, 0:NT], AF.Sigmoid, [pk], [('gab', (si - 4) * 8 + j)])
                i = st['slab']; st['slab'] = (i + 1) % 2
                sl = slabs[i]
                P.dma('pool', sl[:, 0:4, 0:D], I['p_a'][l].rearrange("(c p) n -> p c n", p=128), writes=[sl], slab=True)
                P.dma('pool', sl[:, 4:8, 0:D], I['p_b'][l].rearrange("(c p) n -> p c n", p=128), writes=[sl], slab=True)
                YA = [('yaT', h) for h in range(4)]; YB = [('ybT', h) for h in range(4)]
                for j in range(8):
                    psa, pka = psbig()
                    MM(psa[:, 0:NT], [(sl[:, c, j * 128:(j + 1) * 128], yaT[:, c, 0:NT]) for c in range(4)], [sl] + YA, [pka])
                    psb_, pkb = psbig()
                    MM(psb_[:, 0:NT], [(sl[:, 4 + c, j * 128:(j + 1) * 128], ybT[:, c, 0:NT]) for c in range(4)], [sl] + YB, [pkb])
                    t_ = tmpm[j % 2]
                    TT('dve', t_[:], gab[:, j, :], psa[:, 0:NT], ALU.mult, [('gab', j), pka], [t_])
                    TT('dve', gab[:, 8 + j, :], gab[:, 8 + j, :], psb_[:, 0:NT], ALU.mult, [('gab', 8 + j), pkb], [('gab', 8 + j)])
                    TT('dve', mrg[:, j, :], t_[:], gab[:, 8 + j, :], ALU.add, [t_, ('gab', 8 + j)], [('mrg', j)])
                MR = [('mrg', j) for j in range(8)]
                sl = load_slab(I['w_out'][l], 0, NCH, 0, D)
                for j in range(8):
                    ps, pk = dense_tile(sl, j * 128, 128, NT, rhs=mrg, rkeys=MR)
                    CPalt(acc[:, j, :], ps[:, 0:NT], [pk], [acc])
                rmsnorm_add(acc, NT, l, 'post1', sc)
                rmsnorm_u(xT, NT, l, 'pre2', sc)
                AT_ = [('aT', j) for j in range(8)]
                for q in range(4):
                    slu = load_slab(I['w_ff_up'][l], 0, NCH, q * D, (q + 1) * D)
                    sld = load_slab(I['w_ff_down'][l], q * D, NCH, 0, D)
                    for j in range(8):
                        ps, pk = dense_tile(slu, j * 128, 128, NT)
                        t_ = tmpm[j % 2]
                        ACT(t_[:], ps[:, 0:NT], AF.Relu, [pk], [t_])
                        TT('dve', aT[:, j, :], t_[:], t_[:], ALU.mult, [t_], [('aT', j)])
                    for j in range(8):
                        ps, pk = dense_tile(sld, j * 128, 128, NT, rhs=aT, rkeys=AT_)
                        if q == 0:
                            CPalt(acc[:, j, :], ps[:, 0:NT], [pk], [acc])
                        else:
                            TT('dve', acc[:, j, :], acc[:, j, :], ps[:, 0:NT], ALU.add, [acc, pk], [acc])
                rmsnorm_add(acc, NT, l, 'post2', sc)
                P.barrier()
                P.flush()

        def mkchunks(groups):
            chunks = []
            po = 0
            for (off, nseg, L, kind) in groups:
                for s_ in range(nseg):
                    ch = Chunk()
                    ch.off = off + s_ * L; ch.L = L; ch.kind = kind; ch.seq = s_; ch.padoff = po + s_ * 2 * L
                    chunks.append(ch)
                po += nseg * 2 * L
            return chunks

        try:
          stage('setup')
          for blk_i in range(5):
              if blk_i == 0:
                  NT = NMETA + NS * TSL
                  groups = [(0, 1, NMETA, 'p'), (NMETA, NS, TSL, 's')]
                  x_to_fm(I['meta'], NMETA, 0)
                  x_to_fm(I['xs'], NS * TSL, NMETA)
              else:
                  NT = 512
                  groups = [(0, 8, 64, 'p')]
                  for tt_ in range(4):
                      r0 = (blk_i - 1) * 512 + tt_ * 128
                      x_to_fm(I['xp'][r0:r0 + 128, :], 128, tt_ * 128)
              chunks = mkchunks(groups)
              cgroups = [(0, 1, NT, 'p')] if blk_i > 0 else groups
              for l in range(DEPTH):
                  with ExitStack() as sc0:
                      rmsnorm_u(xT, NT, l, 'pre1', sc0)
                      if debug and l == 0 and blk_i == 0:
                          dump('uT', uT[:, :, 0:NT], [128, NCH, NT], UT_ALL)
                      P.barrier()
                      P.flush()
                  stage('norm1')
                  rwkv_phase(l, NT, cgroups, groups, chunks, blk_i)
                  stage('rwkv')
                  mlstm_phase(l, NT, cgroups, chunks, blk_i)
                  stage('mlstm')
                  tail_phase(l, NT)
                  stage('tail')
              if blk_i == 0:
                  x_from_fm(O['ys'], NS * TSL, NMETA)
              else:
                  for tt_ in range(4):
                      r0 = (blk_i - 1) * 512 + tt_ * 128
                      x_from_fm(O['yp'][r0:r0 + 128, :], 128, tt_ * 128)
              P.flush()

          for l in range(DEPTH):
              for hp in range(4):
                  ps, pk = pssmall()
                  TR(ps[:, 0:128], Sbd[l][hp][:], 128, [Sbd[l][hp]], [pk])
                  sg = stg[st['stg']]; st['stg'] ^= 1
                  CPalt(sg[:, 0:128], ps[:, 0:128], [pk], [sg])
                  for h2 in range(2):
                      P.dma('sp', O['pS'][l, hp * 2 + h2], sg[h2 * 64:(h2 + 1) * 64, h2 * 64:(h2 + 1) * 64], reads=[sg])
              from_fm(O['psh'][l], lambda j, l=l: shc[:, l, :], 14, 128, [shc])
              for h in range(4):
                  ps, pk = pssmall()
                  TR(ps[:, 0:128], CTp[l][:, h, 0:128], 128, [CTp[l]], [pk])
                  sg = stg[st['stg']]; st['stg'] ^= 1
                  CPalt(sg[:, 0:128], ps[:, 0:128], [pk], [sg])
                  P.dma('sp', O['pC'][l, h], sg[:, 0:128], reads=[sg])
                  P.dma('sp', O['pn'][l, h:h + 1, :].rearrange("a k -> k a"), CTp[l][:, h, 128:129], reads=[CTp[l]])
              P.dma('sp', O['pm'][l:l + 1, :], mP[0:1, l, :], reads=[('mP', h) for h in range(4)])
              sg = stg[st['stg']]; st['stg'] ^= 1
              for c in range(8):
                  ps, pk = pssmall()
                  TR(ps[0:3, 0:128], cvc[:, l, c, :], 128, [cvc], [pk])
                  CPalt(sg[0:3, c * 128:(c + 1) * 128], ps[0:3, 0:128], [pk], [sg])
              P.dma('sp', O['pcv'][l], sg[0:3, :], reads=[sg])
          stage('end')
        except StopBuild:
          pass
        P.barrier(engines=('pe', 'act', 'dve', 'pool', 'sp'), with_w=True)
        P.flush()
        print("instructions:", P.nins, "waits:", P.nwaits, {k: P.cnt[k] for k in ('pe', 'act', 'dve', 'pool')})
    return nc, dbg


def _consts():
    c = np.zeros((len(CONST_NAMES), 128, 128), np.float32)
    ix = {n: i for i, n in enumerate(CONST_NAMES)}
    c[ix['ident']] = np.eye(128, dtype=np.float32)
    c[ix['ones']] = 1.0
    c[ix['blk'], 0:64, 0:64] = 1.0
    c[ix['blk'], 64:128, 64:128] = 1.0
    for L in (64, 16, 8):
        s = np.arange(L)[:, None]
        t = np.arange(L)[None, :]
        lt = (s < t).astype(np.float32)
        le = (s <= t).astype(np.float32)
        for h in range(2):
            c[ix['mlt%d' % L], h * L:(h + 1) * L, h * L:(h + 1) * L] = lt
            c[ix['mgt%d' % L], h * L:(h + 1) * L, h * L:(h + 1) * L] = lt.T
            c[ix['mle%d' % L], h * L:(h + 1) * L, 0:L] = le
        c[ix['mneg%d' % L], 0:L, 0:L] = np.where(s <= t, 0.0, -30000.0)
    return np.ascontiguousarray(c.transpose(1, 0, 2).reshape(128, -1))


_CACHE = {}


def _prep_inputs(inp, n_cores=8):
    f = lambda a: np.ascontiguousarray(np.asarray(a, dtype=np.float32))
    prm = np.zeros((DEPTH, NPRM, 128), np.float32)
    for l in range(DEPTH):
        rows = [inp['pre1'][l], inp['post1'][l], inp['pre2'][l], inp['post2'][l], inp['rw_mu'][l], inp['rw_w0'][l], inp['rw_a0'][l],
                inp['rw_k_k'][l], inp['rw_k_a'][l], inp['rw_r_k'][l], inp['rw_gn_g'][l], inp['rw_gn_b'][l], inp['ml_conv_w'][l],
                inp['ml_conv_b'][l], inp['ml_gn_g'][l]]
        prm[l] = np.concatenate([np.asarray(r, np.float32).reshape(-1, 128) for r in rows], axis=0)
    gbias = np.concatenate([np.concatenate([np.asarray(inp['ml_i_bias'][l], np.float32), np.asarray(inp['ml_f_bias'][l], np.float32)])
                            for l in range(DEPTH)]).reshape(1, 16)
    shared = {'meta': f(inp['meta_tokens']), 'w_in': f(inp['w_in']), 'w_up': f(inp['rw_w_up']), 'a_up': f(inp['rw_a_up']),
              'g_up': f(inp['rw_g_up']), 'p_a': f(inp['p_a']), 'p_b': f(inp['p_b']), 'w_out': f(inp['w_out']),
              'w_ff_up': f(inp['w_ff_up']), 'w_ff_down': f(inp['w_ff_down']), 'prm': prm, 'gbias': gbias, 'consts': _consts()}
    maps = []
    xp = np.asarray(inp['x_prompt'], np.float32)
    xs = np.asarray(inp['x_sample'], np.float32)
    for i in range(n_cores):
        sl = slice(i * NS, (i + 1) * NS)
        m = dict(shared)
        m['xp'] = f(xp[i])
        m['xs'] = f(xs[sl].reshape(NS * TSL, D))
        m['sS'] = f(np.asarray(inp['state_rwkv_S'])[:, sl])
        m['ssh'] = f(np.asarray(inp['state_rwkv_shift'])[:, sl])
        m['sC'] = f(np.asarray(inp['state_mlstm_C'])[:, sl])
        m['sn'] = f(np.asarray(inp['state_mlstm_n'])[:, sl])
        m['sm'] = f(np.asarray(inp['state_mlstm_m'])[:, sl].reshape(DEPTH, NS * 4))
        m['scv'] = f(np.asarray(inp['state_mlstm_conv'])[:, sl].reshape(DEPTH, NS * 3, D))
        maps.append(m)
    return maps


def _gather(results, n_cores=8):
    R = results
    cat = lambda k, ax: np.concatenate([R[i][k] for i in range(n_cores)], axis=ax)
    y_prompt = np.stack([R[i]['yp'] for i in range(n_cores)])
    y_sample = np.concatenate([R[i]['ys'].reshape(NS, TSL, D) for i in range(n_cores)], axis=0)
    pS = np.stack([R[i]['pS'] for i in range(n_cores)], axis=1)
    psh = np.stack([R[i]['psh'].reshape(DEPTH, 1792) for i in range(n_cores)], axis=1)
    pC = np.stack([R[i]['pC'] for i in range(n_cores)], axis=1)
    pn = np.stack([R[i]['pn'] for i in range(n_cores)], axis=1)
    pm = np.stack([R[i]['pm'] for i in range(n_cores)], axis=1)
    pcv = np.stack([R[i]['pcv'] for i in range(n_cores)], axis=1)
    oS = cat('oS', 1)
    osh = cat('osh', 1)
    oC = cat('oC', 1)
    on = np.concatenate([R[i]['on'].reshape(DEPTH, NS, 4, 128) for i in range(n_cores)], axis=1)
    om = np.concatenate([R[i]['om'].reshape(DEPTH, NS, 4) for i in range(n_cores)], axis=1)
    ocv = np.concatenate([R[i]['ocv'].reshape(DEPTH, NS, 3, D) for i in range(n_cores)], axis=1)
    outs = (y_prompt, y_sample, pS, psh, pC, pn, pm, pcv, oS, osh, oC, on, om, ocv)
    return tuple(np.ascontiguousarray(o.astype(np.float32)) for o in outs)


def kernel(**inputs):
    if 'nc' not in _CACHE:
        _CACHE['nc'] = build(False)[0]
    nc = _CACHE['nc']
    maps = _prep_inputs(inputs)
    res = run_bass_kernel_spmd(nc, maps, core_ids=list(range(8)))
    return _gather(res.results)
```

```python
import numpy as np
from contextlib import ExitStack
import concourse.bass as bass
import concourse.mybir as mybir
from concourse.bass_utils import run_bass_kernel_spmd

F32 = mybir.dt.float32
BF16 = mybir.dt.bfloat16
AF = mybir.ActivationFunctionType
ALU = mybir.AluOpType

NDS = 40
NWS = 6
NGS = 8
RDT = BF16
F32R = mybir.dt.float32r
USE_F32R = True

D = 1024
NCH = 8
NIN = 5896
SEQ = 2048
NMETA = 16
DEPTH = 2
NS = 16
TSL = 8
SLABW = 1032
SLABS_IN = [(0, 1024), (1024, 1792), (1792, 2816), (2816, 3848), (3848, 4872), (4872, 5896)]
RMS_EPS = 1e-6
RW_GN_EPS = 64e-5
ML_GN_EPS = 1e-5
NPRM = 118
PR = {}
_o = 0
for _n, _c in [('pre1', 8), ('post1', 8), ('pre2', 8), ('post2', 8), ('mu', 14), ('w0', 4), ('a0', 4), ('kk', 4),
               ('ka', 4), ('rk', 4), ('gng', 4), ('gnb', 4), ('cw', 32), ('cb', 8), ('mlg', 4)]:
    PR[_n] = _o
    _o += _c
assert _o == NPRM
CONST_NAMES = ['ident', 'ones', 'blk', 'mlt64', 'mgt64', 'mle64', 'mneg64', 'mlt16', 'mgt16', 'mle16', 'mneg16',
               'mlt8', 'mgt8', 'mle8', 'mneg8']


class Prog:
    ENGS = ['pe', 'act', 'dve', 'pool', 'sp']

    def __init__(self, nc, block, es):
        self.nc = nc
        self.bfn = {'pe': block.tensor, 'act': block.scalar, 'dve': block.vector,
                    'pool': block.gpsimd, 'sp': block.sync}
        self.semh = {}
        for e in ['pe', 'act', 'dve', 'pool']:
            self.semh[e] = es.enter_context(nc.semaphore("sem_" + e))
        for i in range(NDS):
            self.semh['d%d' % i] = es.enter_context(nc.semaphore("dsem%d" % i))
        for i in range(NWS):
            self.semh['w%d' % i] = es.enter_context(nc.semaphore("wsem%d" % i))
        for i in range(NGS):
            self.semh['g%d' % i] = es.enter_context(nc.semaphore("gsem%d" % i))
        self.gnext = 0
        self.cnt = {k: 0 for k in self.semh}
        self.dnext = 0
        self.wnext = 0
        self.buf = {e: [] for e in self.ENGS}
        self.lastw = {}
        self.readers = {}
        self.seen = {e: {} for e in self.ENGS}
        self.nins = 0
        self.clock = {}
        self.tokseq = {}
        self.gseq = 0
        self.nwaits = 0

    KEYMAP = {}

    @staticmethod
    def _k(x):
        if isinstance(x, (str, tuple)):
            return x
        return Prog.KEYMAP.get(x.name, x.name)

    def _deps(self, eng, reads, writes):
        need = {}
        for k in reads:
            w = self.lastw.get(k)
            if w is not None and need.get(w[0], 0) < w[1]:
                need[w[0]] = w[1]
        for k in writes:
            w = self.lastw.get(k)
            if w is not None and need.get(w[0], 0) < w[1]:
                need[w[0]] = w[1]
            for sid, v in self.readers.get(k, {}).items():
                if need.get(sid, 0) < v:
                    need[sid] = v
        out = []
        seen = self.seen[eng]
        items = sorted(need.items(), key=lambda kv: -self.tokseq.get(kv, 0))
        for sid, v in items:
            if seen.get(sid, 0) >= v:
                continue
            seen[sid] = v
            out.append((sid, v))
            ck = self.clock.get((sid, v))
            if ck:
                for s2, v2 in ck.items():
                    if seen.get(s2, 0) < v2:
                        seen[s2] = v2
        out.reverse()
        return out

    def _commit(self, tok, reads, writes, eng=None):
        self.gseq += 1
        self.tokseq[tok] = self.gseq
        if eng is not None:
            self.clock[tok] = dict(self.seen[eng])
        for k in reads:
            d = self.readers.setdefault(k, {})
            if d.get(tok[0], 0) < tok[1]:
                d[tok[0]] = tok[1]
        for k in writes:
            self.lastw[k] = tok
            self.readers[k] = {}

    def op(self, eng, fn, reads=(), writes=(), attach=None, multi=False):
        reads = [self._k(x) for x in reads]
        writes = [self._k(x) for x in writes]
        waits = self._deps(eng, reads, writes)
        self.nwaits += len(waits)
        self.cnt[eng] += 1
        tok = (eng, self.cnt[eng])
        self._commit(tok, reads, writes, eng)
        semh = self.semh
        sem = semh[eng]
        if attach is None:
            attach = eng in ('act', 'dve', 'pe')

        def run(e):
            if attach and waits:
                for sid, v in waits[:-1]:
                    e.wait_ge(semh[sid], v)
                if multi:
                    done = [False]

                    def hook(ins):
                        if not done[0]:
                            ins._wait_ge(semh[waits[-1][0]], waits[-1][1])
                            done[0] = True
                        return ins
                    fn(e, hook).then_inc(sem, 1)
                else:
                    ins = fn(e)
                    ins._wait_ge(semh[waits[-1][0]], waits[-1][1])
                    ins.then_inc(sem, 1)
            elif multi:
                for sid, v in waits:
                    e.wait_ge(semh[sid], v)
                fn(e, lambda ins: ins).then_inc(sem, 1)
            else:
                for sid, v in waits:
                    e.wait_ge(semh[sid], v)
                fn(e).then_inc(sem, 1)
        self.buf[eng].append(run)
        self.nins += 1

    def dma(self, q, out, in_, reads=(), writes=(), slab=False, **kw):
        reads = [self._k(x) for x in reads]
        writes = [self._k(x) for x in writes]
        if slab:
            sid = 'w%d' % self.wnext
            self.wnext = (self.wnext + 1) % NWS
        elif q == 'pool':
            sid = 'g%d' % self.gnext
            self.gnext = (self.gnext + 1) % NGS
        else:
            sid = 'd%d' % self.dnext
            self.dnext = (self.dnext + 1) % NDS
        waits = self._deps(q, reads, writes)
        prev = self.cnt[sid]
        if prev > 0 and self.seen[q].get(sid, 0) < prev:
            self.seen[q][sid] = prev
            waits.append((sid, prev))
        self.cnt[sid] = prev + 16
        tok = (sid, prev + 16)
        self._commit(tok, reads, writes, q)
        semh = self.semh

        def run(e):
            for s, v in waits:
                e.wait_ge(semh[s], v)
            e.dma_start(out=out, in_=in_, **kw).then_inc(semh[sid], 16)
        self.buf[q].append(run)
        self.nins += 1

    def barrier(self, engines=('act', 'dve', 'sp'), with_w=False):
        for e in engines:
            waits = []
            for sid, v in self.cnt.items():
                if v == 0 or (sid.startswith('w') and not with_w):
                    continue
                if self.seen[e].get(sid, 0) >= v:
                    continue
                self.seen[e][sid] = v
                waits.append((sid, v))
            if waits:
                semh = self.semh

                def run(en, waits=waits):
                    for s, v in waits:
                        en.wait_ge(semh[s], v)
                self.buf[e].append(run)

    def flush(self):
        for e in self.ENGS:
            fns = self.buf[e]
            if not fns:
                continue
            self.buf[e] = []

            def body(en, fns=fns):
                for f in fns:
                    f(en)
            self.bfn[e](body)


class Chunk:
    pass


class StopBuild(Exception):
    pass


def build(debug=False, stop=None):
    nc = bass.Bass("TRN2", target_bir_lowering=False)
    stage_i = [0]
    I = {}

    def din(name, shape):
        I[name] = nc.dram_tensor(name, list(shape), F32, kind="ExternalInput").ap()
        return I[name]

    O = {}

    def dout(name, shape):
        O[name] = nc.dram_tensor(name, list(shape), F32, kind="ExternalOutput").ap()
        return O[name]

    din('xp', (SEQ, D)); din('xs', (NS * TSL, D))
    din('sS', (DEPTH, NS, 8, 64, 64)); din('ssh', (DEPTH, NS, 1792)); din('sC', (DEPTH, NS, 4, 128, 128))
    din('sn', (DEPTH, NS, 4, 128)); din('sm', (DEPTH, NS * 4)); din('scv', (DEPTH, NS * 3, D))
    din('meta', (NMETA, D)); din('w_in', (DEPTH, D, NIN)); din('w_up', (DEPTH, 64, 512)); din('a_up', (DEPTH, 64, 512))
    din('g_up', (DEPTH, 128, 512)); din('p_a', (DEPTH, 512, D)); din('p_b', (DEPTH, 512, D)); din('w_out', (DEPTH, D, D))
    din('w_ff_up', (DEPTH, D, 4 * D)); din('w_ff_down', (DEPTH, 4 * D, D))
    din('prm', (DEPTH, NPRM, 128)); din('gbias', (1, 16)); din('consts', (128, len(CONST_NAMES) * 128))

    dout('yp', (SEQ, D)); dout('ys', (NS * TSL, D))
    dout('pS', (DEPTH, 8, 64, 64)); dout('psh', (DEPTH, 14, 128)); dout('pC', (DEPTH, 4, 128, 128)); dout('pn', (DEPTH, 4, 128))
    dout('pm', (DEPTH, 4)); dout('pcv', (DEPTH, 3, D))
    dout('oS', (DEPTH, NS, 8, 64, 64)); dout('osh', (DEPTH, NS, 1792)); dout('oC', (DEPTH, NS, 4, 128, 128))
    dout('on', (DEPTH, NS * 4, 128)); dout('om', (DEPTH, NS * 4)); dout('ocv', (DEPTH, NS * 3, D))
    dbg = {}

    with ExitStack() as es:
        used = {}

        def T(name, shape, dt=F32, st=None):
            n = used.get(name, 0)
            used[name] = n + 1
            un = name if n == 0 else "%s_r%d" % (name, n)
            Prog.KEYMAP[un] = name
            return (st if st is not None else es).enter_context(nc.sbuf_tensor(un, list(shape), dt))

        cst = T('cst', [128, len(CONST_NAMES), 128])
        C = {n: cst[:, i, :] for i, n in enumerate(CONST_NAMES)}
        ones_bf = T('ones_bf', [128, 128], BF16)
        blk_bf = T('blk_bf', [128, 128], BF16)
        ident_bf = T('ident_bf', [128, 128], BF16)
        pc = T('pc', [128, DEPTH, 128])
        npc = T('npc', [128, DEPTH, 8])
        gb = T('gb', [128, 16])
        ngb = T('ngb', [128, 16])
        lw_wa = T('lw_wa', [128, DEPTH, 512], BF16)
        lw_g = T('lw_g', [128, DEPTH, 512], BF16)
        slabs = [T('slab%d' % i, [128, NCH, SLABW], BF16) for i in range(2)]
        xT = T('xT', [128, NCH, 512])
        uT = T('uT', [128, NCH, 512], BF16)
        yaT = T('yaT', [128, 4, 512], BF16)
        ybT = T('ybT', [128, 4, 512], BF16)
        Sbd = [[T('Sbd%d_%d' % (l, hp), [128, 128]) for hp in range(4)] for l in range(DEPTH)]
        CTp = [T('CT%d' % l, [128, 4, 132]) for l in range(DEPTH)]
        mP = T('mP', [128, DEPTH, 4])
        shc = T('shc', [128, DEPTH, 14])
        cvc = T('cvc', [128, DEPTH, 8, 3])
        stg = [T('stg%d' % i, [128, D]) for i in range(2)]
        pss = [es.enter_context(nc.psum_tensor('psb%d' % i, [128, 512], F32)) for i in range(8)]

        block = es.enter_context(nc.Block())
        P = Prog(nc, block, es)

        def stage(name):
            stage_i[0] += 1
            if stop is not None and stage_i[0] >= stop:
                print("STOP at stage", stage_i[0], name)
                P.barrier(engines=('pe', 'act', 'dve', 'pool', 'sp'), with_w=True)
                P.flush()
                raise StopBuild()

        st = {'big': 0, 'small': 0, 'slab': 0, 'stg': 0, 'alt': 0}

        def psbig():
            i = st['big']; st['big'] = (i + 1) % 8
            return pss[i], ('psB', i)

        def pssmall():
            return psbig()

        def kn(x):
            return Prog._k(x)

        def TT(eng, out, in0, in1, op, r, w):
            P.op(eng, lambda e: e.tensor_tensor(out=out, in0=in0, in1=in1, op=op), r, w)

        def TS(eng, out, in0, s1, op0, r, w, s2=None, op1=None):
            if op1 is None:
                P.op(eng, lambda e: e.tensor_scalar(out=out, in0=in0, scalar1=s1, scalar2=None, op0=op0), r, w)
            else:
                P.op(eng, lambda e: e.tensor_scalar(out=out, in0=in0, scalar1=s1, scalar2=s2, op0=op0, op1=op1), r, w)

        def STT(eng, out, in0, scalar, in1, op0, op1, r, w, accum_out=None):
            if accum_out is None:
                P.op(eng, lambda e: e.scalar_tensor_tensor(out=out, in0=in0, scalar=scalar, in1=in1, op0=op0, op1=op1), r, w)
            else:
                P.op(eng, lambda e: e.scalar_tensor_tensor(out=out, in0=in0, scalar=scalar, in1=in1, op0=op0, op1=op1,
                                                           accum_out=accum_out), r, w)

        def ACT(out, in_, func, r, w, bias=0.0, scale=1.0):
            P.op('act', lambda e: e.activation(out=out, in_=in_, func=func, bias=bias, scale=scale), r, w)

        def CP(eng, out, in_, r, w):
            if eng == 'act':
                P.op('act', lambda e: e.copy(out=out, in_=in_), r, w)
            else:
                P.op(eng, lambda e: e.tensor_copy(out=out, in_=in_), r, w)

        def CPalt(out, in_, r, w):
            st['alt'] = (st['alt'] + 1) % 4
            CP('dve' if st['alt'] == 0 else 'act', out, in_, r, w)

        def MSET(eng, ap, val, w):
            P.op(eng, lambda e: e.memset(ap, val), (), w)

        def SCAN(out, d0, d1, init, op0, op1, r, w):
            P.op('dve', lambda e: e.tensor_tensor_scan(out=out, data0=d0, data1=d1, initial=init, op0=op0, op1=op1), r, w)

        def RECIP(out, in_, r, w):
            P.op('dve', lambda e: e.reciprocal(out=out, in_=in_), r, w)

        def MM(out, pairs, r, w):
            def fn(e, hook):
                n = len(pairs)
                ins = None
                for i, (l_, r_) in enumerate(pairs):
                    ins = hook(e.matmul(out, lhsT=l_, rhs=r_, start=(i == 0), stop=(i == n - 1)))
                return ins
            P.op('pe', fn, r, w, multi=True)

        def MMB(trip, r, w):
            def fn(e, hook):
                ins = None
                for (o_, l_, r_) in trip:
                    ins = hook(e.matmul(o_, lhsT=l_, rhs=r_, start=True, stop=True))
                return ins
            P.op('pe', fn, r, w, multi=True)

        def MMG(groups_, r, w):
            def fn(e, hook):
                ins = None
                for (o_, pairs) in groups_:
                    n = len(pairs)
                    for i, (l_, r_) in enumerate(pairs):
                        ins = hook(e.matmul(o_, lhsT=l_, rhs=r_, start=(i == 0), stop=(i == n - 1)))
                return ins
            P.op('pe', fn, r, w, multi=True)

        def TRB(pairs, r, w):
            def fn(e, hook):
                ins = None
                for (o_, i_) in pairs:
                    ins = hook(e.transpose(out=o_, in_=i_, identity=C['ident'][:, :]))
                return ins
            P.op('pe', fn, list(r) + ['cst'], w, multi=True)

        def TR(out, in_, n_in_part, r, w):
            P.op('pe', lambda e: e.transpose(out=out, in_=in_, identity=C['ident'][0:n_in_part, 0:n_in_part]), r + ['cst'], w)

        def dump(name, ap, shape, r):
            if not debug:
                return
            dbg[name] = dout('dbg_' + name, shape)
            P.dma('pool', dbg[name], ap, reads=r)

        def load_slab(src2d, r0, nrow_chunks, c0, c1):
            i = st['slab']; st['slab'] = (i + 1) % 2
            sl = slabs[i]
            src = src2d[r0:r0 + nrow_chunks * 128, c0:c1].rearrange("(c p) n -> p c n", p=128)
            P.dma('pool', sl[:, 0:nrow_chunks, 0:c1 - c0], src, writes=[sl], slab=True)
            return sl

        def to_fm(dst_fn, rows_ap, R, F, dst_keys):
            i = st['stg']; st['stg'] ^= 1
            sg = stg[i]
            P.dma('sp', sg[0:R, 0:F], rows_ap, writes=[sg])
            for j in range(F // 128):
                ps, pk = pssmall()
                TR(ps[:, 0:R], sg[0:R, j * 128:(j + 1) * 128], R, [sg], [pk])
                CPalt(dst_fn(j), ps[:, 0:R], [pk], dst_keys)

        def from_fm(rows_ap, src_fn, R, F, src_keys):
            i = st['stg']; st['stg'] ^= 1
            sg = stg[i]
            for j in range(F // 128):
                ps, pk = pssmall()
                TR(ps[0:R, 0:128], src_fn(j), 128, src_keys, [pk])
                CPalt(sg[0:R, j * 128:(j + 1) * 128], ps[0:R, 0:128], [pk], [sg])
            P.dma('sp', rows_ap, sg[0:R, 0:F], reads=[sg])

        def x_to_fm(rows_ap, R, t0):
            i = st['stg']; st['stg'] ^= 1
            sg = stg[i]
            P.dma('sp', sg[0:R, 0:D], rows_ap, writes=[sg])
            for j0 in (0, 4):
                ps, pk = psbig()
                P.op('pe', lambda e, hook, ps=ps, j0=j0: [hook(e.transpose(out=ps[:, k * R:(k + 1) * R], in_=sg[0:R, (j0 + k) * 128:(j0 + k + 1) * 128],
                                                                      identity=C['ident'][0:R, 0:R])) for k in range(4)][-1], [sg, 'cst'], [pk], multi=True)
                CPalt(xT[:, j0:j0 + 4, t0:t0 + R], ps[:, 0:4 * R].rearrange("p (k r) -> p k r", r=R), [pk], [xT])

        def x_from_fm(rows_ap, R, t0):
            i = st['stg']; st['stg'] ^= 1
            sg = stg[i]
            for j0 in (0, 4):
                ps, pk = psbig()
                P.op('pe', lambda e, hook, ps=ps, j0=j0: [hook(e.transpose(out=ps[0:R, k * 128:(k + 1) * 128], in_=xT[:, j0 + k, t0:t0 + R],
                                                                      identity=C['ident'][:, :])) for k in range(4)][-1], [xT, 'cst'], [pk], multi=True)
                CPalt(sg[0:R, j0 * 128:(j0 + 4) * 128], ps[0:R, 0:512], [pk], [sg])
            P.dma('sp', rows_ap, sg[0:R, 0:D], reads=[sg])

        P.dma('sp', cst[:].rearrange("p a b -> p (a b)"), I['consts'], writes=[cst])
        CP('dve', ones_bf[:], C['ones'], [cst], [ones_bf])
        CP('dve', blk_bf[:], C['blk'], [cst], [blk_bf])
        CP('dve', ident_bf[:], C['ident'], [cst], [ident_bf])
        for l in range(DEPTH):
            to_fm(lambda j, l=l: pc[:, l, 0:NPRM], I['prm'][l], NPRM, 128, [pc])
        P.dma('sp', gb[:], I['gbias'].partition_broadcast(128), writes=[gb])
        TS('dve', ngb[:], gb[:], -1.0, ALU.mult, [gb], [ngb])
        for l in range(DEPTH):
            TS('dve', npc[:, l, 0:8], pc[:, l, PR['w0']:PR['w0'] + 8], -1.0, ALU.mult, [pc], [npc])
            P.dma('pool', lw_wa[0:64, l, :], I['w_up'][l], writes=[lw_wa])
            P.dma('pool', lw_wa[64:128, l, :], I['a_up'][l], writes=[lw_wa])
            P.dma('pool', lw_g[:, l, :], I['g_up'][l], writes=[lw_g])
            for hp in range(4):
                MSET('dve', Sbd[l][hp][:], 0.0, [Sbd[l][hp]])
        MSET('dve', mP[:], 0.0, [('mP', h) for h in range(4)])
        for l in range(DEPTH):
            MSET('dve', CTp[l][:], 0.0, [CTp[l]])
        MSET('dve', shc[:], 0.0, [shc])
        MSET('dve', cvc[:], 0.0, [cvc])

        def pcol(l, name, j):
            return pc[:, l, PR[name] + j:PR[name] + j + 1]

        def rmsnorm_u(src, NT, l, gname, sc):
            sq = T('nsq', [128, NCH, NT], BF16, sc)
            rstd = T('nrstd', [128, NT], F32, sc)
            P.op('act', lambda e: e.activation(out=sq[:], in_=src[:, :, 0:NT], func=AF.Square), [src], [sq])
            ps, pk = psbig()
            MM(ps[:, 0:NT], [(ones_bf[:], sq[:, c, :]) for c in range(NCH)], [ones_bf, sq], [pk])
            ACT(rstd[:], ps[:, 0:NT], AF.Ln, [pk], [rstd], bias=RMS_EPS, scale=1.0 / D)
            ACT(rstd[:], rstd[:], AF.Exp, [rstd], [rstd], scale=-0.5)
            for c in range(NCH):
                STT('dve', uT[:, c, 0:NT], src[:, c, 0:NT], pcol(l, gname, c), rstd[:], ALU.mult, ALU.mult,
                    [src, rstd, pc], [('uT', c)])

        def rmsnorm_add(acc, NT, l, gname, sc):
            sq = T('asq', [128, NCH, NT], BF16, sc)
            rstd = T('arstd', [128, NT], F32, sc)
            P.op('act', lambda e: e.activation(out=sq[:], in_=acc[:, :, 0:NT], func=AF.Square), [acc], [sq])
            ps, pk = psbig()
            MM(ps[:, 0:NT], [(ones_bf[:], sq[:, c, :]) for c in range(NCH)], [ones_bf, sq], [pk])
            ACT(rstd[:], ps[:, 0:NT], AF.Ln, [pk], [rstd], bias=RMS_EPS, scale=1.0 / D)
            ACT(rstd[:], rstd[:], AF.Exp, [rstd], [rstd], scale=-0.5)
            for c in range(NCH):
                STT('dve', acc[:, c, 0:NT], acc[:, c, 0:NT], pcol(l, gname, c), rstd[:], ALU.mult, ALU.mult,
                    [acc, rstd, pc], [acc])
            P.op('dve', lambda e: e.tensor_tensor(out=xT[:, :, 0:NT], in0=xT[:, :, 0:NT], in1=acc[:, :, 0:NT], op=ALU.add),
                 [acc, xT], [xT])

        UT_ALL = [('uT', c) for c in range(NCH)]

        def dense_tile(sl, col0, ncols, NT, nch=NCH, rhs=None, rkeys=None):
            ps, pk = psbig()
            if rhs is None:
                rhs = uT; rkeys = UT_ALL
            MM(ps[0:ncols, 0:NT], [(sl[:, c, col0:col0 + ncols], rhs[:, c, 0:NT]) for c in range(nch)], [sl] + list(rkeys), [pk])
            return ps, pk

        def rwkv_phase(l, NT, sgroups, groups, chunks, blk_i):
            with ExitStack() as sc:
                zs = T('zs', [128, 12, NT], F32, sc)
                lora = T('lora', [128, NT], BF16, sc)
                sgl = T('sgl', [128, NT], BF16, sc)
                zraw = [T('zraw0', [128, NT + 32], F32, sc)] * 2
                dtmp = [T('dtmp0', [128, NT], F32, sc)] * 2
                if blk_i == 0:
                    shin = T('shin', [128, 14, NS], F32, sc)
                    shout = T('shout', [128, 14, NS], F32, sc)
                    to_fm(lambda j: shin[:, j, :], I['ssh'][l][:, 0:1024], NS, 1024, [shin])
                    to_fm(lambda j: shin[:, 8 + j, :], I['ssh'][l][:, 1024:1792], NS, 768, [shin])
                    Sst = T('Sst', [128, 32, 128], F32, sc)
                    Sbs = T('Sbs', [128, NS * 4, 128], F32, sc)
                    MSET('dve', Sst[:], 0.0, [('Sst', i) for i in range(8)])
                    src = I['sS'][l].rearrange("s (hp h2) v k -> h2 v s hp k", h2=2)
                    Sst4 = Sst[:].rearrange("p (s hp) k -> p s hp k", hp=4)
                    for half in range(2):
                        for h2 in range(2):
                            P.dma('sp', Sst4[h2 * 64:(h2 + 1) * 64, :, :, h2 * 64:(h2 + 1) * 64], src[h2][:, half * 8:(half + 1) * 8],
                                  writes=[('Sst', s8) for s8 in range(8)])
                        for s8 in range(8):
                            u0 = (half * 8 + s8) * 4
                            ps, pk = psbig()
                            TRB([(ps[:, i * 128:(i + 1) * 128], Sst[:, s8 * 4 + i, :]) for i in range(4)], [('Sst', s8)], [pk])
                            CPalt(Sbs[:, u0:u0 + 4, :], ps[:, 0:512].rearrange("p (u k) -> p u k", k=128), [pk], [('Sbs', u0 + i) for i in range(4)])
                slA = load_slab(I['w_in'][l], 0, NCH, SLABS_IN[0][0], SLABS_IN[0][1])
                slB = load_slab(I['w_in'][l], 0, NCH, SLABS_IN[1][0], SLABS_IN[1][1])
                for j in range(14):
                    sl = slA if j < 8 else slB
                    ps, pk = dense_tile(sl, (j % 8 if j < 8 else j - 8) * 128, 128, NT)
                    zr = zraw[j % 2]; dt_ = dtmp[j % 2]
                    off_r = 0
                    for (off, nseg, L, kind) in sgroups:
                        zv = zr[:, off_r:off_r + nseg * (L + 1)].rearrange("p (s t) -> p s t", t=L + 1)
                        pv = ps[:, off:off + nseg * L].rearrange("p (s t) -> p s t", t=L)
                        CP('act', zv[:, :, 1:L + 1], pv, [pk], [zr])
                        if kind == 'p':
                            CP('dve', zv[:, 0, 0:1], shc[:, l, j:j + 1], [shc], [zr])
                        else:
                            CP('dve', zv[:, :, 0], shin[:, j, :], [shin], [zr])
                        dv = dt_[:, off:off + nseg * L].rearrange("p (s t) -> p s t", t=L)
                        TT('dve', dv, zv[:, :, 0:L], pv, ALU.subtract, [zr, pk], [dt_])
                        if kind == 'p':
                            CP('dve', shc[:, l, j:j + 1], zv[:, 0, L:L + 1], [zr], [shc])
                        else:
                            CP('dve', shout[:, j, :], zv[:, :, L], [zr], [shout])
                        off_r += nseg * (L + 1)
                    if j < 12:
                        STT('dve', zs[:, j, :], dt_[:, 0:NT], pcol(l, 'mu', j), ps[:, 0:NT], ALU.mult, ALU.add,
                            [dt_, pk, pc], [('zs', j)])
                    elif j == 12:
                        tmpz = dtmp[j % 2]
                        STT('dve', tmpz[:, 0:NT], dt_[:, 0:NT], pcol(l, 'mu', j), ps[:, 0:NT], ALU.mult, ALU.add,
                            [dt_, pk, pc], [dt_])
                        ACT(lora[0:64, :], tmpz[0:64, 0:NT], AF.Tanh, [dt_], [lora])
                        CP('act', lora[64:128, :], tmpz[64:128, 0:NT], [dt_], [lora])
                    else:
                        tmpz = dtmp[j % 2]
                        STT('dve', tmpz[:, 0:NT], dt_[:, 0:NT], pcol(l, 'mu', j), ps[:, 0:NT], ALU.mult, ALU.add,
                            [dt_, pk, pc], [dt_])
                        ACT(sgl[:], tmpz[:, 0:NT], AF.Sigmoid, [dt_], [sgl])
                if blk_i == 0:
                    from_fm(O['osh'][l][:, 0:1024], lambda j: shout[:, j, :], NS, 1024, [shout])
                    from_fm(O['osh'][l][:, 1024:1792], lambda j: shout[:, 8 + j, :], NS, 768, [shout])
                if debug and l == 0 and blk_i == 0:
                    dump('zs', zs[:].rearrange("p a b -> p (a b)"), [128, 12 * NT], [('zs', j) for j in range(12)])

                a_t = T('a_t', [128, NT], F32, sc); kap = T('kap', [128, NT], F32, sc)
                kmod = T('kmod', [128, NT], F32, sc); b_t = T('b_t', [128, NT], F32, sc); ew = T('ew', [128, NT], F32, sc)
                cl = T('cl', [128, NT], F32, sc); E3 = a_t
                sqk = T('sqk', [128, NT], BF16, sc)
                tmpa = dtmp[0]
                npad = sum(nseg * 2 * L for (off, nseg, L, kind) in groups)
                E1p = [T('E1_%d' % i, [128, NT], F32, sc) for i in range(2)]
                E2p = [T('E2_%d' % i, [128, NT], F32, sc) for i in range(2)]
                rtp = [T('rt_%d' % i, [128, NT], F32, sc) for i in range(2)]
                bonp = [T('bon_%d' % i, [128, NT], F32, sc) for i in range(2)]
                g_tp = [T('g_t_%d' % i, [128, NT], F32, sc) for i in range(2)]
                rtbp = [T('rtb_%d' % i, [128, NT], RDT, sc) for i in range(2)]
                padp = [{n: T('pad%s_%d' % (n, i), [128, npad], RDT, sc) for n in 'kbqv'} for i in range(2)]
                for i in range(2):
                    for n in 'kbqv':
                        MSET('dve', padp[i][n][:], 0.0, [padp[i][n]])
                RW = [128, 512]
                wk = {n: T('w' + n, RW, RDT, sc) for n in ['bh', 'kh', 'PT', 'Vbd', 'Bbd', 'Kbd', 'Kh', 'U0', 'Ktm', 'PVs', 'Zb']}
                for n in ['A0', 'A0T', 'A1', 'A1T', 'Z']:
                    wk[n] = T('w' + n, RW, F32, sc)
                wk['QbT'] = T('wQbT', [128, 256], RDT, sc)
                wk['QkT'] = T('wQkT', [128, 256], RDT, sc)
                wk['gI'] = T('wgI', RW, F32, sc)
                kps = [{'Mx': T('kMx0', RW, F32, sc), 'D0': T('kD00', RW, F32, sc),
                        'Rh': T('kRh0', [128, 256], F32, sc), 'Y0': T('kY00', [128, 256], F32, sc)}] * 2
                batches = []
                cur = []
                for ch in chunks:
                    if cur and (cur[0].L != ch.L or len(cur) >= 4):
                        batches.append(cur); cur = []
                    cur.append(ch)
                if cur:
                    batches.append(cur)
                ui = [0]
                bi = [0]
                def pre_gen(hp):
                    E1 = E1p[hp % 2]; E2 = E2p[hp % 2]; rt = rtp[hp % 2]; bon = bonp[hp % 2]; g_t = g_tp[hp % 2]; rtb = rtbp[hp % 2]
                    padk = padp[hp % 2]['k']; padb = padp[hp % 2]['b']; padq = padp[hp % 2]['q']; padv = padp[hp % 2]['v']
                    kk_t = bon
                    r_ = zs[:, hp, :]; k_ = zs[:, 4 + hp, :]; v_ = zs[:, 8 + hp, :]
                    rk_, kk_, vk_ = ('zs', hp), ('zs', 4 + hp), ('zs', 8 + hp)
                    hc = slice(hp * 128, (hp + 1) * 128)
                    ps, pk = psbig()
                    MM(ps[:, 0:NT], [(lw_wa[0:64, l, hc], lora[0:64, :])], [lw_wa, lora], [pk])
                    ACT(ew[:], ps[:, 0:NT], AF.Exp, [pk, npc], [ew], bias=npc[:, l, hp:hp + 1], scale=-1.0)
                    ACT(ew[:], ew[:], AF.Ln, [ew], [ew], bias=1.0)
                    ACT(ew[:], ew[:], AF.Exp, [ew], [ew], bias=-0.5, scale=-1.0)
                    yield
                    ps, pk = psbig()
                    MM(ps[:, 0:NT], [(lw_wa[64:128, l, hc], lora[64:128, :])], [lw_wa, lora], [pk])
                    ACT(a_t[:], ps[:, 0:NT], AF.Exp, [pk, npc], [a_t], bias=npc[:, l, 4 + hp:5 + hp], scale=-1.0)
                    ACT(a_t[:], a_t[:], AF.Ln, [a_t], [a_t], bias=1.0)
                    ACT(a_t[:], a_t[:], AF.Exp, [a_t], [a_t], scale=-1.0)
                    ps, pk = psbig()
                    MM(ps[:, 0:NT], [(lw_g[:, l, hc], sgl[:])], [lw_g, sgl], [pk])
                    CP('act', g_t[:], ps[:, 0:NT], [pk], [g_t])
                    yield
                    TS('dve', kk_t[:], k_, pcol(l, 'kk', hp), ALU.mult, [kk_, pc], [kk_t])
                    ACT(sqk[:], kk_t[:], AF.Square, [kk_t], [sqk])
                    ps, pk = psbig()
                    MM(ps[:, 0:NT], [(blk_bf[:], sqk[:])], [blk_bf, sqk], [pk])
                    yield
                    TS('dve', tmpa[:], ps[:, 0:NT], 1e-18, ALU.max, [pk], [tmpa])
                    ACT(tmpa[:], tmpa[:], AF.Ln, [tmpa], [tmpa])
                    ACT(tmpa[:], tmpa[:], AF.Exp, [tmpa], [tmpa], scale=-0.5)
                    TT('dve', kap[:], kk_t[:], tmpa[:], ALU.mult, [kk_t, tmpa], [kap])
                    yield
                    TS('dve', tmpa[:], a_t[:], -1.0, ALU.add, [a_t, pc], [tmpa], s2=pcol(l, 'ka', hp), op1=ALU.mult)
                    STT('dve', kmod[:], tmpa[:], 1.0, k_, ALU.add, ALU.mult, [tmpa, kk_], [kmod])
                    TT('dve', b_t[:], kap[:], a_t[:], ALU.mult, [kap, a_t], [b_t])
                    yield
                    STT('dve', tmpa[:], r_, pcol(l, 'rk', hp), kmod[:], ALU.mult, ALU.mult, [rk_, pc, kmod], [tmpa])
                    ps, pk = psbig()
                    MM(ps[:, 0:NT], [(C['blk'], tmpa[:])], ['cst', tmpa], [pk])
                    TT('dve', bon[:], ps[:, 0:NT], v_, ALU.mult, [pk, vk_], [bon])
                    yield
                    for ch in chunks:
                        SCAN(cl[:, ch.off:ch.off + ch.L], C['ones'][:, 0:ch.L], ew[:, ch.off:ch.off + ch.L], 0.0, ALU.mult, ALU.subtract,
                             ['cst', ew], [cl])
                    yield
                    ACT(E1[:], cl[:], AF.Exp, [cl], [E1])
                    TT('dve', tmpa[:], cl[:], ew[:], ALU.add, [cl, ew], [tmpa])
                    ACT(E2[:], tmpa[:], AF.Exp, [tmpa], [E2])
                    ACT(E3[:], cl[:], AF.Exp, [cl], [E3], scale=-1.0)
                    yield
                    TT('dve', rt[:], r_, E1[:], ALU.mult, [rk_, E1], [rt])
                    CP('act', rtb[:], rt[:], [rt], [rtb])
                    yield
                    po = 0
                    for (off, nseg, L, kind) in groups:
                        for (pd, src, sk, E_, ek) in [(padk, kap[:], kap, E2, E2), (padb, b_t[:], b_t, E3, E3),
                                                       (padq, kmod[:], kmod, E3, E3), (padv, v_, vk_, None, None)]:
                            pv = pd[:, po:po + nseg * 2 * L].rearrange("p (s t) -> p s t", t=2 * L)
                            for h2 in range(2):
                                prt = slice(h2 * 64, (h2 + 1) * 64)
                                sv = src[prt, off:off + nseg * L].rearrange("p (s t) -> p s t", t=L)
                                if E_ is None:
                                    CP('dve', pv[prt, :, h2 * L:(h2 + 1) * L], sv, [sk], [pd])
                                else:
                                    ev = E_[prt, off:off + nseg * L].rearrange("p (s t) -> p s t", t=L)
                                    TT('dve', pv[prt, :, h2 * L:(h2 + 1) * L], sv, ev, ALU.mult, [sk, ek], [pd])
                            yield
                        po += nseg * 2 * L
                def stage(hp, tick):
                    E1 = E1p[hp % 2]; yT = E2p[hp % 2]; rt = rtp[hp % 2]; rtb = rtbp[hp % 2]
                    padk = padp[hp % 2]['k']; padb = padp[hp % 2]['b']; padq = padp[hp % 2]['q']; padv = padp[hp % 2]['v']
                    pending = []
                    IDT = ident_bf if RDT == BF16 else C['ident']
                    IDK = kn(ident_bf) if RDT == BF16 else 'cst'
                    for bt_ in batches:
                        L = bt_[0].L; L2 = 2 * L; nb = len(bt_); po0 = bt_[0].padoff; o0 = bt_[0].off
                        Wd = nb * L2; Wl = nb * L; W8 = nb * 128
                        kp_ = kps[bi[0] % 2]; bi[0] += 1
                        J = {64: 5, 16: 3, 8: 2}[L]

                        def v3(ap, w):
                            return ap.rearrange("p (n t) -> p n t", t=w)
                        mlt = C['mlt%d' % L][0:L2, 0:L2].unsqueeze(1).to_broadcast([L2, nb, L2])
                        mgt = C['mgt%d' % L][0:L2, 0:L2].unsqueeze(1).to_broadcast([L2, nb, L2])
                        mle = C['mle%d' % L][0:L2, 0:L].unsqueeze(1).to_broadcast([L2, nb, L])
                        idb = C['ident'][0:L2, 0:L2].unsqueeze(1).to_broadcast([L2, nb, L2])
                        E1L = v3(E1[:, o0:o0 + Wl], L)[:, :, L - 1:L]
                        TT('dve', v3(wk['bh'][:, 0:Wd], L2), v3(padb[:, po0:po0 + Wd], L2), E1L.to_broadcast([128, nb, L2]), ALU.mult, [padb, E1], [wk['bh']])
                        TT('dve', v3(wk['kh'][:, 0:Wd], L2), v3(padq[:, po0:po0 + Wd], L2), E1L.to_broadcast([128, nb, L2]), ALU.mult, [padq, E1], [wk['kh']])
                        TT('dve', v3(wk['gI'][:, 0:W8], 128), C['ident'].unsqueeze(1).to_broadcast([128, nb, 128]), E1L.to_broadcast([128, nb, 128]),
                           ALU.mult, ['cst', E1], [wk['gI']])

                        def sl2(t, i):
                            return t[:, po0 + i * L2:po0 + (i + 1) * L2]

                        def bl(t, i):
                            return t[0:L2, i * L2:(i + 1) * L2]

                        def b8(t, i):
                            return t[0:L2, i * 128:(i + 1) * 128]
                        RB = range(nb)

                        def fr(ap):
                            return ap.bitcast(F32R) if USE_F32R else ap
                        ps, pk = psbig()
                        MMB([(bl(ps, i), sl2(padb, i), sl2(padk, i)) for i in RB], [padb, padk], [pk])
                        STT('dve', fr(v3(wk['A0T'][0:L2, 0:Wd], L2)), v3(ps[0:L2, 0:Wd], L2), -1.0, mlt, ALU.mult, ALU.mult, [pk, 'cst'], [wk['A0T']])
                        ps, pk = psbig()
                        MMB([(bl(ps, i), sl2(padk, i), sl2(padb, i)) for i in RB], [padb, padk], [pk])
                        STT('dve', fr(v3(wk['A0'][0:L2, 0:Wd], L2)), v3(ps[0:L2, 0:Wd], L2), -1.0, mgt, ALU.mult, ALU.mult, [pk, 'cst'], [wk['A0']])
                        ps, pk = psbig()
                        MMB([(bl(ps, i), sl2(padq, i), sl2(padk, i)) for i in RB], [padq, padk], [pk])
                        TT('dve', v3(wk['PT'][0:L2, 0:Wd], L2), v3(ps[0:L2, 0:Wd], L2), mlt, ALU.mult, [pk, 'cst'], [wk['PT']])
                        ps, pk = psbig()
                        MMB([(ps[0:L2, i * L:(i + 1) * L], sl2(padb, i), rtb[:, o0 + i * L:o0 + (i + 1) * L]) for i in RB], [padb, rtb], [pk])
                        TT('dve', v3(wk['QbT'][0:L2, 0:Wl], L), v3(ps[0:L2, 0:Wl], L), mle, ALU.mult, [pk, 'cst'], [wk['QbT']])
                        ps, pk = psbig()
                        MMB([(ps[0:L2, i * L:(i + 1) * L], sl2(padq, i), rtb[:, o0 + i * L:o0 + (i + 1) * L]) for i in RB], [padq, rtb], [pk])
                        TT('dve', v3(wk['QkT'][0:L2, 0:Wl], L), v3(ps[0:L2, 0:Wl], L), mle, ALU.mult, [pk, 'cst'], [wk['QkT']])
                        tick()
                        for (dst, srct, off_) in [(wk['Vbd'], padv, po0), (wk['Bbd'], wk['bh'], 0), (wk['Kbd'], wk['kh'], 0), (wk['Ktm'], padk, po0)]:
                            ps, pk = psbig()
                            MMB([(b8(ps, i), srct[:, off_ + i * L2:off_ + (i + 1) * L2], IDT[:, :]) for i in RB], [srct, IDK], [pk])
                            CPalt(dst[0:L2, 0:W8], ps[0:L2, 0:W8], [pk], [dst])
                        ps, pk = psbig()
                        MMB([(b8(ps, i), bl(wk['PT'], i), b8(wk['Vbd'], i)) for i in RB], [wk['PT'], wk['Vbd']], [pk])
                        CPalt(wk['PVs'][0:L2, 0:W8], ps[0:L2, 0:W8], [pk], [wk['PVs']])
                        tick()
                        Z = wk['Z']
                        TT('dve', fr(v3(Z[0:L2, 0:Wd], L2)), v3(wk['A0T'][0:L2, 0:Wd], L2), idb, ALU.add, [wk['A0T'], 'cst'], [Z])
                        Ap, ApT = wk['A0'], wk['A0T']
                        An, AnT = wk['A1'], wk['A1T']
                        for jj in range(1, J + 1):
                            ps, pk = psbig()
                            MMB([(bl(ps, i), fr(bl(ApT, i)), fr(bl(Ap, i))) for i in RB], [Ap, ApT], [pk])
                            CP('act', fr(An[0:L2, 0:Wd]), ps[0:L2, 0:Wd], [pk], [An])
                            if jj < J:
                                ps, pk = psbig()
                                MMB([(bl(ps, i), fr(bl(Ap, i)), fr(bl(ApT, i))) for i in RB], [Ap, ApT], [pk])
                                CP('act', fr(AnT[0:L2, 0:Wd]), ps[0:L2, 0:Wd], [pk], [AnT])
                            ps, pk = psbig()
                            MMB([(bl(ps, i), fr(bl(An, i)), fr(bl(Z, i))) for i in RB], [An, Z], [pk])
                            TT('dve', fr(Z[0:L2, 0:Wd]), Z[0:L2, 0:Wd], ps[0:L2, 0:Wd], ALU.add, [Z, pk], [Z])
                            Ap, ApT, An, AnT = An, AnT, Ap, ApT
                            if pending:
                                pending.pop(0)()
                            tick()
                        tick()
                        Zb = wk['Zb']
                        CP('act', Zb[0:L2, 0:Wd], Z[0:L2, 0:Wd], [Z], [Zb])
                        ps, pk = psbig()
                        MMB([(b8(ps, i), bl(Zb, i), b8(wk['Ktm'], i)) for i in RB], [Zb, wk['Ktm']], [pk])
                        CP('act', wk['Kh'][0:L2, 0:W8], ps[0:L2, 0:W8], [pk], [wk['Kh']])
                        ps, pk = psbig()
                        MMB([(b8(ps, i), bl(Zb, i), b8(wk['PVs'], i)) for i in RB], [Zb, wk['PVs']], [pk])
                        P.op('act', lambda e, o_=wk['U0'][0:L2, 0:W8], i_=ps[0:L2, 0:W8]: e.mul(out=o_, in_=i_, mul=-1.0), [pk], [kn(wk['U0'])])
                        tick()
                        while pending:
                            pending.pop(0)()
                        ps, pk = psbig()
                        MMB([(ps[:, i * L:(i + 1) * L], b8(wk['Kh'], i), wk['QbT'][0:L2, i * L:(i + 1) * L]) for i in RB], [wk['Kh'], wk['QbT']], [pk])
                        TT('dve', kp_['Rh'][:, 0:Wl], rt[:, o0:o0 + Wl], ps[:, 0:Wl], ALU.subtract, [rt, pk], [kp_['Rh']])
                        ps, pk = psbig()
                        MMG([(ps[:, i * L:(i + 1) * L], [(b8(wk['U0'], i), wk['QbT'][0:L2, i * L:(i + 1) * L]),
                                                        (b8(wk['Vbd'], i), wk['QkT'][0:L2, i * L:(i + 1) * L])]) for i in RB],
                            [wk['U0'], wk['QbT'], wk['Vbd'], wk['QkT']], [pk])
                        CP('act', kp_['Y0'][:, 0:Wl], ps[:, 0:Wl], [pk], [kp_['Y0']])
                        ps, pk = psbig()
                        MMG([(ps[:, i * 128:(i + 1) * 128], [(b8(wk['Bbd'], i), b8(wk['U0'], i)), (b8(wk['Kbd'], i), b8(wk['Vbd'], i))]) for i in RB],
                            [wk['Bbd'], wk['U0'], wk['Kbd'], wk['Vbd']], [pk])
                        CP('act', kp_['D0'][:, 0:W8], ps[:, 0:W8], [pk], [kp_['D0']])
                        ps, pk = psbig()
                        MMB([(ps[:, i * 128:(i + 1) * 128], b8(wk['Kh'], i), b8(wk['Bbd'], i)) for i in RB], [wk['Kh'], wk['Bbd']], [pk])
                        TT('dve', kp_['Mx'][:, 0:W8], wk['gI'][:, 0:W8], ps[:, 0:W8], ALU.subtract, [wk['gI'], pk], [kp_['Mx']])
                        while pending:
                            pending.pop(0)()

                        def step2(i, ch, kp_=kp_, L=L):
                            o = ch.off
                            if ch.kind == 'p':
                                S_ap = Sbd[l][hp][:]; S_key = kn(Sbd[l][hp])
                            else:
                                S_ap = Sbs[:, ch.seq * 4 + hp, :]; S_key = ('Sbs', ch.seq * 4 + hp)
                            psY, pkY = psbig()
                            MM(psY[:, 0:L], [(S_ap, kp_['Rh'][:, i * L:(i + 1) * L])], [S_key, kp_['Rh']], [pkY])
                            psS, pkS = psbig()
                            MM(psS[:, 0:128], [(kp_['Mx'][:, i * 128:(i + 1) * 128], S_ap)], [S_key, kp_['Mx']], [pkS])
                            TT('dve', S_ap, psS[:, 0:128], kp_['D0'][:, i * 128:(i + 1) * 128], ALU.add, [pkS, kp_['D0']], [S_key])
                            TT('dve', yT[:, o:o + L], psY[:, 0:L], kp_['Y0'][:, i * L:(i + 1) * L], ALU.add, [pkY, kp_['Y0']], [yT])
                        for i, ch in enumerate(bt_):
                            pending.append(lambda i=i, ch=ch, f=step2: f(i, ch))
                    while pending:
                        pending.pop(0)()
                def post(hp):
                    yT = E2p[hp % 2]; bon = bonp[hp % 2]; g_t = g_tp[hp % 2]
                    ps, pk = psbig()
                    MM(ps[:, 0:NT], [(C['blk'], yT[:])], ['cst', yT], [pk])
                    STT('dve', yT[:], ps[:, 0:NT], -1.0 / 64, yT[:], ALU.mult, ALU.add, [pk, yT], [yT])
                    ACT(tmpa[:], yT[:], AF.Square, [yT], [tmpa])
                    ps, pk = psbig()
                    MM(ps[:, 0:NT], [(C['blk'], tmpa[:])], ['cst', tmpa], [pk])
                    ACT(tmpa[:], ps[:, 0:NT], AF.Ln, [pk], [tmpa], bias=RW_GN_EPS, scale=1.0 / 64)
                    ACT(tmpa[:], tmpa[:], AF.Exp, [tmpa], [tmpa], scale=-0.5)
                    TT('dve', yT[:], yT[:], tmpa[:], ALU.mult, [yT, tmpa], [yT])
                    STT('dve', yT[:], yT[:], pcol(l, 'gng', hp), bon[:], ALU.mult, ALU.add, [yT, pc, bon], [yT])
                    STT('dve', yaT[:, hp, 0:NT], yT[:], pcol(l, 'gnb', hp), g_t[:], ALU.add, ALU.mult, [yT, pc, g_t], [('yaT', hp)])
                def advance(g, n=1):
                    if g is None:
                        return
                    for _ in range(n):
                        try:
                            next(g)
                        except StopIteration:
                            return
                for _ in pre_gen(0):
                    pass
                for hp in range(4):
                    nxt = pre_gen(hp + 1) if hp < 3 else None
                    stage(hp, lambda: advance(nxt, 1))
                    if nxt is not None:
                        for _ in nxt:
                            pass
                    post(hp)
                if blk_i == 0:
                    dst = O['oS'][l].rearrange("s (hp h2) v k -> h2 v s hp k", h2=2)
                    for half in range(2):
                        for s8 in range(8):
                            s_ = half * 8 + s8; u0 = s_ * 4
                            ps, pk = psbig()
                            TRB([(ps[:, i * 128:(i + 1) * 128], Sbs[:, u0 + i, :]) for i in range(4)], [('Sbs', u0 + i) for i in range(4)], [pk])
                            CPalt(Sst[:, s8 * 4:s8 * 4 + 4, :], ps[:, 0:512].rearrange("p (u k) -> p u k", k=128), [pk], [('Sst', s8)])
                        for h2 in range(2):
                            P.dma('sp', dst[h2][:, half * 8:(half + 1) * 8], Sst4[h2 * 64:(h2 + 1) * 64, :, :, h2 * 64:(h2 + 1) * 64],
                                  reads=[('Sst', s8) for s8 in range(8)])
                if debug and l == 0 and blk_i == 0:
                    dump('yaT', yaT[:, :, 0:NT], [128, 4, NT], [('yaT', h) for h in range(4)])
                P.barrier()
                P.flush()

        def mlstm_phase(l, NT, groups, chunks, blk_i):
            with ExitStack() as sc:
                nck = len(chunks)
                npd = sum(nseg * (L + 3) for (off, nseg, L, kind) in groups)
                qkp = T('qkp', [128, 8, npd], F32, sc)
                cacc = T('cacc', [128, NT], F32, sc)
                qb = T('qb', [128, 4, NT], BF16, sc); kb = T('kb', [128, 4, NT], BF16, sc); kf = T('kf', [128, 4, NT], F32, sc)
                sgo = T('sgo', [128, 4, NT], BF16, sc)
                vtm = T('vtm', [128, nck, 4, 132], BF16, sc)
                gsb = T('gsb', [8, NT], F32, sc)
                h4 = T('h4', [128, 4, NT], F32, sc)
                igb = T('igb', [128, NT], F32, sc); sp_ = T('sp_', [128, NT], F32, sc)
                cc4 = T('cc4', [128, 4, NT], F32, sc); G4 = T('G4', [128, 4, NT], F32, sc); em4 = T('em4', [128, 4, NT], F32, sc)
                sc4 = T('sc4', [128, 4, NT], F32, sc); tmpb = T('tmpb', [128, NT], F32, sc)
                if blk_i == 0:
                    cvin = T('cvin', [128, 8, NS * 3], F32, sc); cvout = T('cvout', [128, 8, NS * 3], F32, sc)
                    to_fm(lambda j: cvin[:, j, :], I['scv'][l], NS * 3, D, [cvin])
                    Cst = T('Cst', [128, 32, 128], F32, sc)
                    CTs = T('CTs', [128, NS * 4, 132], F32, sc)
                    m_s = T('m_s', [128, NS * 4], F32, sc); mo_s = T('mo_s', [128, NS * 4], F32, sc)
                    nst = T('nst', [128, NS * 4], F32, sc)
                    CTbC = Cst[:].bitcast(BF16).rearrange("p a (b c) -> p (a b) c", b=2)
                    CTbn = T('CTbn', [128, NS * 4], BF16, sc)
                    for g4 in range(4):
                        g2 = g4 % 2
                        P.dma('sp', Cst[:, g2 * 16:(g2 + 1) * 16, :].rearrange("p (s h) k -> p s h k", h=4),
                              I['sC'][l, g4 * 4:(g4 + 1) * 4].rearrange("s h v k -> v s h k"), writes=[('Cst', g2)])
                        for q4 in range(4):
                            u0 = g4 * 16 + q4 * 4
                            ps, pk = psbig()
                            TRB([(ps[:, i * 128:(i + 1) * 128], Cst[:, g2 * 16 + q4 * 4 + i, :]) for i in range(4)], [('Cst', g2)], [pk])
                            CPalt(CTs[:, u0:u0 + 4, 0:128], ps[:, 0:512].rearrange("p (u k) -> p u k", k=128), [pk], [('CTs', u0 // 4)])
                    to_fm(lambda j: nst[:, :], I['sn'][l].rearrange("s h k -> (s h) k"), NS * 4, 128, [nst])
                    CP('dve', CTs[:, :, 128], nst[:], [nst] + [('CTs', u) for u in range(NS)], [('CTs', u) for u in range(NS)])
                    P.dma('sp', m_s[:], I['sm'][l:l + 1, :].partition_broadcast(128), writes=[m_s])
                    CP('act', CTbC, CTs[:, :, 0:128], [('CTs', u) for u in range(NS)], [('Cst', 0), ('Cst', 1)])
                    CP('act', CTbn[:], CTs[:, :, 128], [('CTs', u) for u in range(NS)], [CTbn])
                MSET('dve', vtm[:, :, :, 128:129], 1.0, [vtm])
                slC = load_slab(I['w_in'][l], 0, NCH, SLABS_IN[2][0], SLABS_IN[2][1])
                slD = load_slab(I['w_in'][l], 0, NCH, SLABS_IN[3][0], SLABS_IN[3][1])
                for j in range(8):
                    ps, pk = dense_tile(slC, j * 128, 128, NT)
                    po = 0
                    for (off, nseg, L, kind) in groups:
                        qv = qkp[:, j, po:po + nseg * (L + 3)].rearrange("p (s t) -> p s t", t=L + 3)
                        pv = ps[:, off:off + nseg * L].rearrange("p (s t) -> p s t", t=L)
                        CP('act', qv[:, :, 3:L + 3], pv, [pk], [('qkp', j)])
                        if kind == 'p':
                            CP('dve', qv[:, 0, 0:3], cvc[:, l, j, :], [cvc], [('qkp', j)])
                            CP('dve', cvc[:, l, j, :], qv[:, 0, L:L + 3], [('qkp', j)], [cvc])
                        else:
                            CP('dve', qv[:, :, 0:3], cvin[:, j, :].rearrange("p (s t) -> p s t", t=3), [cvin], [('qkp', j)])
                            CP('dve', cvout[:, j, :].rearrange("p (s t) -> p s t", t=3), qv[:, :, L:L + 3], [('qkp', j)], [cvout])
                        av = cacc[:, off:off + nseg * L].rearrange("p (s t) -> p s t", t=L)
                        TS('dve', av, qv[:, :, 0:L], pcol(l, 'cw', 0 * 8 + j), ALU.mult, [('qkp', j), pc], [cacc],
                           s2=pcol(l, 'cb', j), op1=ALU.add)
                        for tp in range(1, 4):
                            STT('dve', av, qv[:, :, tp:tp + L], pcol(l, 'cw', tp * 8 + j), av, ALU.mult, ALU.add, [('qkp', j), pc, cacc], [cacc])
                        po += nseg * (L + 3)
                    if j < 4:
                        ACT(qb[:, j, :], cacc[:], AF.Silu, [cacc], [('qb', j)])
                    else:
                        ACT(kf[:, j - 4, :], cacc[:], AF.Silu, [cacc], [('kf', j - 4)])
                        TS('dve', kf[:, j - 4, :], kf[:, j - 4, :], 128.0 ** -0.5, ALU.mult, [('kf', j - 4)], [('kf', j - 4)])
                        CP('dve', kb[:, j - 4, :], kf[:, j - 4, :], [('kf', j - 4)], [('kb', j - 4)])
                if blk_i == 0:
                    from_fm(O['ocv'][l], lambda j: cvout[:, j, :], NS * 3, D, [cvout])
                for ci, ch in enumerate(chunks):
                    ps, pk = psbig()
                    MM(ps[0:ch.L, 0:512], [(uT[:, c, ch.off:ch.off + ch.L], slD[:, c, 0:512]) for c in range(NCH)], [slD] + UT_ALL, [pk])
                    CPalt(vtm[0:ch.L, ci, :, 0:128], ps[0:ch.L, 0:512].rearrange("p (h v) -> p h v", v=128), [pk], [vtm])
                for j in range(4):
                    ps, pk = dense_tile(slD, 512 + j * 128, 128, NT)
                    ACT(sgo[:, j, :], ps[:, 0:NT], AF.Sigmoid, [pk], [('sgo', j)])
                ps, pk = psbig()
                MM(ps[0:8, 0:NT], [(slD[:, c, 1024:1032], uT[:, c, 0:NT]) for c in range(NCH)], [slD] + UT_ALL, [pk])
                CP('act', gsb[:], ps[0:8, 0:NT], [pk], [gsb])
                if debug and l == 0 and blk_i == 0:
                    dump('qb', qb[:], [128, 4, NT], [('qb', j) for j in range(4)])
                ctm = [{n: T('m%s%d' % (n, i), [128, w_], dt_, sc) for (n, w_, dt_) in
                        [('t3', 512, F32), ('AT', 512, F32), ('qt', 256, F32),
                         ('aqk', 512, BF16), ('qtb', 512, BF16), ('cco', 64, F32), ('wco', 64, F32), ('wtmp', 64, F32)]} for i in range(2)]
                for d_ in ctm:
                    d_['den'] = d_['t3']
                for h in range(4):
                    ps, pk = psbig()
                    MM(ps[:, 0:NT], [(C['ident'][0:8, h:h + 1].to_broadcast([8, 128]), gsb[:])], ['cst', gsb], [pk])
                    TS('dve', igb[:], ps[:, 0:NT], gb[:, l * 8 + h:l * 8 + h + 1], ALU.add, [pk, gb], [igb])
                    ps, pk = psbig()
                    MM(ps[:, 0:NT], [(C['ident'][0:8, 4 + h:5 + h].to_broadcast([8, 128]), gsb[:])], ['cst', gsb], [pk])
                    ACT(sp_[:], ps[:, 0:NT], AF.Exp, [pk, ngb], [sp_], bias=ngb[:, l * 8 + 4 + h:l * 8 + 5 + h], scale=-1.0)
                    ACT(sp_[:], sp_[:], AF.Ln, [sp_], [sp_], bias=1.0)
                    for ch in chunks:
                        SCAN(em4[:, h, ch.off:ch.off + ch.L], C['ones'][:, 0:ch.L], sp_[:, ch.off:ch.off + ch.L], 0.0, ALU.mult, ALU.subtract,
                             ['cst', sp_], [('em4', h)])
                    TT('dve', cc4[:, h, :], igb[:], em4[:, h, :], ALU.subtract, [igb, ('em4', h)], [('cc4', h)])
                for ci, ch in enumerate(chunks):
                    for h in range(4):
                        L = ch.L; o = ch.off
                        if ch.kind == 'p':
                            m_in = mP[:, l, h:h + 1]; m_key = ('mP', h); m_out = m_in; mo_key = ('mP', h)
                        else:
                            u = ch.seq * 4 + h
                            m_in = m_s[:, u:u + 1]; m_key = 'm_s'; m_out = mo_s[:, u:u + 1]; mo_key = ('mo_s', u)
                        SCAN(G4[:, h, o:o + L], cc4[:, h, o:o + L], cc4[:, h, o:o + L], m_in, ALU.max, ALU.max, [('cc4', h), m_key], [('G4', h)])
                        ACT(sc4[:, h, o:o + L], G4[:, h, o:o + L], AF.Exp, [('G4', h), m_key], [('sc4', h)], bias=m_in, scale=-1.0)
                        TT('dve', m_out, em4[:, h, o + L - 1:o + L], G4[:, h, o + L - 1:o + L], ALU.add, [('em4', h), ('G4', h)], [mo_key])
                for h in range(4):
                    TT('dve', tmpb[:], em4[:, h, :], G4[:, h, :], ALU.add, [('em4', h), ('G4', h)], [tmpb])
                    ACT(em4[:, h, :], tmpb[:], AF.Exp, [tmpb], [('em4', h)], scale=-1.0)
                H4 = lambda n: [(n, h) for h in range(4)]

                mgroups = []
                for ci, ch in enumerate(chunks):
                    if mgroups and ch.kind == 's' and chunks[mgroups[-1][0]].kind == 's' and mgroups[-1][1] < 16:
                        mgroups[-1][1] += 1
                    else:
                        mgroups.append([ci, 1])

                def g_ctx(gi):
                    ci0, ns = mgroups[gi]
                    ch0 = chunks[ci0]
                    return ci0, ns, ch0.L, ch0.off, ctm[gi % 2]

                def CT_of(ch):
                    if ch.kind == 'p':
                        return CTp[l][:, :, :], kn(CTp[l])
                    return CTs[:, ch.seq * 4:(ch.seq + 1) * 4, :], ('CTs', ch.seq)

                def ml_A(gi):
                    ci0, ns, L, o0, tm = g_ctx(gi)
                    W = ns * L; W4 = 4 * W

                    def src4(t):
                        return t.rearrange("p h (n t) -> p h n t", t=L)

                    def f4(ap):
                        return ap.rearrange("p (h n t) -> p h n t", h=4, n=ns)
                    mnegb = C['mneg%d' % L][0:L, 0:L].unsqueeze(1).unsqueeze(1).to_broadcast([L, 4, ns, L])
                    idb = C['ident'][0:L, 0:L].unsqueeze(1).unsqueeze(1).to_broadcast([L, 4, ns, L])
                    t3 = f4(tm['t3'][0:L, 0:W4]); cco = tm['cco']; wco = tm['wco']
                    cc3 = cco[0:L, 0:4 * ns].rearrange("p (h n) -> p h n", n=ns)
                    TT('dve', t3, src4(cc4[0:L, :, o0:o0 + W]), idb, ALU.mult, H4('cc4') + ['cst'], [tm['t3']])
                    P.op('dve', lambda e, o_=cc3, i_=t3: e.tensor_reduce(out=o_, in_=i_, axis=mybir.AxisListType.X, op=ALU.add),
                         [kn(tm['t3'])], [kn(cco)])
                    TT('dve', t3, mnegb, src4(G4[0:L, :, o0:o0 + W]), ALU.subtract, H4('G4') + ['cst'], [tm['t3']])
                    TT('dve', t3, t3, cc3.unsqueeze(3).to_broadcast([L, 4, ns, L]), ALU.add, [tm['t3'], cco], [tm['t3']])
                    ACT(tm['AT'][0:L, 0:W4], tm['t3'][0:L, 0:W4], AF.Exp, [tm['t3']], [tm['AT']])
                    ps, pk = psbig()
                    MMB([(ps[0:L, (h * ns + n) * L:(h * ns + n + 1) * L], kb[:, h, o0 + n * L:o0 + (n + 1) * L], qb[:, h, o0 + n * L:o0 + (n + 1) * L])
                         for h in range(4) for n in range(ns)], H4('kb') + H4('qb'), [pk])
                    TT('dve', tm['aqk'][0:L, 0:W4], tm['AT'][0:L, 0:W4], ps[0:L, 0:W4], ALU.mult, [tm['AT'], pk], [tm['aqk']])
                    qtt = tm['qtb'] if chunks[ci0].kind == 's' else tm['qt']
                    TT('dve', f4(qtt[:, 0:W4]), src4(qb[:, :, o0:o0 + W]), src4(sc4[:, :, o0:o0 + W]), ALU.mult, H4('qb') + H4('sc4'), [qtt])
                    TT('dve', tm['wtmp'][0:L, 0:4 * ns].rearrange("p (h n) -> p h n", n=ns), cc3, src4(G4[0:L, :, o0:o0 + W])[:, :, :, L - 1],
                       ALU.subtract, [cco] + H4('G4'), [tm['wtmp']])
                    ACT(wco[0:L, 0:4 * ns], tm['wtmp'][0:L, 0:4 * ns], AF.Exp, [tm['wtmp']], [wco])

                def ml_B(gi):
                    ci0, ns, L, o0, tm = g_ctx(gi)
                    W = ns * L; W4 = 4 * W
                    grpN = []; grpD = []; ckeys = []
                    smp = chunks[ci0].kind == 's'
                    qtt = tm['qtb'] if smp else tm['qt']
                    for h in range(4):
                        for n in range(ns):
                            ch_ = chunks[ci0 + n]
                            if smp:
                                u_ = ch_.seq * 4 + h
                                Cm = CTbC[:, u_, :]; ncol_ = CTbn[:, u_:u_ + 1]
                                for k_ in (('Cst', 0), ('Cst', 1), kn(CTbn)):
                                    if k_ not in ckeys:
                                        ckeys.append(k_)
                            else:
                                CTv, CT_key = CT_of(ch_)
                                Cm = CTv[:, h, 0:128]; ncol_ = CTv[:, h, 128:129]
                                if CT_key not in ckeys:
                                    ckeys.append(CT_key)
                            c0 = (h * ns + n) * L
                            grpN.append((None, c0, [(vtm[0:L, ci0 + n, h, 0:128], tm['aqk'][0:L, c0:c0 + L]), (Cm, qtt[:, c0:c0 + L])]))
                            grpD.append((None, c0, [(ones_bf[0:L, :], tm['aqk'][0:L, c0:c0 + L]),
                                                    (ncol_.to_broadcast([128, 128]), qtt[:, c0:c0 + L])]))
                    psN, pkN = psbig()
                    MMG([(psN[:, c0:c0 + L], prs) for (_, c0, prs) in grpN], [vtm, tm['aqk'], qtt] + ckeys, [pkN])
                    psD, pkD = psbig()
                    MMG([(psD[:, c0:c0 + L], prs) for (_, c0, prs) in grpD], [ones_bf, tm['aqk'], qtt] + ckeys, [pkD])
                    ACT(tm['den'][:, 0:W4], psD[:, 0:W4], AF.Abs, [pkD], [tm['den']])
                    den4 = tm['den'][:, 0:W4].rearrange("p (h n t) -> p h n t", h=4, n=ns)
                    TT('dve', den4, den4, em4[:, :, o0:o0 + W].rearrange("p h (n t) -> p h n t", t=L), ALU.max, [tm['den']] + H4('em4'), [tm['den']])
                    ACT(tm['den'][:, 0:W4], tm['den'][:, 0:W4], AF.Ln, [tm['den']], [tm['den']])
                    ACT(tm['den'][:, 0:W4], tm['den'][:, 0:W4], AF.Exp, [tm['den']], [tm['den']], scale=-1.0)
                    TT('dve', h4[:, :, o0:o0 + W].rearrange("p h (n t) -> p h n t", t=L), psN[:, 0:W4].rearrange("p (h n t) -> p h n t", h=4, n=ns),
                       den4, ALU.mult, [pkN, tm['den']], H4('h4'))

                kwt = [T('kwt%d' % i, [128, 512], BF16, sc) for i in range(2)]

                def ml_K(gi, n):
                    ci0, ns, L, o0, tm = g_ctx(gi)
                    ci = ci0 + n; o = chunks[ci].off
                    kw = kwt[ci % 2]
                    wc3 = tm['wco'][0:L, 0:4 * ns].rearrange("p (h n) -> p h n", n=ns)[:, :, n:n + 1]
                    ps, pk = psbig()
                    TRB([(ps[0:L, h * 128:(h + 1) * 128], kf[:, h, o:o + L]) for h in range(4)], H4('kf'), [pk])
                    TT('dve', kw[0:L, 0:512].rearrange("p (h c) -> p h c", c=128), ps[0:L, 0:512].rearrange("p (h c) -> p h c", c=128),
                       wc3.to_broadcast([L, 4, 128]), ALU.mult, [pk, tm['wco']], [kw])

                def ml_C(gi, n):
                    ci0, ns, L, o0, tm = g_ctx(gi)
                    ci = ci0 + n; ch = chunks[ci]; o = ch.off
                    CTv, CT_key = CT_of(ch)
                    kw = kwt[ci % 2]
                    psC, pkC = psbig()
                    MMB([(psC[:, h * 128:(h + 1) * 128], kw[0:L, h * 128:(h + 1) * 128], vtm[0:L, ci, h, 0:128]) for h in range(4)], [kw, vtm], [pkC])
                    psn, pkn = psbig()
                    MMB([(psn[:, h:h + 1], kw[0:L, h * 128:(h + 1) * 128], vtm[0:L, ci, h, 128:129]) for h in range(4)], [kw, vtm], [pkn])
                    TT('dve', CTv[:, :, 0:129], CTv[:, :, 0:129], sc4[:, :, o + L - 1:o + L].to_broadcast([128, 4, 129]), ALU.mult,
                       [CT_key] + H4('sc4'), [CT_key])
                    TT('dve', CTv[:, :, 0:128], CTv[:, :, 0:128], psC[:, 0:512].rearrange("p (h c) -> p h c", c=128), ALU.add, [CT_key, pkC], [CT_key])
                    TT('dve', CTv[:, :, 128], CTv[:, :, 128], psn[:, 0:4], ALU.add, [CT_key, pkn], [CT_key])

                ml_A(0)
                ml_K(0, 0)
                for gi in range(len(mgroups)):
                    if gi + 1 < len(mgroups):
                        ml_A(gi + 1)
                    ml_B(gi)
                    ns_ = mgroups[gi][1]
                    for n in range(ns_):
                        if n + 1 < ns_:
                            ml_K(gi, n + 1)
                        elif gi + 1 < len(mgroups):
                            ml_K(gi + 1, 0)
                        ml_C(gi, n)
                tq = [tmpb, igb, sp_, cacc]
                pks = []
                for h in range(4):
                    ps, pk = psbig()
                    MM(ps[:, 0:NT], [(C['ones'], h4[:, h, :])], ['cst', ('h4', h)], [pk])
                    pks.append((ps, pk))
                for h in range(4):
                    ps, pk = pks[h]
                    STT('dve', h4[:, h, :], ps[:, 0:NT], -1.0 / 128, h4[:, h, :], ALU.mult, ALU.add, [pk, ('h4', h)], [('h4', h)])
                for h in range(4):
                    ACT(tq[h][:], h4[:, h, :], AF.Square, [('h4', h)], [tq[h]])
                pks = []
                for h in range(4):
                    ps, pk = psbig()
                    MM(ps[:, 0:NT], [(C['ones'], tq[h][:])], ['cst', tq[h]], [pk])
                    pks.append((ps, pk))
                for h in range(4):
                    ps, pk = pks[h]
                    ACT(tq[h][:], ps[:, 0:NT], AF.Ln, [pk], [tq[h]], bias=ML_GN_EPS, scale=1.0 / 128)
                for h in range(4):
                    ACT(tq[h][:], tq[h][:], AF.Exp, [tq[h]], [tq[h]], scale=-0.5)
                for h in range(4):
                    STT('dve', h4[:, h, :], h4[:, h, :], pcol(l, 'mlg', h), tq[h][:], ALU.mult, ALU.mult, [('h4', h), pc, tq[h]], [('h4', h)])
                for h in range(4):
                    TT('dve', ybT[:, h, 0:NT], h4[:, h, :], sgo[:, h, :], ALU.mult, [('h4', h), ('sgo', h)], [('ybT', h)])
                if blk_i == 0:
                    for g4 in range(4):
                        g2 = g4 % 2
                        for q4 in range(4):
                            u0 = g4 * 16 + q4 * 4
                            ps, pk = psbig()
                            TRB([(ps[:, i * 128:(i + 1) * 128], CTs[:, u0 + i, 0:128]) for i in range(4)], [('CTs', u0 // 4)], [pk])
                            CPalt(Cst[:, g2 * 16 + q4 * 4:g2 * 16 + q4 * 4 + 4, :], ps[:, 0:512].rearrange("p (u k) -> p u k", k=128), [pk], [('Cst', g2)])
                        P.dma('sp', O['oC'][l, g4 * 4:(g4 + 1) * 4].rearrange("s h v k -> v s h k"),
                              Cst[:, g2 * 16:(g2 + 1) * 16, :].rearrange("p (s h) k -> p s h k", h=4), reads=[('Cst', g2)])
                    CP('dve', nst[:], CTs[:, :, 128], [('CTs', u) for u in range(NS)], [nst])
                    from_fm(O['on'][l], lambda j: nst[:, :], NS * 4, 128, [nst])
                    P.dma('sp', O['om'][l:l + 1, :], mo_s[0:1, :], reads=[('mo_s', u) for u in range(NS * 4)])
                if debug and l == 0 and blk_i == 0:
                    dump('ybT', ybT[:, :, 0:NT], [128, 4, NT], [('ybT', h) for h in range(4)])
                P.barrier()
                P.flush()

        def tail_phase(l, NT):
            with ExitStack() as sc:
                gab = T('gab', [128, 16, NT], F32, sc)
                mrg = T('mrg', [128, NCH, NT], BF16, sc)
                acc = T('acc', [128, NCH, NT], F32, sc)
                tmpm = [T('tmpm%d' % i, [128, NT], F32, sc) for i in range(2)]
                aT = T('aT', [128, NCH, NT], BF16, sc)
                for si in (4, 5):
                    sl = load_slab(I['w_in'][l], 0, NCH, SLABS_IN[si][0], SLABS_IN[si][1])
                    for j in range(8):
                        ps, pk = dense_tile(sl, j * 128, 128, NT)
                        ACT(gab[:, (si - 4) * 8 + j, :], ps[:, 0:NT], AF.Sigmoid, [pk], [('gab', (si - 4) * 8 + j)])
                i = st['slab']; st['slab'] = (i + 1) % 2
                sl = slabs[i]
                P.dma('pool', sl[:, 0:4, 0:D], I['p_a'][l].rearrange("(c p) n -> p c n", p=128), writes=[sl], slab=True)
                P.dma('pool', sl[:, 4:8, 0:D], I['p_b'][l].rearrange("(c p) n -> p c n", p=128), writes=[sl], slab=True)
                YA = [('yaT', h) for h in range(4)]; YB = [('ybT', h) for h in range(4)]
                for j in range(8):
                    psa, pka = psbig()
                    MM(psa[:, 0:NT], [(sl[:, c, j * 128:(j + 1) * 128], yaT[:, c, 0:NT]) for c in range(4)], [sl] + YA, [pka])
                    psb_, pkb = psbig()
                    MM(psb_[:, 0:NT], [(sl[:, 4 + c, j * 128:(j + 1) * 128], ybT[:, c, 0:NT]) for c in range(4)], [sl] + YB, [pkb])
                    t_ = tmpm[j % 2]
                    TT('dve', t_[:], gab[:, j, :], psa[:, 0:NT], ALU.mult, [('gab', j), pka], [t_])
                    TT('dve', gab[:, 8 + j, :], gab[:, 8 + j, :], psb_[:, 0:NT], ALU.mult, [('gab', 8 + j), pkb], [('gab', 8 + j)])
                    TT('dve', mrg[:, j, :], t_[:], gab[:, 8 + j, :], ALU.add, [t_, ('gab', 8 + j)], [('mrg', j)])
                MR = [('mrg', j) for j in range(8)]
                sl = load_slab(I['w_out'][l], 0, NCH, 0, D)
                for j in range(8):
                    ps, pk = dense_tile(sl, j * 128, 128, NT, rhs=mrg, rkeys=MR)
                    CPalt(acc[:, j, :], ps[:, 0:NT], [pk], [acc])
                rmsnorm_add(acc, NT, l, 'post1', sc)
                rmsnorm_u(xT, NT, l, 'pre2', sc)
                AT_ = [('aT', j) for j in range(8)]
                for q in range(4):
                    slu = load_slab(I['w_ff_up'][l], 0, NCH, q * D, (q + 1) * D)
                    sld = load_slab(I['w_ff_down'][l], q * D, NCH, 0, D)
                    for j in range(8):
                        ps, pk = dense_tile(slu, j * 128, 128, NT)
                        t_ = tmpm[j % 2]
                        ACT(t_[:], ps[:, 0:NT], AF.Relu, [pk], [t_])
                        TT('dve', aT[:, j, :], t_[:], t_[:], ALU.mult, [t_], [('aT', j)])
                    for j in range(8):
                        ps, pk = dense_tile(sld, j * 128, 128, NT, rhs=aT, rkeys=AT_)
                        if q == 0:
                            CPalt(acc[:, j, :], ps[:, 0:NT], [pk], [acc])
                        else:
                            TT('dve', acc[:, j, :], acc[:, j, :], ps[:, 0:NT], ALU.add, [acc, pk], [acc])
                rmsnorm_add(acc, NT, l, 'post2', sc)
                P.barrier()
                P.flush()

        def mkchunks(groups):
            chunks = []
            po = 0
            for (off, nseg, L, kind) in groups:
                for s_ in range(nseg):
                    ch = Chunk()
                    ch.off = off + s_ * L; ch.L = L; ch.kind = kind; ch.seq = s_; ch.padoff = po + s_ * 2 * L
                    chunks.append(ch)
                po += nseg * 2 * L
            return chunks

        try:
          stage('setup')
          for blk_i in range(5):
              if blk_i == 0:
                  NT = NMETA + NS * TSL
                  groups = [(0, 1, NMETA, 'p'), (NMETA, NS, TSL, 's')]
                  x_to_fm(I['meta'], NMETA, 0)
                  x_to_fm(I['xs'], NS * TSL, NMETA)
              else:
                  NT = 512
                  groups = [(0, 8, 64, 'p')]
                  for tt_ in range(4):
                      r0 = (blk_i - 1) * 512 + tt_ * 128
                      x_to_fm(I['xp'][r0:r0 + 128, :], 128, tt_ * 128)
              chunks = mkchunks(groups)
              cgroups = [(0, 1, NT, 'p')] if blk_i > 0 else groups
              for l in range(DEPTH):
                  with ExitStack() as sc0:
                      rmsnorm_u(xT, NT, l, 'pre1', sc0)
                      if debug and l == 0 and blk_i == 0:
                          dump('uT', uT[:, :, 0:NT], [128, NCH, NT], UT_ALL)
                      P.barrier()
                      P.flush()
                  stage('norm1')
                  rwkv_phase(l, NT, cgroups, groups, chunks, blk_i)
                  stage('rwkv')
                  mlstm_phase(l, NT, cgroups, chunks, blk_i)
                  stage('mlstm')
                  tail_phase(l, NT)
                  stage('tail')
              if blk_i == 0:
                  x_from_fm(O['ys'], NS * TSL, NMETA)
              else:
                  for tt_ in range(4):
                      r0 = (blk_i - 1) * 512 + tt_ * 128
                      x_from_fm(O['yp'][r0:r0 + 128, :], 128, tt_ * 128)
              P.barrier()
              P.flush()

          for l in range(DEPTH):
              for hp in range(4):
                  ps, pk = pssmall()
                  TR(ps[:, 0:128], Sbd[l][hp][:], 128, [Sbd[l][hp]], [pk])
                  sg = stg[st['stg']]; st['stg'] ^= 1
                  CPalt(sg[:, 0:128], ps[:, 0:128], [pk], [sg])
                  for h2 in range(2):
                      P.dma('sp', O['pS'][l, hp * 2 + h2], sg[h2 * 64:(h2 + 1) * 64, h2 * 64:(h2 + 1) * 64], reads=[sg])
              from_fm(O['psh'][l], lambda j, l=l: shc[:, l, :], 14, 128, [shc])
              for h in range(4):
                  ps, pk = pssmall()
                  TR(ps[:, 0:128], CTp[l][:, h, 0:128], 128, [CTp[l]], [pk])
                  sg = stg[st['stg']]; st['stg'] ^= 1
                  CPalt(sg[:, 0:128], ps[:, 0:128], [pk], [sg])
                  P.dma('sp', O['pC'][l, h], sg[:, 0:128], reads=[sg])
                  P.dma('sp', O['pn'][l, h:h + 1, :].rearrange("a k -> k a"), CTp[l][:, h, 128:129], reads=[CTp[l]])
              P.dma('sp', O['pm'][l:l + 1, :], mP[0:1, l, :], reads=[('mP', h) for h in range(4)])
              sg = stg[st['stg']]; st['stg'] ^= 1
              for c in range(8):
                  ps, pk = pssmall()
                  TR(ps[0:3, 0:128], cvc[:, l, c, :], 128, [cvc], [pk])
                  CPalt(sg[0:3, c * 128:(c + 1) * 128], ps[0:3, 0:128], [pk], [sg])
              P.dma('sp', O['pcv'][l], sg[0:3, :], reads=[sg])
          stage('end')
        except StopBuild:
          pass
        P.barrier(engines=('pe', 'act', 'dve', 'pool', 'sp'), with_w=True)
        P.flush()
        print("instructions:", P.nins, "waits:", P.nwaits, {k: P.cnt[k] for k in ('pe', 'act', 'dve', 'pool')})
    return nc, dbg


def _consts():
    c = np.zeros((len(CONST_NAMES), 128, 128), np.float32)
    ix = {n: i for i, n in enumerate(CONST_NAMES)}
    c[ix['ident']] = np.eye(128, dtype=np.float32)
    c[ix['ones']] = 1.0
    c[ix['blk'], 0:64, 0:64] = 1.0
    c[ix['blk'], 64:128, 64:128] = 1.0
    for L in (64, 16, 8):
        s = np.arange(L)[:, None]
        t = np.arange(L)[None, :]
        lt = (s < t).astype(np.float32)
        le = (s <= t).astype(np.float32)
        for h in range(2):
            c[ix['mlt%d' % L], h * L:(h + 1) * L, h * L:(h + 1) * L] = lt
            c[ix['mgt%d' % L], h * L:(h + 1) * L, h * L:(h + 1) * L] = lt.T
            c[ix['mle%d' % L], h * L:(h + 1) * L, 0:L] = le
        c[ix['mneg%d' % L], 0:L, 0:L] = np.where(s <= t, 0.0, -30000.0)
    return np.ascontiguousarray(c.transpose(1, 0, 2).reshape(128, -1))


_CACHE = {}


def _prep_inputs(inp, n_cores=8):
    f = lambda a: np.ascontiguousarray(np.asarray(a, dtype=np.float32))
    prm = np.zeros((DEPTH, NPRM, 128), np.float32)
    for l in range(DEPTH):
        rows = [inp['pre1'][l], inp['post1'][l], inp['pre2'][l], inp['post2'][l], inp['rw_mu'][l], inp['rw_w0'][l], inp['rw_a0'][l],
                inp['rw_k_k'][l], inp['rw_k_a'][l], inp['rw_r_k'][l], inp['rw_gn_g'][l], inp['rw_gn_b'][l], inp['ml_conv_w'][l],
                inp['ml_conv_b'][l], inp['ml_gn_g'][l]]
        prm[l] = np.concatenate([np.asarray(r, np.float32).reshape(-1, 128) for r in rows], axis=0)
    gbias = np.concatenate([np.concatenate([np.asarray(inp['ml_i_bias'][l], np.float32), np.asarray(inp['ml_f_bias'][l], np.float32)])
                            for l in range(DEPTH)]).reshape(1, 16)
    shared = {'meta': f(inp['meta_tokens']), 'w_in': f(inp['w_in']), 'w_up': f(inp['rw_w_up']), 'a_up': f(inp['rw_a_up']),
              'g_up': f(inp['rw_g_up']), 'p_a': f(inp['p_a']), 'p_b': f(inp['p_b']), 'w_out': f(inp['w_out']),
              'w_ff_up': f(inp['w_ff_up']), 'w_ff_down': f(inp['w_ff_down']), 'prm': prm, 'gbias': gbias, 'consts': _consts()}
    maps = []
    xp = np.asarray(inp['x_prompt'], np.float32)
    xs = np.asarray(inp['x_sample'], np.float32)
    for i in range(n_cores):
        sl = slice(i * NS, (i + 1) * NS)
        m = dict(shared)
        m['xp'] = f(xp[i])
        m['xs'] = f(xs[sl].reshape(NS * TSL, D))
        m['sS'] = f(np.asarray(inp['state_rwkv_S'])[:, sl])
        m['ssh'] = f(np.asarray(inp['state_rwkv_shift'])[:, sl])
        m['sC'] = f(np.asarray(inp['state_mlstm_C'])[:, sl])
        m['sn'] = f(np.asarray(inp['state_mlstm_n'])[:, sl])
        m['sm'] = f(np.asarray(inp['state_mlstm_m'])[:, sl].reshape(DEPTH, NS * 4))
        m['scv'] = f(np.asarray(inp['state_mlstm_conv'])[:, sl].reshape(DEPTH, NS * 3, D))
        maps.append(m)
    return maps


def _gather(results, n_cores=8):
    R = results
    cat = lambda k, ax: np.concatenate([R[i][k] for i in range(n_cores)], axis=ax)
    y_prompt = np.stack([R[i]['yp'] for i in range(n_cores)])
    y_sample = np.concatenate([R[i]['ys'].reshape(NS, TSL, D) for i in range(n_cores)], axis=0)
    pS = np.stack([R[i]['pS'] for i in range(n_cores)], axis=1)
    psh = np.stack([R[i]['psh'].reshape(DEPTH, 1792) for i in range(n_cores)], axis=1)
    pC = np.stack([R[i]['pC'] for i in range(n_cores)], axis=1)
    pn = np.stack([R[i]['pn'] for i in range(n_cores)], axis=1)
    pm = np.stack([R[i]['pm'] for i in range(n_cores)], axis=1)
    pcv = np.stack([R[i]['pcv'] for i in range(n_cores)], axis=1)
    oS = cat('oS', 1)
    osh = cat('osh', 1)
    oC = cat('oC', 1)
    on = np.concatenate([R[i]['on'].reshape(DEPTH, NS, 4, 128) for i in range(n_cores)], axis=1)
    om = np.concatenate([R[i]['om'].reshape(DEPTH, NS, 4) for i in range(n_cores)], axis=1)
    ocv = np.concatenate([R[i]['ocv'].reshape(DEPTH, NS, 3, D) for i in range(n_cores)], axis=1)
    outs = (y_prompt, y_sample, pS, psh, pC, pn, pm, pcv, oS, osh, oC, on, om, ocv)
    return tuple(np.ascontiguousarray(o.astype(np.float32)) for o in outs)


def kernel(**inputs):
    if 'nc' not in _CACHE:
        _CACHE['nc'] = build(False)[0]
    nc = _CACHE['nc']
    maps = _prep_inputs(inputs)
    res = run_bass_kernel_spmd(nc, maps, core_ids=list(range(8)))
    return _gather(res.results)
```

```python
import numpy as np
from contextlib import ExitStack
import concourse.bass as bass
import concourse.mybir as mybir
from concourse.bass_utils import run_bass_kernel_spmd

F32 = mybir.dt.float32
BF16 = mybir.dt.bfloat16
AF = mybir.ActivationFunctionType
ALU = mybir.AluOpType

NDS = 40
NWS = 6
NGS = 8
RDT = BF16
F32R = mybir.dt.float32r
USE_F32R = True

D = 1024
NCH = 8
NIN = 5896
SEQ = 2048
NMETA = 16
DEPTH = 2
NS = 16
TSL = 8
SLABW = 1032
SLABS_IN = [(0, 1024), (1024, 1792), (1792, 2816), (2816, 3848), (3848, 4872), (4872, 5896)]
RMS_EPS = 1e-6
RW_GN_EPS = 64e-5
ML_GN_EPS = 1e-5
NPRM = 118
PR = {}
_o = 0
for _n, _c in [('pre1', 8), ('post1', 8), ('pre2', 8), ('post2', 8), ('mu', 14), ('w0', 4), ('a0', 4), ('kk', 4),
               ('ka', 4), ('rk', 4), ('gng', 4), ('gnb', 4), ('cw', 32), ('cb', 8), ('mlg', 4)]:
    PR[_n] = _o
    _o += _c
assert _o == NPRM
CONST_NAMES = ['ident', 'ones', 'blk', 'mlt64', 'mgt64', 'mle64', 'mneg64', 'mlt16', 'mgt16', 'mle16', 'mneg16',
               'mlt8', 'mgt8', 'mle8', 'mneg8']


class Prog:
    ENGS = ['pe', 'act', 'dve', 'pool', 'sp']

    def __init__(self, nc, block, es):
        self.nc = nc
        self.bfn = {'pe': block.tensor, 'act': block.scalar, 'dve': block.vector,
                    'pool': block.gpsimd, 'sp': block.sync}
        self.semh = {}
        for e in ['pe', 'act', 'dve', 'pool']:
            self.semh[e] = es.enter_context(nc.semaphore("sem_" + e))
        for i in range(NDS):
            self.semh['d%d' % i] = es.enter_context(nc.semaphore("dsem%d" % i))
        for i in range(NWS):
            self.semh['w%d' % i] = es.enter_context(nc.semaphore("wsem%d" % i))
        for i in range(NGS):
            self.semh['g%d' % i] = es.enter_context(nc.semaphore("gsem%d" % i))
        self.gnext = 0
        self.cnt = {k: 0 for k in self.semh}
        self.dnext = 0
        self.wnext = 0
        self.buf = {e: [] for e in self.ENGS}
        self.lastw = {}
        self.readers = {}
        self.seen = {e: {} for e in self.ENGS}
        self.nins = 0
        self.clock = {}
        self.tokseq = {}
        self.gseq = 0
        self.nwaits = 0

    KEYMAP = {}

    @staticmethod
    def _k(x):
        if isinstance(x, (str, tuple)):
            return x
        return Prog.KEYMAP.get(x.name, x.name)

    def _deps(self, eng, reads, writes):
        need = {}
        for k in reads:
            w = self.lastw.get(k)
            if w is not None and need.get(w[0], 0) < w[1]:
                need[w[0]] = w[1]
        for k in writes:
            w = self.lastw.get(k)
            if w is not None and need.get(w[0], 0) < w[1]:
                need[w[0]] = w[1]
            for sid, v in self.readers.get(k, {}).items():
                if need.get(sid, 0) < v:
                    need[sid] = v
        out = []
        seen = self.seen[eng]
        items = sorted(need.items(), key=lambda kv: -self.tokseq.get(kv, 0))
        for sid, v in items:
            if seen.get(sid, 0) >= v:
                continue
            seen[sid] = v
            out.append((sid, v))
            ck = self.clock.get((sid, v))
            if ck:
                for s2, v2 in ck.items():
                    if seen.get(s2, 0) < v2:
                        seen[s2] = v2
        out.reverse()
        return out

    def _commit(self, tok, reads, writes, eng=None):
        self.gseq += 1
        self.tokseq[tok] = self.gseq
        if eng is not None:
            self.clock[tok] = dict(self.seen[eng])
        for k in reads:
            d = self.readers.setdefault(k, {})
            if d.get(tok[0], 0) < tok[1]:
                d[tok[0]] = tok[1]
        for k in writes:
            self.lastw[k] = tok
            self.readers[k] = {}

    def op(self, eng, fn, reads=(), writes=(), attach=None, multi=False):
        reads = [self._k(x) for x in reads]
        writes = [self._k(x) for x in writes]
        waits = self._deps(eng, reads, writes)
        self.nwaits += len(waits)
        self.cnt[eng] += 1
        tok = (eng, self.cnt[eng])
        self._commit(tok, reads, writes, eng)
        semh = self.semh
        sem = semh[eng]
        if attach is None:
            attach = eng in ('act', 'dve', 'pe')

        def run(e):
            if attach and waits:
                for sid, v in waits[:-1]:
                    e.wait_ge(semh[sid], v)
                if multi:
                    done = [False]

                    def hook(ins):
                        if not done[0]:
                            ins._wait_ge(semh[waits[-1][0]], waits[-1][1])
                            done[0] = True
                        return ins
                    fn(e, hook).then_inc(sem, 1)
                else:
                    ins = fn(e)
                    ins._wait_ge(semh[waits[-1][0]], waits[-1][1])
                    ins.then_inc(sem, 1)
            elif multi:
                for sid, v in waits:
                    e.wait_ge(semh[sid], v)
                fn(e, lambda ins: ins).then_inc(sem, 1)
            else:
                for sid, v in waits:
                    e.wait_ge(semh[sid], v)
                fn(e).then_inc(sem, 1)
        self.buf[eng].append(run)
        self.nins += 1

    def dma(self, q, out, in_, reads=(), writes=(), slab=False, **kw):
        reads = [self._k(x) for x in reads]
        writes = [self._k(x) for x in writes]
        if slab:
            sid = 'w%d' % self.wnext
            self.wnext = (self.wnext + 1) % NWS
        elif q == 'pool':
            sid = 'g%d' % self.gnext
            self.gnext = (self.gnext + 1) % NGS
        else:
            sid = 'd%d' % self.dnext
            self.dnext = (self.dnext + 1) % NDS
        waits = self._deps(q, reads, writes)
        prev = self.cnt[sid]
        if prev > 0 and self.seen[q].get(sid, 0) < prev:
            self.seen[q][sid] = prev
            waits.append((sid, prev))
        self.cnt[sid] = prev + 16
        tok = (sid, prev + 16)
        self._commit(tok, reads, writes, q)
        semh = self.semh

        def run(e):
            for s, v in waits:
                e.wait_ge(semh[s], v)
            e.dma_start(out=out, in_=in_, **kw).then_inc(semh[sid], 16)
        self.buf[q].append(run)
        self.nins += 1

    def barrier(self, engines=('act', 'dve', 'sp'), with_w=False):
        for e in engines:
            waits = []
            for sid, v in self.cnt.items():
                if v == 0 or (sid.startswith('w') and not with_w):
                    continue
                if self.seen[e].get(sid, 0) >= v:
                    continue
                self.seen[e][sid] = v
                waits.append((sid, v))
            if waits:
                semh = self.semh

                def run(en, waits=waits):
                    for s, v in waits:
                        en.wait_ge(semh[s], v)
                self.buf[e].append(run)

    def flush(self):
        for e in self.ENGS:
            fns = self.buf[e]
            if not fns:
                continue
            self.buf[e] = []

            def body(en, fns=fns):
                for f in fns:
                    f(en)
            self.bfn[e](body)


class Chunk:
    pass


class StopBuild(Exception):
    pass


def build(debug=False, stop=None):
    nc = bass.Bass("TRN2", target_bir_lowering=False)
    stage_i = [0]
    I = {}

    def din(name, shape):
        I[name] = nc.dram_tensor(name, list(shape), F32, kind="ExternalInput").ap()
        return I[name]

    O = {}

    def dout(name, shape):
        O[name] = nc.dram_tensor(name, list(shape), F32, kind="ExternalOutput").ap()
        return O[name]

    din('xp', (SEQ, D)); din('xs', (NS * TSL, D))
    din('sS', (DEPTH, NS, 8, 64, 64)); din('ssh', (DEPTH, NS, 1792)); din('sC', (DEPTH, NS, 4, 128, 128))
    din('sn', (DEPTH, NS, 4, 128)); din('sm', (DEPTH, NS * 4)); din('scv', (DEPTH, NS * 3, D))
    din('meta', (NMETA, D)); din('w_in', (DEPTH, D, NIN)); din('w_up', (DEPTH, 64, 512)); din('a_up', (DEPTH, 64, 512))
    din('g_up', (DEPTH, 128, 512)); din('p_a', (DEPTH, 512, D)); din('p_b', (DEPTH, 512, D)); din('w_out', (DEPTH, D, D))
    din('w_ff_up', (DEPTH, D, 4 * D)); din('w_ff_down', (DEPTH, 4 * D, D))
    din('prm', (DEPTH, NPRM, 128)); din('gbias', (1, 16)); din('consts', (128, len(CONST_NAMES) * 128))

    dout('yp', (SEQ, D)); dout('ys', (NS * TSL, D))
    dout('pS', (DEPTH, 8, 64, 64)); dout('psh', (DEPTH, 14, 128)); dout('pC', (DEPTH, 4, 128, 128)); dout('pn', (DEPTH, 4, 128))
    dout('pm', (DEPTH, 4)); dout('pcv', (DEPTH, 3, D))
    dout('oS', (DEPTH, NS, 8, 64, 64)); dout('osh', (DEPTH, NS, 1792)); dout('oC', (DEPTH, NS, 4, 128, 128))
    dout('on', (DEPTH, NS * 4, 128)); dout('om', (DEPTH, NS * 4)); dout('ocv', (DEPTH, NS * 3, D))
    dbg = {}

    with ExitStack() as es:
        used = {}

        def T(name, shape, dt=F32, st=None):
            n = used.get(name, 0)
            used[name] = n + 1
            un = name if n == 0 else "%s_r%d" % (name, n)
            Prog.KEYMAP[un] = name
            return (st if st is not None else es).enter_context(nc.sbuf_tensor(un, list(shape), dt))

        cst = T('cst', [128, len(CONST_NAMES), 128])
        C = {n: cst[:, i, :] for i, n in enumerate(CONST_NAMES)}
        ones_bf = T('ones_bf', [128, 128], BF16)
        blk_bf = T('blk_bf', [128, 128], BF16)
        ident_bf = T('ident_bf', [128, 128], BF16)
        pc = T('pc', [128, DEPTH, 128])
        npc = T('npc', [128, DEPTH, 8])
        gb = T('gb', [128, 16])
        ngb = T('ngb', [128, 16])
        lw_wa = T('lw_wa', [128, DEPTH, 512], BF16)
        lw_g = T('lw_g', [128, DEPTH, 512], BF16)
        slabs = [T('slab%d' % i, [128, NCH, SLABW], BF16) for i in range(2)]
        xT = T('xT', [128, NCH, 512])
        uT = T('uT', [128, NCH, 512], BF16)
        yaT = T('yaT', [128, 4, 512], BF16)
        ybT = T('ybT', [128, 4, 512], BF16)
        Sbd = [[T('Sbd%d_%d' % (l, hp), [128, 128]) for hp in range(4)] for l in range(DEPTH)]
        CTp = [T('CT%d' % l, [128, 4, 132]) for l in range(DEPTH)]
        mP = T('mP', [128, DEPTH, 4])
        shc = T('shc', [128, DEPTH, 14])
        cvc = T('cvc', [128, DEPTH, 8, 3])
        stg = [T('stg%d' % i, [128, D]) for i in range(2)]
        pss = [es.enter_context(nc.psum_tensor('psb%d' % i, [128, 512], F32)) for i in range(8)]

        block = es.enter_context(nc.Block())
        P = Prog(nc, block, es)

        def stage(name):
            stage_i[0] += 1
            if stop is not None and stage_i[0] >= stop:
                print("STOP at stage", stage_i[0], name)
                P.barrier(engines=('pe', 'act', 'dve', 'pool', 'sp'), with_w=True)
                P.flush()
                raise StopBuild()

        st = {'big': 0, 'small': 0, 'slab': 0, 'stg': 0, 'alt': 0}

        def psbig():
            i = st['big']; st['big'] = (i + 1) % 8
            return pss[i], ('psB', i)

        def pssmall():
            return psbig()

        def kn(x):
            return Prog._k(x)

        def TT(eng, out, in0, in1, op, r, w):
            P.op(eng, lambda e: e.tensor_tensor(out=out, in0=in0, in1=in1, op=op), r, w)

        def TS(eng, out, in0, s1, op0, r, w, s2=None, op1=None):
            if op1 is None:
                P.op(eng, lambda e: e.tensor_scalar(out=out, in0=in0, scalar1=s1, scalar2=None, op0=op0), r, w)
            else:
                P.op(eng, lambda e: e.tensor_scalar(out=out, in0=in0, scalar1=s1, scalar2=s2, op0=op0, op1=op1), r, w)

        def STT(eng, out, in0, scalar, in1, op0, op1, r, w, accum_out=None):
            if accum_out is None:
                P.op(eng, lambda e: e.scalar_tensor_tensor(out=out, in0=in0, scalar=scalar, in1=in1, op0=op0, op1=op1), r, w)
            else:
                P.op(eng, lambda e: e.scalar_tensor_tensor(out=out, in0=in0, scalar=scalar, in1=in1, op0=op0, op1=op1,
                                                           accum_out=accum_out), r, w)

        def ACT(out, in_, func, r, w, bias=0.0, scale=1.0):
            P.op('act', lambda e: e.activation(out=out, in_=in_, func=func, bias=bias, scale=scale), r, w)

        def CP(eng, out, in_, r, w):
            if eng == 'act':
                P.op('act', lambda e: e.copy(out=out, in_=in_), r, w)
            else:
                P.op(eng, lambda e: e.tensor_copy(out=out, in_=in_), r, w)

        def CPalt(out, in_, r, w):
            st['alt'] = (st['alt'] + 1) % 4
            CP('act', out, in_, r, w)

        def MSET(eng, ap, val, w):
            P.op(eng, lambda e: e.memset(ap, val), (), w)

        def SCAN(out, d0, d1, init, op0, op1, r, w):
            P.op('dve', lambda e: e.tensor_tensor_scan(out=out, data0=d0, data1=d1, initial=init, op0=op0, op1=op1), r, w)

        def RECIP(out, in_, r, w):
            P.op('dve', lambda e: e.reciprocal(out=out, in_=in_), r, w)

        def MM(out, pairs, r, w):
            def fn(e, hook):
                n = len(pairs)
                ins = None
                for i, (l_, r_) in enumerate(pairs):
                    ins = hook(e.matmul(out, lhsT=l_, rhs=r_, start=(i == 0), stop=(i == n - 1)))
                return ins
            P.op('pe', fn, r, w, multi=True)

        def MMB(trip, r, w):
            def fn(e, hook):
                ins = None
                for (o_, l_, r_) in trip:
                    ins = hook(e.matmul(o_, lhsT=l_, rhs=r_, start=True, stop=True))
                return ins
            P.op('pe', fn, r, w, multi=True)

        def MMG(groups_, r, w):
            def fn(e, hook):
                ins = None
                for (o_, pairs) in groups_:
                    n = len(pairs)
                    for i, (l_, r_) in enumerate(pairs):
                        ins = hook(e.matmul(o_, lhsT=l_, rhs=r_, start=(i == 0), stop=(i == n - 1)))
                return ins
            P.op('pe', fn, r, w, multi=True)

        def TRB(pairs, r, w):
            def fn(e, hook):
                ins = None
                for (o_, i_) in pairs:
                    ins = hook(e.transpose(out=o_, in_=i_, identity=C['ident'][:, :]))
                return ins
            P.op('pe', fn, list(r) + ['cst'], w, multi=True)

        def TR(out, in_, n_in_part, r, w):
            P.op('pe', lambda e: e.transpose(out=out, in_=in_, identity=C['ident'][0:n_in_part, 0:n_in_part]), r + ['cst'], w)

        def dump(name, ap, shape, r):
            if not debug:
                return
            dbg[name] = dout('dbg_' + name, shape)
            P.dma('pool', dbg[name], ap, reads=r)

        def load_slab(src2d, r0, nrow_chunks, c0, c1):
            i = st['slab']; st['slab'] = (i + 1) % 2
            sl = slabs[i]
            src = src2d[r0:r0 + nrow_chunks * 128, c0:c1].rearrange("(c p) n -> p c n", p=128)
            P.dma('pool', sl[:, 0:nrow_chunks, 0:c1 - c0], src, writes=[sl], slab=True)
            return sl

        def to_fm(dst_fn, rows_ap, R, F, dst_keys):
            i = st['stg']; st['stg'] ^= 1
            sg = stg[i]
            P.dma('sp', sg[0:R, 0:F], rows_ap, writes=[sg])
            for j in range(F // 128):
                ps, pk = pssmall()
                TR(ps[:, 0:R], sg[0:R, j * 128:(j + 1) * 128], R, [sg], [pk])
                CPalt(dst_fn(j), ps[:, 0:R], [pk], dst_keys)

        def from_fm(rows_ap, src_fn, R, F, src_keys):
            i = st['stg']; st['stg'] ^= 1
            sg = stg[i]
            for j in range(F // 128):
                ps, pk = pssmall()
                TR(ps[0:R, 0:128], src_fn(j), 128, src_keys, [pk])
                CPalt(sg[0:R, j * 128:(j + 1) * 128], ps[0:R, 0:128], [pk], [sg])
            P.dma('sp', rows_ap, sg[0:R, 0:F], reads=[sg])

        def x_to_fm(rows_ap, R, t0):
            i = st['stg']; st['stg'] ^= 1
            sg = stg[i]
            P.dma('sp', sg[0:R, 0:D], rows_ap, writes=[sg])
            for j0 in (0, 4):
                ps, pk = psbig()
                P.op('pe', lambda e, hook, ps=ps, j0=j0: [hook(e.transpose(out=ps[:, k * R:(k + 1) * R], in_=sg[0:R, (j0 + k) * 128:(j0 + k + 1) * 128],
                                                                      identity=C['ident'][0:R, 0:R])) for k in range(4)][-1], [sg, 'cst'], [pk], multi=True)
                CPalt(xT[:, j0:j0 + 4, t0:t0 + R], ps[:, 0:4 * R].rearrange("p (k r) -> p k r", r=R), [pk], [xT])

        def x_from_fm(rows_ap, R, t0):
            i = st['stg']; st['stg'] ^= 1
            sg = stg[i]
            for j0 in (0, 4):
                ps, pk = psbig()
                P.op('pe', lambda e, hook, ps=ps, j0=j0: [hook(e.transpose(out=ps[0:R, k * 128:(k + 1) * 128], in_=xT[:, j0 + k, t0:t0 + R],
                                                                      identity=C['ident'][:, :])) for k in range(4)][-1], [xT, 'cst'], [pk], multi=True)
                CPalt(sg[0:R, j0 * 128:(j0 + 4) * 128], ps[0:R, 0:512], [pk], [sg])
            P.dma('sp', rows_ap, sg[0:R, 0:D], reads=[sg])

        P.dma('sp', cst[:].rearrange("p a b -> p (a b)"), I['consts'], writes=[cst])
        CP('dve', ones_bf[:], C['ones'], [cst], [ones_bf])
        CP('dve', blk_bf[:], C['blk'], [cst], [blk_bf])
        CP('dve', ident_bf[:], C['ident'], [cst], [ident_bf])
        for l in range(DEPTH):
            to_fm(lambda j, l=l: pc[:, l, 0:NPRM], I['prm'][l], NPRM, 128, [pc])
        P.dma('sp', gb[:], I['gbias'].partition_broadcast(128), writes=[gb])
        TS('dve', ngb[:], gb[:], -1.0, ALU.mult, [gb], [ngb])
        for l in range(DEPTH):
            TS('dve', npc[:, l, 0:8], pc[:, l, PR['w0']:PR['w0'] + 8], -1.0, ALU.mult, [pc], [npc])
            P.dma('pool', lw_wa[0:64, l, :], I['w_up'][l], writes=[lw_wa])
            P.dma('pool', lw_wa[64:128, l, :], I['a_up'][l], writes=[lw_wa])
            P.dma('pool', lw_g[:, l, :], I['g_up'][l], writes=[lw_g])
            for hp in range(4):
                MSET('dve', Sbd[l][hp][:], 0.0, [Sbd[l][hp]])
        MSET('dve', mP[:], 0.0, [('mP', h) for h in range(4)])
        for l in range(DEPTH):
            MSET('dve', CTp[l][:], 0.0, [CTp[l]])
        MSET('dve', shc[:], 0.0, [shc])
        MSET('dve', cvc[:], 0.0, [cvc])

        def pcol(l, name, j):
            return pc[:, l, PR[name] + j:PR[name] + j + 1]

        def rmsnorm_u(src, NT, l, gname, sc):
            sq = T('nsq', [128, NCH, NT], BF16, sc)
            rstd = T('nrstd', [128, NT], F32, sc)
            P.op('act', lambda e: e.activation(out=sq[:], in_=src[:, :, 0:NT], func=AF.Square), [src], [sq])
            ps, pk = psbig()
            MM(ps[:, 0:NT], [(ones_bf[:], sq[:, c, :]) for c in range(NCH)], [ones_bf, sq], [pk])
            ACT(rstd[:], ps[:, 0:NT], AF.Ln, [pk], [rstd], bias=RMS_EPS, scale=1.0 / D)
            ACT(rstd[:], rstd[:], AF.Exp, [rstd], [rstd], scale=-0.5)
            for c in range(NCH):
                STT('dve', uT[:, c, 0:NT], src[:, c, 0:NT], pcol(l, gname, c), rstd[:], ALU.mult, ALU.mult,
                    [src, rstd, pc], [('uT', c)])

        def rmsnorm_add(acc, NT, l, gname, sc):
            sq = T('asq', [128, NCH, NT], BF16, sc)
            rstd = T('arstd', [128, NT], F32, sc)
            P.op('act', lambda e: e.activation(out=sq[:], in_=acc[:, :, 0:NT], func=AF.Square), [acc], [sq])
            ps, pk = psbig()
            MM(ps[:, 0:NT], [(ones_bf[:], sq[:, c, :]) for c in range(NCH)], [ones_bf, sq], [pk])
            ACT(rstd[:], ps[:, 0:NT], AF.Ln, [pk], [rstd], bias=RMS_EPS, scale=1.0 / D)
            ACT(rstd[:], rstd[:], AF.Exp, [rstd], [rstd], scale=-0.5)
            for c in range(NCH):
                STT('dve', acc[:, c, 0:NT], acc[:, c, 0:NT], pcol(l, gname, c), rstd[:], ALU.mult, ALU.mult,
                    [acc, rstd, pc], [acc])
            P.op('dve', lambda e: e.tensor_tensor(out=xT[:, :, 0:NT], in0=xT[:, :, 0:NT], in1=acc[:, :, 0:NT], op=ALU.add),
                 [acc, xT], [xT])

        UT_ALL = [('uT', c) for c in range(NCH)]

        def dense_tile(sl, col0, ncols, NT, nch=NCH, rhs=None, rkeys=None):
            ps, pk = psbig()
            if rhs is None:
                rhs = uT; rkeys = UT_ALL
            MM(ps[0:ncols, 0:NT], [(sl[:, c, col0:col0 + ncols], rhs[:, c, 0:NT]) for c in range(nch)], [sl] + list(rkeys), [pk])
            return ps, pk

        def rwkv_phase(l, NT, sgroups, groups, chunks, blk_i):
            with ExitStack() as sc:
                zs = T('zs', [128, 12, NT], F32, sc)
                lora = T('lora', [128, NT], BF16, sc)
                sgl = T('sgl', [128, NT], BF16, sc)
                zraw = [T('zraw0', [128, NT + 32], F32, sc)] * 2
                dtmp = [T('dtmp0', [128, NT], F32, sc)] * 2
                if blk_i == 0:
                    shin = T('shin', [128, 14, NS], F32, sc)
                    shout = T('shout', [128, 14, NS], F32, sc)
                    to_fm(lambda j: shin[:, j, :], I['ssh'][l][:, 0:1024], NS, 1024, [shin])
                    to_fm(lambda j: shin[:, 8 + j, :], I['ssh'][l][:, 1024:1792], NS, 768, [shin])
                    Sst = T('Sst', [128, 32, 128], F32, sc)
                    Sbs = T('Sbs', [128, NS * 4, 128], F32, sc)
                    MSET('dve', Sst[:], 0.0, [('Sst', i) for i in range(8)])
                    src = I['sS'][l].rearrange("s (hp h2) v k -> h2 v s hp k", h2=2)
                    Sst4 = Sst[:].rearrange("p (s hp) k -> p s hp k", hp=4)
                    for half in range(2):
                        for h2 in range(2):
                            P.dma('sp', Sst4[h2 * 64:(h2 + 1) * 64, :, :, h2 * 64:(h2 + 1) * 64], src[h2][:, half * 8:(half + 1) * 8],
                                  writes=[('Sst', s8) for s8 in range(8)])
                        for s8 in range(8):
                            u0 = (half * 8 + s8) * 4
                            ps, pk = psbig()
                            TRB([(ps[:, i * 128:(i + 1) * 128], Sst[:, s8 * 4 + i, :]) for i in range(4)], [('Sst', s8)], [pk])
                            CPalt(Sbs[:, u0:u0 + 4, :], ps[:, 0:512].rearrange("p (u k) -> p u k", k=128), [pk], [('Sbs', u0 + i) for i in range(4)])
                slA = load_slab(I['w_in'][l], 0, NCH, SLABS_IN[0][0], SLABS_IN[0][1])
                slB = load_slab(I['w_in'][l], 0, NCH, SLABS_IN[1][0], SLABS_IN[1][1])
                for j in range(14):
                    sl = slA if j < 8 else slB
                    ps, pk = dense_tile(sl, (j % 8 if j < 8 else j - 8) * 128, 128, NT)
                    zr = zraw[j % 2]; dt_ = dtmp[j % 2]
                    off_r = 0
                    for (off, nseg, L, kind) in sgroups:
                        zv = zr[:, off_r:off_r + nseg * (L + 1)].rearrange("p (s t) -> p s t", t=L + 1)
                        pv = ps[:, off:off + nseg * L].rearrange("p (s t) -> p s t", t=L)
                        CP('act', zv[:, :, 1:L + 1], pv, [pk], [zr])
                        if kind == 'p':
                            CP('dve', zv[:, 0, 0:1], shc[:, l, j:j + 1], [shc], [zr])
                        else:
                            CP('dve', zv[:, :, 0], shin[:, j, :], [shin], [zr])
                        dv = dt_[:, off:off + nseg * L].rearrange("p (s t) -> p s t", t=L)
                        TT('dve', dv, zv[:, :, 0:L], pv, ALU.subtract, [zr, pk], [dt_])
                        if kind == 'p':
                            CP('dve', shc[:, l, j:j + 1], zv[:, 0, L:L + 1], [zr], [shc])
                        else:
                            CP('dve', shout[:, j, :], zv[:, :, L], [zr], [shout])
                        off_r += nseg * (L + 1)
                    if j < 12:
                        STT('dve', zs[:, j, :], dt_[:, 0:NT], pcol(l, 'mu', j), ps[:, 0:NT], ALU.mult, ALU.add,
                            [dt_, pk, pc], [('zs', j)])
                    elif j == 12:
                        tmpz = dtmp[j % 2]
                        STT('dve', tmpz[:, 0:NT], dt_[:, 0:NT], pcol(l, 'mu', j), ps[:, 0:NT], ALU.mult, ALU.add,
                            [dt_, pk, pc], [dt_])
                        ACT(lora[0:64, :], tmpz[0:64, 0:NT], AF.Tanh, [dt_], [lora])
                        CP('act', lora[64:128, :], tmpz[64:128, 0:NT], [dt_], [lora])
                    else:
                        tmpz = dtmp[j % 2]
                        STT('dve', tmpz[:, 0:NT], dt_[:, 0:NT], pcol(l, 'mu', j), ps[:, 0:NT], ALU.mult, ALU.add,
                            [dt_, pk, pc], [dt_])
                        ACT(sgl[:], tmpz[:, 0:NT], AF.Sigmoid, [dt_], [sgl])
                if blk_i == 0:
                    from_fm(O['osh'][l][:, 0:1024], lambda j: shout[:, j, :], NS, 1024, [shout])
                    from_fm(O['osh'][l][:, 1024:1792], lambda j: shout[:, 8 + j, :], NS, 768, [shout])
                if debug and l == 0 and blk_i == 0:
                    dump('zs', zs[:].rearrange("p a b -> p (a b)"), [128, 12 * NT], [('zs', j) for j in range(12)])

                a_t = T('a_t', [128, NT], F32, sc); kap = T('kap', [128, NT], F32, sc)
                kmod = T('kmod', [128, NT], F32, sc); b_t = T('b_t', [128, NT], F32, sc); ew = T('ew', [128, NT], F32, sc)
                cl = T('cl', [128, NT], F32, sc); E3 = a_t
                sqk = T('sqk', [128, NT], BF16, sc)
                tmpa = dtmp[0]
                npad = sum(nseg * 2 * L for (off, nseg, L, kind) in groups)
                E1p = [T('E1_%d' % i, [128, NT], F32, sc) for i in range(2)]
                E2p = [T('E2_%d' % i, [128, NT], F32, sc) for i in range(2)]
                rtp = [T('rt_%d' % i, [128, NT], F32, sc) for i in range(2)]
                bonp = [T('bon_%d' % i, [128, NT], F32, sc) for i in range(2)]
                g_tp = [T('g_t_%d' % i, [128, NT], F32, sc) for i in range(2)]
                rtbp = [T('rtb_%d' % i, [128, NT], RDT, sc) for i in range(2)]
                padp = [{n: T('pad%s_%d' % (n, i), [128, npad], RDT, sc) for n in 'kbqv'} for i in range(2)]
                for i in range(2):
                    for n in 'kbqv':
                        MSET('dve', padp[i][n][:], 0.0, [padp[i][n]])
                RW = [128, 512]
                wk = {n: T('w' + n, RW, RDT, sc) for n in ['bh', 'kh', 'PT', 'Vbd', 'Bbd', 'Kbd', 'Kh', 'U0', 'Ktm', 'PVs', 'Zb']}
                for n in ['A0', 'A0T', 'A1', 'A1T', 'Z']:
                    wk[n] = T('w' + n, RW, F32, sc)
                wk['QbT'] = T('wQbT', [128, 256], RDT, sc)
                wk['QkT'] = T('wQkT', [128, 256], RDT, sc)
                wk['gI'] = T('wgI', RW, F32, sc)
                kps = [{'Mx': T('kMx0', RW, F32, sc), 'D0': T('kD00', RW, F32, sc),
                        'Rh': T('kRh0', [128, 256], F32, sc), 'Y0': T('kY00', [128, 256], F32, sc)}] * 2
                batches = []
                cur = []
                for ch in chunks:
                    if cur and (cur[0].L != ch.L or len(cur) >= 4):
                        batches.append(cur); cur = []
                    cur.append(ch)
                if cur:
                    batches.append(cur)
                ui = [0]
                bi = [0]
                def pre_gen(hp):
                    E1 = E1p[hp % 2]; E2 = E2p[hp % 2]; rt = rtp[hp % 2]; bon = bonp[hp % 2]; g_t = g_tp[hp % 2]; rtb = rtbp[hp % 2]
                    padk = padp[hp % 2]['k']; padb = padp[hp % 2]['b']; padq = padp[hp % 2]['q']; padv = padp[hp % 2]['v']
                    kk_t = bon
                    r_ = zs[:, hp, :]; k_ = zs[:, 4 + hp, :]; v_ = zs[:, 8 + hp, :]
                    rk_, kk_, vk_ = ('zs', hp), ('zs', 4 + hp), ('zs', 8 + hp)
                    hc = slice(hp * 128, (hp + 1) * 128)
                    ps, pk = psbig()
                    MM(ps[:, 0:NT], [(lw_wa[0:64, l, hc], lora[0:64, :])], [lw_wa, lora], [pk])
                    ACT(ew[:], ps[:, 0:NT], AF.Exp, [pk, npc], [ew], bias=npc[:, l, hp:hp + 1], scale=-1.0)
                    ACT(ew[:], ew[:], AF.Ln, [ew], [ew], bias=1.0)
                    ACT(ew[:], ew[:], AF.Exp, [ew], [ew], bias=-0.5, scale=-1.0)
                    yield
                    ps, pk = psbig()
                    MM(ps[:, 0:NT], [(lw_wa[64:128, l, hc], lora[64:128, :])], [lw_wa, lora], [pk])
                    ACT(a_t[:], ps[:, 0:NT], AF.Exp, [pk, npc], [a_t], bias=npc[:, l, 4 + hp:5 + hp], scale=-1.0)
                    ACT(a_t[:], a_t[:], AF.Ln, [a_t], [a_t], bias=1.0)
                    ACT(a_t[:], a_t[:], AF.Exp, [a_t], [a_t], scale=-1.0)
                    ps, pk = psbig()
                    MM(ps[:, 0:NT], [(lw_g[:, l, hc], sgl[:])], [lw_g, sgl], [pk])
                    CP('act', g_t[:], ps[:, 0:NT], [pk], [g_t])
                    yield
                    TS('dve', kk_t[:], k_, pcol(l, 'kk', hp), ALU.mult, [kk_, pc], [kk_t])
                    ACT(sqk[:], kk_t[:], AF.Square, [kk_t], [sqk])
                    ps, pk = psbig()
                    MM(ps[:, 0:NT], [(blk_bf[:], sqk[:])], [blk_bf, sqk], [pk])
                    yield
                    TS('dve', tmpa[:], ps[:, 0:NT], 1e-18, ALU.max, [pk], [tmpa])
                    ACT(tmpa[:], tmpa[:], AF.Ln, [tmpa], [tmpa])
                    ACT(tmpa[:], tmpa[:], AF.Exp, [tmpa], [tmpa], scale=-0.5)
                    TT('dve', kap[:], kk_t[:], tmpa[:], ALU.mult, [kk_t, tmpa], [kap])
                    yield
                    TS('dve', tmpa[:], a_t[:], -1.0, ALU.add, [a_t, pc], [tmpa], s2=pcol(l, 'ka', hp), op1=ALU.mult)
                    STT('dve', kmod[:], tmpa[:], 1.0, k_, ALU.add, ALU.mult, [tmpa, kk_], [kmod])
                    TT('dve', b_t[:], kap[:], a_t[:], ALU.mult, [kap, a_t], [b_t])
                    yield
                    STT('dve', tmpa[:], r_, pcol(l, 'rk', hp), kmod[:], ALU.mult, ALU.mult, [rk_, pc, kmod], [tmpa])
                    ps, pk = psbig()
                    MM(ps[:, 0:NT], [(C['blk'], tmpa[:])], ['cst', tmpa], [pk])
                    TT('dve', bon[:], ps[:, 0:NT], v_, ALU.mult, [pk, vk_], [bon])
                    yield
                    for ch in chunks:
                        SCAN(cl[:, ch.off:ch.off + ch.L], C['ones'][:, 0:ch.L], ew[:, ch.off:ch.off + ch.L], 0.0, ALU.mult, ALU.subtract,
                             ['cst', ew], [cl])
                    yield
                    ACT(E1[:], cl[:], AF.Exp, [cl], [E1])
                    TT('dve', tmpa[:], cl[:], ew[:], ALU.add, [cl, ew], [tmpa])
                    ACT(E2[:], tmpa[:], AF.Exp, [tmpa], [E2])
                    ACT(E3[:], cl[:], AF.Exp, [cl], [E3], scale=-1.0)
                    yield
                    TT('dve', rt[:], r_, E1[:], ALU.mult, [rk_, E1], [rt])
                    CP('act', rtb[:], rt[:], [rt], [rtb])
                    yield
                    po = 0
                    for (off, nseg, L, kind) in groups:
                        for (pd, src, sk, E_, ek) in [(padk, kap[:], kap, E2, E2), (padb, b_t[:], b_t, E3, E3),
                                                       (padq, kmod[:], kmod, E3, E3), (padv, v_, vk_, None, None)]:
                            pv = pd[:, po:po + nseg * 2 * L].rearrange("p (s t) -> p s t", t=2 * L)
                            for h2 in range(2):
                                prt = slice(h2 * 64, (h2 + 1) * 64)
                                sv = src[prt, off:off + nseg * L].rearrange("p (s t) -> p s t", t=L)
                                if E_ is None:
                                    CP('dve', pv[prt, :, h2 * L:(h2 + 1) * L], sv, [sk], [pd])
                                else:
                                    ev = E_[prt, off:off + nseg * L].rearrange("p (s t) -> p s t", t=L)
                                    TT('dve', pv[prt, :, h2 * L:(h2 + 1) * L], sv, ev, ALU.mult, [sk, ek], [pd])
                            yield
                        po += nseg * 2 * L
                def stage(hp, tick):
                    E1 = E1p[hp % 2]; yT = E2p[hp % 2]; rt = rtp[hp % 2]; rtb = rtbp[hp % 2]
                    padk = padp[hp % 2]['k']; padb = padp[hp % 2]['b']; padq = padp[hp % 2]['q']; padv = padp[hp % 2]['v']
                    pending = []
                    IDT = ident_bf if RDT == BF16 else C['ident']
                    IDK = kn(ident_bf) if RDT == BF16 else 'cst'
                    for bt_ in batches:
                        L = bt_[0].L; L2 = 2 * L; nb = len(bt_); po0 = bt_[0].padoff; o0 = bt_[0].off
                        Wd = nb * L2; Wl = nb * L; W8 = nb * 128
                        kp_ = kps[bi[0] % 2]; bi[0] += 1
                        J = {64: 5, 16: 3, 8: 2}[L]

                        def v3(ap, w):
                            return ap.rearrange("p (n t) -> p n t", t=w)
                        mlt = C['mlt%d' % L][0:L2, 0:L2].unsqueeze(1).to_broadcast([L2, nb, L2])
                        mgt = C['mgt%d' % L][0:L2, 0:L2].unsqueeze(1).to_broadcast([L2, nb, L2])
                        mle = C['mle%d' % L][0:L2, 0:L].unsqueeze(1).to_broadcast([L2, nb, L])
                        idb = C['ident'][0:L2, 0:L2].unsqueeze(1).to_broadcast([L2, nb, L2])
                        E1L = v3(E1[:, o0:o0 + Wl], L)[:, :, L - 1:L]
                        TT('dve', v3(wk['bh'][:, 0:Wd], L2), v3(padb[:, po0:po0 + Wd], L2), E1L.to_broadcast([128, nb, L2]), ALU.mult, [padb, E1], [wk['bh']])
                        TT('dve', v3(wk['kh'][:, 0:Wd], L2), v3(padq[:, po0:po0 + Wd], L2), E1L.to_broadcast([128, nb, L2]), ALU.mult, [padq, E1], [wk['kh']])
                        TT('dve', v3(wk['gI'][:, 0:W8], 128), C['ident'].unsqueeze(1).to_broadcast([128, nb, 128]), E1L.to_broadcast([128, nb, 128]),
                           ALU.mult, ['cst', E1], [wk['gI']])

                        def sl2(t, i):
                            return t[:, po0 + i * L2:po0 + (i + 1) * L2]

                        def bl(t, i):
                            return t[0:L2, i * L2:(i + 1) * L2]

                        def b8(t, i):
                            return t[0:L2, i * 128:(i + 1) * 128]
                        RB = range(nb)

                        def fr(ap):
                            return ap.bitcast(F32R) if USE_F32R else ap
                        ps, pk = psbig()
                        MMB([(bl(ps, i), sl2(padb, i), sl2(padk, i)) for i in RB], [padb, padk], [pk])
                        STT('dve', fr(v3(wk['A0T'][0:L2, 0:Wd], L2)), v3(ps[0:L2, 0:Wd], L2), -1.0, mlt, ALU.mult, ALU.mult, [pk, 'cst'], [wk['A0T']])
                        ps, pk = psbig()
                        MMB([(bl(ps, i), sl2(padk, i), sl2(padb, i)) for i in RB], [padb, padk], [pk])
                        STT('dve', fr(v3(wk['A0'][0:L2, 0:Wd], L2)), v3(ps[0:L2, 0:Wd], L2), -1.0, mgt, ALU.mult, ALU.mult, [pk, 'cst'], [wk['A0']])
                        ps, pk = psbig()
                        MMB([(bl(ps, i), sl2(padq, i), sl2(padk, i)) for i in RB], [padq, padk], [pk])
                        TT('dve', v3(wk['PT'][0:L2, 0:Wd], L2), v3(ps[0:L2, 0:Wd], L2), mlt, ALU.mult, [pk, 'cst'], [wk['PT']])
                        ps, pk = psbig()
                        MMB([(ps[0:L2, i * L:(i + 1) * L], sl2(padb, i), rtb[:, o0 + i * L:o0 + (i + 1) * L]) for i in RB], [padb, rtb], [pk])
                        TT('dve', v3(wk['QbT'][0:L2, 0:Wl], L), v3(ps[0:L2, 0:Wl], L), mle, ALU.mult, [pk, 'cst'], [wk['QbT']])
                        ps, pk = psbig()
                        MMB([(ps[0:L2, i * L:(i + 1) * L], sl2(padq, i), rtb[:, o0 + i * L:o0 + (i + 1) * L]) for i in RB], [padq, rtb], [pk])
                        TT('dve', v3(wk['QkT'][0:L2, 0:Wl], L), v3(ps[0:L2, 0:Wl], L), mle, ALU.mult, [pk, 'cst'], [wk['QkT']])
                        tick()
                        for (dst, srct, off_) in [(wk['Vbd'], padv, po0), (wk['Bbd'], wk['bh'], 0), (wk['Kbd'], wk['kh'], 0), (wk['Ktm'], padk, po0)]:
                            ps, pk = psbig()
                            MMB([(b8(ps, i), srct[:, off_ + i * L2:off_ + (i + 1) * L2], IDT[:, :]) for i in RB], [srct, IDK], [pk])
                            CPalt(dst[0:L2, 0:W8], ps[0:L2, 0:W8], [pk], [dst])
                        ps, pk = psbig()
                        MMB([(b8(ps, i), bl(wk['PT'], i), b8(wk['Vbd'], i)) for i in RB], [wk['PT'], wk['Vbd']], [pk])
                        CPalt(wk['PVs'][0:L2, 0:W8], ps[0:L2, 0:W8], [pk], [wk['PVs']])
                        tick()
                        Z = wk['Z']
                        TT('dve', fr(v3(Z[0:L2, 0:Wd], L2)), v3(wk['A0T'][0:L2, 0:Wd], L2), idb, ALU.add, [wk['A0T'], 'cst'], [Z])
                        Ap, ApT = wk['A0'], wk['A0T']
                        An, AnT = wk['A1'], wk['A1T']
                        for jj in range(1, J + 1):
                            ps, pk = psbig()
                            MMB([(bl(ps, i), fr(bl(ApT, i)), fr(bl(Ap, i))) for i in RB], [Ap, ApT], [pk])
                            CP('act', fr(An[0:L2, 0:Wd]), ps[0:L2, 0:Wd], [pk], [An])
                            if jj < J:
                                ps, pk = psbig()
                                MMB([(bl(ps, i), fr(bl(Ap, i)), fr(bl(ApT, i))) for i in RB], [Ap, ApT], [pk])
                                CP('act', fr(AnT[0:L2, 0:Wd]), ps[0:L2, 0:Wd], [pk], [AnT])
                            ps, pk = psbig()
                            MMB([(bl(ps, i), fr(bl(An, i)), fr(bl(Z, i))) for i in RB], [An, Z], [pk])
                            TT('dve', fr(Z[0:L2, 0:Wd]), Z[0:L2, 0:Wd], ps[0:L2, 0:Wd], ALU.add, [Z, pk], [Z])
                            Ap, ApT, An, AnT = An, AnT, Ap, ApT
                            if pending:
                                pending.pop(0)()
                            tick()
                        tick()
                        Zb = wk['Zb']
                        CP('act', Zb[0:L2, 0:Wd], Z[0:L2, 0:Wd], [Z], [Zb])
                        ps, pk = psbig()
                        MMB([(b8(ps, i), bl(Zb, i), b8(wk['Ktm'], i)) for i in RB], [Zb, wk['Ktm']], [pk])
                        CP('act', wk['Kh'][0:L2, 0:W8], ps[0:L2, 0:W8], [pk], [wk['Kh']])
                        ps, pk = psbig()
                        MMB([(b8(ps, i), bl(Zb, i), b8(wk['PVs'], i)) for i in RB], [Zb, wk['PVs']], [pk])
                        P.op('act', lambda e, o_=wk['U0'][0:L2, 0:W8], i_=ps[0:L2, 0:W8]: e.mul(out=o_, in_=i_, mul=-1.0), [pk], [kn(wk['U0'])])
                        tick()
                        while pending:
                            pending.pop(0)()
                        ps, pk = psbig()
                        MMB([(ps[:, i * L:(i + 1) * L], b8(wk['Kh'], i), wk['QbT'][0:L2, i * L:(i + 1) * L]) for i in RB], [wk['Kh'], wk['QbT']], [pk])
                        TT('dve', kp_['Rh'][:, 0:Wl], rt[:, o0:o0 + Wl], ps[:, 0:Wl], ALU.subtract, [rt, pk], [kp_['Rh']])
                        ps, pk = psbig()
                        MMG([(ps[:, i * L:(i + 1) * L], [(b8(wk['U0'], i), wk['QbT'][0:L2, i * L:(i + 1) * L]),
                                                        (b8(wk['Vbd'], i), wk['QkT'][0:L2, i * L:(i + 1) * L])]) for i in RB],
                            [wk['U0'], wk['QbT'], wk['Vbd'], wk['QkT']], [pk])
                        CP('act', kp_['Y0'][:, 0:Wl], ps[:, 0:Wl], [pk], [kp_['Y0']])
                        ps, pk = psbig()
                        MMG([(ps[:, i * 128:(i + 1) * 128], [(b8(wk['Bbd'], i), b8(wk['U0'], i)), (b8(wk['Kbd'], i), b8(wk['Vbd'], i))]) for i in RB],
                            [wk['Bbd'], wk['U0'], wk['Kbd'], wk['Vbd']], [pk])
                        CP('act', kp_['D0'][:, 0:W8], ps[:, 0:W8], [pk], [kp_['D0']])
                        ps, pk = psbig()
                        MMB([(ps[:, i * 128:(i + 1) * 128], b8(wk['Kh'], i), b8(wk['Bbd'], i)) for i in RB], [wk['Kh'], wk['Bbd']], [pk])
                        TT('dve', kp_['Mx'][:, 0:W8], wk['gI'][:, 0:W8], ps[:, 0:W8], ALU.subtract, [wk['gI'], pk], [kp_['Mx']])
                        while pending:
                            pending.pop(0)()

                        def step2(i, ch, kp_=kp_, L=L):
                            o = ch.off
                            if ch.kind == 'p':
                                S_ap = Sbd[l][hp][:]; S_key = kn(Sbd[l][hp])
                            else:
                                S_ap = Sbs[:, ch.seq * 4 + hp, :]; S_key = ('Sbs', ch.seq * 4 + hp)
                            psY, pkY = psbig()
                            MM(psY[:, 0:L], [(S_ap, kp_['Rh'][:, i * L:(i + 1) * L])], [S_key, kp_['Rh']], [pkY])
                            psS, pkS = psbig()
                            MM(psS[:, 0:128], [(kp_['Mx'][:, i * 128:(i + 1) * 128], S_ap)], [S_key, kp_['Mx']], [pkS])
                            TT('dve', S_ap, psS[:, 0:128], kp_['D0'][:, i * 128:(i + 1) * 128], ALU.add, [pkS, kp_['D0']], [S_key])
                            TT('dve', yT[:, o:o + L], psY[:, 0:L], kp_['Y0'][:, i * L:(i + 1) * L], ALU.add, [pkY, kp_['Y0']], [yT])
                        for i, ch in enumerate(bt_):
                            pending.append(lambda i=i, ch=ch, f=step2: f(i, ch))
                    while pending:
                        pending.pop(0)()
                def post(hp):
                    yT = E2p[hp % 2]; bon = bonp[hp % 2]; g_t = g_tp[hp % 2]
                    ps, pk = psbig()
                    MM(ps[:, 0:NT], [(C['blk'], yT[:])], ['cst', yT], [pk])
                    STT('dve', yT[:], ps[:, 0:NT], -1.0 / 64, yT[:], ALU.mult, ALU.add, [pk, yT], [yT])
                    ACT(tmpa[:], yT[:], AF.Square, [yT], [tmpa])
                    ps, pk = psbig()
                    MM(ps[:, 0:NT], [(C['blk'], tmpa[:])], ['cst', tmpa], [pk])
                    ACT(tmpa[:], ps[:, 0:NT], AF.Ln, [pk], [tmpa], bias=RW_GN_EPS, scale=1.0 / 64)
                    ACT(tmpa[:], tmpa[:], AF.Exp, [tmpa], [tmpa], scale=-0.5)
                    TT('dve', yT[:], yT[:], tmpa[:], ALU.mult, [yT, tmpa], [yT])
                    STT('dve', yT[:], yT[:], pcol(l, 'gng', hp), bon[:], ALU.mult, ALU.add, [yT, pc, bon], [yT])
                    STT('dve', yaT[:, hp, 0:NT], yT[:], pcol(l, 'gnb', hp), g_t[:], ALU.add, ALU.mult, [yT, pc, g_t], [('yaT', hp)])
                def advance(g, n=1):
                    if g is None:
                        return
                    for _ in range(n):
                        try:
                            next(g)
                        except StopIteration:
                            return
                for _ in pre_gen(0):
                    pass
                for hp in range(4):
                    nxt = pre_gen(hp + 1) if hp < 3 else None
                    stage(hp, lambda: advance(nxt, 1))
                    if nxt is not None:
                        for _ in nxt:
                            pass
                    post(hp)
                if blk_i == 0:
                    dst = O['oS'][l].rearrange("s (hp h2) v k -> h2 v s hp k", h2=2)
                    for half in range(2):
                        for s8 in range(8):
                            s_ = half * 8 + s8; u0 = s_ * 4
                            ps, pk = psbig()
                            TRB([(ps[:, i * 128:(i + 1) * 128], Sbs[:, u0 + i, :]) for i in range(4)], [('Sbs', u0 + i) for i in range(4)], [pk])
                            CPalt(Sst[:, s8 * 4:s8 * 4 + 4, :], ps[:, 0:512].rearrange("p (u k) -> p u k", k=128), [pk], [('Sst', s8)])
                        for h2 in range(2):
                            P.dma('sp', dst[h2][:, half * 8:(half + 1) * 8], Sst4[h2 * 64:(h2 + 1) * 64, :, :, h2 * 64:(h2 + 1) * 64],
                                  reads=[('Sst', s8) for s8 in range(8)])
                if debug and l == 0 and blk_i == 0:
                    dump('yaT', yaT[:, :, 0:NT], [128, 4, NT], [('yaT', h) for h in range(4)])
                P.barrier()
                P.flush()

        def mlstm_phase(l, NT, groups, chunks, blk_i):
            with ExitStack() as sc:
                nck = len(chunks)
                npd = sum(nseg * (L + 3) for (off, nseg, L, kind) in groups)
                qkp = T('qkp', [128, 8, npd], F32, sc)
                cacc = T('cacc', [128, NT], F32, sc)
                qb = T('qb', [128, 4, NT], BF16, sc); kb = T('kb', [128, 4, NT], BF16, sc); kf = T('kf', [128, 4, NT], F32, sc)
                sgo = T('sgo', [128, 4, NT], BF16, sc)
                vtm = T('vtm', [128, nck, 4, 132], BF16, sc)
                gsb = T('gsb', [8, NT], F32, sc)
                h4 = T('h4', [128, 4, NT], F32, sc)
                igb = T('igb', [128, NT], F32, sc); sp_ = T('sp_', [128, NT], F32, sc)
                cc4 = T('cc4', [128, 4, NT], F32, sc); G4 = T('G4', [128, 4, NT], F32, sc); em4 = T('em4', [128, 4, NT], F32, sc)
                sc4 = T('sc4', [128, 4, NT], F32, sc); tmpb = T('tmpb', [128, NT], F32, sc)
                if blk_i == 0:
                    cvin = T('cvin', [128, 8, NS * 3], F32, sc); cvout = T('cvout', [128, 8, NS * 3], F32, sc)
                    to_fm(lambda j: cvin[:, j, :], I['scv'][l], NS * 3, D, [cvin])
                    Cst = T('Cst', [128, 32, 128], F32, sc)
                    CTs = T('CTs', [128, NS * 4, 132], F32, sc)
                    m_s = T('m_s', [128, NS * 4], F32, sc); mo_s = T('mo_s', [128, NS * 4], F32, sc)
                    nst = T('nst', [128, NS * 4], F32, sc)
                    CTbC = Cst[:].bitcast(BF16).rearrange("p a (b c) -> p (a b) c", b=2)
                    CTbn = T('CTbn', [128, NS * 4], BF16, sc)
                    for g4 in range(4):
                        g2 = g4 % 2
                        P.dma('sp', Cst[:, g2 * 16:(g2 + 1) * 16, :].rearrange("p (s h) k -> p s h k", h=4),
                              I['sC'][l, g4 * 4:(g4 + 1) * 4].rearrange("s h v k -> v s h k"), writes=[('Cst', g2)])
                        for q4 in range(4):
                            u0 = g4 * 16 + q4 * 4
                            ps, pk = psbig()
                            TRB([(ps[:, i * 128:(i + 1) * 128], Cst[:, g2 * 16 + q4 * 4 + i, :]) for i in range(4)], [('Cst', g2)], [pk])
                            CPalt(CTs[:, u0:u0 + 4, 0:128], ps[:, 0:512].rearrange("p (u k) -> p u k", k=128), [pk], [('CTs', u0 // 4)])
                    to_fm(lambda j: nst[:, :], I['sn'][l].rearrange("s h k -> (s h) k"), NS * 4, 128, [nst])
                    CP('dve', CTs[:, :, 128], nst[:], [nst] + [('CTs', u) for u in range(NS)], [('CTs', u) for u in range(NS)])
                    P.dma('sp', m_s[:], I['sm'][l:l + 1, :].partition_broadcast(128), writes=[m_s])
                    CP('act', CTbC, CTs[:, :, 0:128], [('CTs', u) for u in range(NS)], [('Cst', 0), ('Cst', 1)])
                    CP('act', CTbn[:], CTs[:, :, 128], [('CTs', u) for u in range(NS)], [CTbn])
                MSET('dve', vtm[:, :, :, 128:129], 1.0, [vtm])
                slC = load_slab(I['w_in'][l], 0, NCH, SLABS_IN[2][0], SLABS_IN[2][1])
                slD = load_slab(I['w_in'][l], 0, NCH, SLABS_IN[3][0], SLABS_IN[3][1])
                for j in range(8):
                    ps, pk = dense_tile(slC, j * 128, 128, NT)
                    po = 0
                    for (off, nseg, L, kind) in groups:
                        qv = qkp[:, j, po:po + nseg * (L + 3)].rearrange("p (s t) -> p s t", t=L + 3)
                        pv = ps[:, off:off + nseg * L].rearrange("p (s t) -> p s t", t=L)
                        CP('act', qv[:, :, 3:L + 3], pv, [pk], [('qkp', j)])
                        if kind == 'p':
                            CP('dve', qv[:, 0, 0:3], cvc[:, l, j, :], [cvc], [('qkp', j)])
                            CP('dve', cvc[:, l, j, :], qv[:, 0, L:L + 3], [('qkp', j)], [cvc])
                        else:
                            CP('dve', qv[:, :, 0:3], cvin[:, j, :].rearrange("p (s t) -> p s t", t=3), [cvin], [('qkp', j)])
                            CP('dve', cvout[:, j, :].rearrange("p (s t) -> p s t", t=3), qv[:, :, L:L + 3], [('qkp', j)], [cvout])
                        av = cacc[:, off:off + nseg * L].rearrange("p (s t) -> p s t", t=L)
                        TS('dve', av, qv[:, :, 0:L], pcol(l, 'cw', 0 * 8 + j), ALU.mult, [('qkp', j), pc], [cacc],
                           s2=pcol(l, 'cb', j), op1=ALU.add)
                        for tp in range(1, 4):
                            STT('dve', av, qv[:, :, tp:tp + L], pcol(l, 'cw', tp * 8 + j), av, ALU.mult, ALU.add, [('qkp', j), pc, cacc], [cacc])
                        po += nseg * (L + 3)
                    if j < 4:
                        ACT(qb[:, j, :], cacc[:], AF.Silu, [cacc], [('qb', j)])
                    else:
                        ACT(kf[:, j - 4, :], cacc[:], AF.Silu, [cacc], [('kf', j - 4)])
                        TS('dve', kf[:, j - 4, :], kf[:, j - 4, :], 128.0 ** -0.5, ALU.mult, [('kf', j - 4)], [('kf', j - 4)])
                        CP('dve', kb[:, j - 4, :], kf[:, j - 4, :], [('kf', j - 4)], [('kb', j - 4)])
                if blk_i == 0:
                    from_fm(O['ocv'][l], lambda j: cvout[:, j, :], NS * 3, D, [cvout])
                for ci, ch in enumerate(chunks):
                    ps, pk = psbig()
                    MM(ps[0:ch.L, 0:512], [(uT[:, c, ch.off:ch.off + ch.L], slD[:, c, 0:512]) for c in range(NCH)], [slD] + UT_ALL, [pk])
                    CPalt(vtm[0:ch.L, ci, :, 0:128], ps[0:ch.L, 0:512].rearrange("p (h v) -> p h v", v=128), [pk], [vtm])
                for j in range(4):
                    ps, pk = dense_tile(slD, 512 + j * 128, 128, NT)
                    ACT(sgo[:, j, :], ps[:, 0:NT], AF.Sigmoid, [pk], [('sgo', j)])
                ps, pk = psbig()
                MM(ps[0:8, 0:NT], [(slD[:, c, 1024:1032], uT[:, c, 0:NT]) for c in range(NCH)], [slD] + UT_ALL, [pk])
                CP('act', gsb[:], ps[0:8, 0:NT], [pk], [gsb])
                if debug and l == 0 and blk_i == 0:
                    dump('qb', qb[:], [128, 4, NT], [('qb', j) for j in range(4)])
                ctm = [{n: T('m%s%d' % (n, i), [128, w_], dt_, sc) for (n, w_, dt_) in
                        [('t3', 512, F32), ('AT', 512, F32), ('qt', 256, F32),
                         ('aqk', 512, BF16), ('qtb', 512, BF16), ('cco', 64, F32), ('wco', 64, F32), ('wtmp', 64, F32)]} for i in range(2)]
                for d_ in ctm:
                    d_['den'] = d_['t3']
                for h in range(4):
                    ps, pk = psbig()
                    MM(ps[:, 0:NT], [(C['ident'][0:8, h:h + 1].to_broadcast([8, 128]), gsb[:])], ['cst', gsb], [pk])
                    TS('dve', igb[:], ps[:, 0:NT], gb[:, l * 8 + h:l * 8 + h + 1], ALU.add, [pk, gb], [igb])
                    ps, pk = psbig()
                    MM(ps[:, 0:NT], [(C['ident'][0:8, 4 + h:5 + h].to_broadcast([8, 128]), gsb[:])], ['cst', gsb], [pk])
                    ACT(sp_[:], ps[:, 0:NT], AF.Exp, [pk, ngb], [sp_], bias=ngb[:, l * 8 + 4 + h:l * 8 + 5 + h], scale=-1.0)
                    ACT(sp_[:], sp_[:], AF.Ln, [sp_], [sp_], bias=1.0)
                    for ch in chunks:
                        SCAN(em4[:, h, ch.off:ch.off + ch.L], C['ones'][:, 0:ch.L], sp_[:, ch.off:ch.off + ch.L], 0.0, ALU.mult, ALU.subtract,
                             ['cst', sp_], [('em4', h)])
                    TT('dve', cc4[:, h, :], igb[:], em4[:, h, :], ALU.subtract, [igb, ('em4', h)], [('cc4', h)])
                for ci, ch in enumerate(chunks):
                    for h in range(4):
                        L = ch.L; o = ch.off
                        if ch.kind == 'p':
                            m_in = mP[:, l, h:h + 1]; m_key = ('mP', h); m_out = m_in; mo_key = ('mP', h)
                        else:
                            u = ch.seq * 4 + h
                            m_in = m_s[:, u:u + 1]; m_key = 'm_s'; m_out = mo_s[:, u:u + 1]; mo_key = ('mo_s', u)
                        SCAN(G4[:, h, o:o + L], cc4[:, h, o:o + L], cc4[:, h, o:o + L], m_in, ALU.max, ALU.max, [('cc4', h), m_key], [('G4', h)])
                        ACT(sc4[:, h, o:o + L], G4[:, h, o:o + L], AF.Exp, [('G4', h), m_key], [('sc4', h)], bias=m_in, scale=-1.0)
                        TT('dve', m_out, em4[:, h, o + L - 1:o + L], G4[:, h, o + L - 1:o + L], ALU.add, [('em4', h), ('G4', h)], [mo_key])
                for h in range(4):
                    TT('dve', tmpb[:], em4[:, h, :], G4[:, h, :], ALU.add, [('em4', h), ('G4', h)], [tmpb])
                    ACT(em4[:, h, :], tmpb[:], AF.Exp, [tmpb], [('em4', h)], scale=-1.0)
                H4 = lambda n: [(n, h) for h in range(4)]

                mgroups = []
                for ci, ch in enumerate(chunks):
                    if mgroups and ch.kind == 's' and chunks[mgroups[-1][0]].kind == 's' and mgroups[-1][1] < 16:
                        mgroups[-1][1] += 1
                    else:
                        mgroups.append([ci, 1])

                def g_ctx(gi):
                    ci0, ns = mgroups[gi]
                    ch0 = chunks[ci0]
                    return ci0, ns, ch0.L, ch0.off, ctm[gi % 2]

                def CT_of(ch):
                    if ch.kind == 'p':
                        return CTp[l][:, :, :], kn(CTp[l])
                    return CTs[:, ch.seq * 4:(ch.seq + 1) * 4, :], ('CTs', ch.seq)

                def ml_A(gi):
                    ci0, ns, L, o0, tm = g_ctx(gi)
                    W = ns * L; W4 = 4 * W

                    def src4(t):
                        return t.rearrange("p h (n t) -> p h n t", t=L)

                    def f4(ap):
                        return ap.rearrange("p (h n t) -> p h n t", h=4, n=ns)
                    mnegb = C['mneg%d' % L][0:L, 0:L].unsqueeze(1).unsqueeze(1).to_broadcast([L, 4, ns, L])
                    idb = C['ident'][0:L, 0:L].unsqueeze(1).unsqueeze(1).to_broadcast([L, 4, ns, L])
                    t3 = f4(tm['t3'][0:L, 0:W4]); cco = tm['cco']; wco = tm['wco']
                    cc3 = cco[0:L, 0:4 * ns].rearrange("p (h n) -> p h n", n=ns)
                    TT('dve', t3, src4(cc4[0:L, :, o0:o0 + W]), idb, ALU.mult, H4('cc4') + ['cst'], [tm['t3']])
                    P.op('dve', lambda e, o_=cc3, i_=t3: e.tensor_reduce(out=o_, in_=i_, axis=mybir.AxisListType.X, op=ALU.add),
                         [kn(tm['t3'])], [kn(cco)])
                    TT('dve', t3, mnegb, src4(G4[0:L, :, o0:o0 + W]), ALU.subtract, H4('G4') + ['cst'], [tm['t3']])
                    TT('dve', t3, t3, cc3.unsqueeze(3).to_broadcast([L, 4, ns, L]), ALU.add, [tm['t3'], cco], [tm['t3']])
                    ACT(tm['AT'][0:L, 0:W4], tm['t3'][0:L, 0:W4], AF.Exp, [tm['t3']], [tm['AT']])
                    ps, pk = psbig()
                    MMB([(ps[0:L, (h * ns + n) * L:(h * ns + n + 1) * L], kb[:, h, o0 + n * L:o0 + (n + 1) * L], qb[:, h, o0 + n * L:o0 + (n + 1) * L])
                         for h in range(4) for n in range(ns)], H4('kb') + H4('qb'), [pk])
                    TT('dve', tm['aqk'][0:L, 0:W4], tm['AT'][0:L, 0:W4], ps[0:L, 0:W4], ALU.mult, [tm['AT'], pk], [tm['aqk']])
                    qtt = tm['qtb'] if chunks[ci0].kind == 's' else tm['qt']
                    TT('dve', f4(qtt[:, 0:W4]), src4(qb[:, :, o0:o0 + W]), src4(sc4[:, :, o0:o0 + W]), ALU.mult, H4('qb') + H4('sc4'), [qtt])
                    TT('dve', tm['wtmp'][0:L, 0:4 * ns].rearrange("p (h n) -> p h n", n=ns), cc3, src4(G4[0:L, :, o0:o0 + W])[:, :, :, L - 1],
                       ALU.subtract, [cco] + H4('G4'), [tm['wtmp']])
                    ACT(wco[0:L, 0:4 * ns], tm['wtmp'][0:L, 0:4 * ns], AF.Exp, [tm['wtmp']], [wco])

                def ml_B(gi):
                    ci0, ns, L, o0, tm = g_ctx(gi)
                    W = ns * L; W4 = 4 * W
                    grpN = []; grpD = []; ckeys = []
                    smp = chunks[ci0].kind == 's'
                    qtt = tm['qtb'] if smp else tm['qt']
                    for h in range(4):
                        for n in range(ns):
                            ch_ = chunks[ci0 + n]
                            if smp:
                                u_ = ch_.seq * 4 + h
                                Cm = CTbC[:, u_, :]; ncol_ = CTbn[:, u_:u_ + 1]
                                for k_ in (('Cst', 0), ('Cst', 1), kn(CTbn)):
                                    if k_ not in ckeys:
                                        ckeys.append(k_)
                            else:
                                CTv, CT_key = CT_of(ch_)
                                Cm = CTv[:, h, 0:128]; ncol_ = CTv[:, h, 128:129]
                                if CT_key not in ckeys:
                                    ckeys.append(CT_key)
                            c0 = (h * ns + n) * L
                            grpN.append((None, c0, [(vtm[0:L, ci0 + n, h, 0:128], tm['aqk'][0:L, c0:c0 + L]), (Cm, qtt[:, c0:c0 + L])]))
                            grpD.append((None, c0, [(ones_bf[0:L, :], tm['aqk'][0:L, c0:c0 + L]),
                                                    (ncol_.to_broadcast([128, 128]), qtt[:, c0:c0 + L])]))
                    psN, pkN = psbig()
                    MMG([(psN[:, c0:c0 + L], prs) for (_, c0, prs) in grpN], [vtm, tm['aqk'], qtt] + ckeys, [pkN])
                    psD, pkD = psbig()
                    MMG([(psD[:, c0:c0 + L], prs) for (_, c0, prs) in grpD], [ones_bf, tm['aqk'], qtt] + ckeys, [pkD])
                    ACT(tm['den'][:, 0:W4], psD[:, 0:W4], AF.Abs, [pkD], [tm['den']])
                    den4 = tm['den'][:, 0:W4].rearrange("p (h n t) -> p h n t", h=4, n=ns)
                    TT('dve', den4, den4, em4[:, :, o0:o0 + W].rearrange("p h (n t) -> p h n t", t=L), ALU.max, [tm['den']] + H4('em4'), [tm['den']])
                    ACT(tm['den'][:, 0:W4], tm['den'][:, 0:W4], AF.Ln, [tm['den']], [tm['den']])
                    ACT(tm['den'][:, 0:W4], tm['den'][:, 0:W4], AF.Exp, [tm['den']], [tm['den']], scale=-1.0)
                    TT('dve', h4[:, :, o0:o0 + W].rearrange("p h (n t) -> p h n t", t=L), psN[:, 0:W4].rearrange("p (h n t) -> p h n t", h=4, n=ns),
                       den4, ALU.mult, [pkN, tm['den']], H4('h4'))

                kwt = [T('kwt%d' % i, [128, 512], BF16, sc) for i in range(2)]

                def ml_K(gi, n):
                    ci0, ns, L, o0, tm = g_ctx(gi)
                    ci = ci0 + n; o = chunks[ci].off
                    kw = kwt[ci % 2]
                    wc3 = tm['wco'][0:L, 0:4 * ns].rearrange("p (h n) -> p h n", n=ns)[:, :, n:n + 1]
                    ps, pk = psbig()
                    TRB([(ps[0:L, h * 128:(h + 1) * 128], kf[:, h, o:o + L]) for h in range(4)], H4('kf'), [pk])
                    TT('dve', kw[0:L, 0:512].rearrange("p (h c) -> p h c", c=128), ps[0:L, 0:512].rearrange("p (h c) -> p h c", c=128),
                       wc3.to_broadcast([L, 4, 128]), ALU.mult, [pk, tm['wco']], [kw])

                def ml_C(gi, n):
                    ci0, ns, L, o0, tm = g_ctx(gi)
                    ci = ci0 + n; ch = chunks[ci]; o = ch.off
                    CTv, CT_key = CT_of(ch)
                    kw = kwt[ci % 2]
                    psC, pkC = psbig()
                    MMB([(psC[:, h * 128:(h + 1) * 128], kw[0:L, h * 128:(h + 1) * 128], vtm[0:L, ci, h, 0:128]) for h in range(4)], [kw, vtm], [pkC])
                    psn, pkn = psbig()
                    MMB([(psn[:, h:h + 1], kw[0:L, h * 128:(h + 1) * 128], vtm[0:L, ci, h, 128:129]) for h in range(4)], [kw, vtm], [pkn])
                    TT('dve', CTv[:, :, 0:129], CTv[:, :, 0:129], sc4[:, :, o + L - 1:o + L].to_broadcast([128, 4, 129]), ALU.mult,
                       [CT_key] + H4('sc4'), [CT_key])
                    TT('dve', CTv[:, :, 0:128], CTv[:, :, 0:128], psC[:, 0:512].rearrange("p (h c) -> p h c", c=128), ALU.add, [CT_key, pkC], [CT_key])
                    TT('dve', CTv[:, :, 128], CTv[:, :, 128], psn[:, 0:4], ALU.add, [CT_key, pkn], [CT_key])

                ml_A(0)
                ml_K(0, 0)
                for gi in range(len(mgroups)):
                    if gi + 1 < len(mgroups):
                        ml_A(gi + 1)
                    ml_B(gi)
                    ns_ = mgroups[gi][1]
                    for n in range(ns_):
                        if n + 1 < ns_:
                            ml_K(gi, n + 1)
                        elif gi + 1 < len(mgroups):
                            ml_K(gi + 1, 0)
                        ml_C(gi, n)
                tq = [tmpb, igb, sp_, cacc]
                pks = []
                for h in range(4):
                    ps, pk = psbig()
                    MM(ps[:, 0:NT], [(C['ones'], h4[:, h, :])], ['cst', ('h4', h)], [pk])
                    pks.append((ps, pk))
                for h in range(4):
                    ps, pk = pks[h]
                    STT('dve', h4[:, h, :], ps[:, 0:NT], -1.0 / 128, h4[:, h, :], ALU.mult, ALU.add, [pk, ('h4', h)], [('h4', h)])
                for h in range(4):
                    ACT(tq[h][:], h4[:, h, :], AF.Square, [('h4', h)], [tq[h]])
                pks = []
                for h in range(4):
                    ps, pk = psbig()
                    MM(ps[:, 0:NT], [(C['ones'], tq[h][:])], ['cst', tq[h]], [pk])
                    pks.append((ps, pk))
                for h in range(4):
                    ps, pk = pks[h]
                    ACT(tq[h][:], ps[:, 0:NT], AF.Ln, [pk], [tq[h]], bias=ML_GN_EPS, scale=1.0 / 128)
                for h in range(4):
                    ACT(tq[h][:], tq[h][:], AF.Exp, [tq[h]], [tq[h]], scale=-0.5)
                for h in range(4):
                    STT('dve', h4[:, h, :], h4[:, h, :], pcol(l, 'mlg', h), tq[h][:], ALU.mult, ALU.mult, [('h4', h), pc, tq[h]], [('h4', h)])
                for h in range(4):
                    TT('dve', ybT[:, h, 0:NT], h4[:, h, :], sgo[:, h, :], ALU.mult, [('h4', h), ('sgo', h)], [('ybT', h)])
                if blk_i == 0:
                    for g4 in range(4):
                        g2 = g4 % 2
                        for q4 in range(4):
                            u0 = g4 * 16 + q4 * 4
                            ps, pk = psbig()
                            TRB([(ps[:, i * 128:(i + 1) * 128], CTs[:, u0 + i, 0:128]) for i in range(4)], [('CTs', u0 // 4)], [pk])
                            CPalt(Cst[:, g2 * 16 + q4 * 4:g2 * 16 + q4 * 4 + 4, :], ps[:, 0:512].rearrange("p (u k) -> p u k", k=128), [pk], [('Cst', g2)])
                        P.dma('sp', O['oC'][l, g4 * 4:(g4 + 1) * 4].rearrange("s h v k -> v s h k"),
                              Cst[:, g2 * 16:(g2 + 1) * 16, :].rearrange("p (s h) k -> p s h k", h=4), reads=[('Cst', g2)])
                    CP('dve', nst[:], CTs[:, :, 128], [('CTs', u) for u in range(NS)], [nst])
                    from_fm(O['on'][l], lambda j: nst[:, :], NS * 4, 128, [nst])
                    P.dma('sp', O['om'][l:l + 1, :], mo_s[0:1, :], reads=[('mo_s', u) for u in range(NS * 4)])
                if debug and l == 0 and blk_i == 0:
                    dump('ybT', ybT[:, :, 0:NT], [128, 4, NT], [('ybT', h) for h in range(4)])
                P.barrier()
                P.flush()

        def tail_phase(l, NT):
            with ExitStack() as sc:
                gab = T('gab', [128, 16, NT], F32, sc)
                mrg = T('mrg', [128, NCH, NT], BF16, sc)
                acc = T('acc', [128, NCH, NT], F32, sc)
                tmpm = [T('tmpm%d' % i, [128, NT], F32, sc) for i in range(2)]
                aT = T('aT', [128, NCH, NT], BF16, sc)
                for si in (4, 5):
                    sl = load_slab(I['w_in'][l], 0, NCH, SLABS_IN[si][0], SLABS_IN[si][1])
                    for j in range(8):
                        ps, pk = dense_tile(sl, j * 128, 128, NT)
                        ACT(gab[:, (si - 4) * 8 + j, :], ps[:, 0:NT], AF.Sigmoid, [pk], [('gab', (si - 4) * 8 + j)])
                i = st['slab']; st['slab'] = (i + 1) % 2
                sl = slabs[i]
                P.dma('pool', sl[:, 0:4, 0:D], I['p_a'][l].rearrange("(c p) n -> p c n", p=128), writes=[sl], slab=True)
                P.dma('pool', sl[:, 4:8, 0:D], I['p_b'][l].rearrange("(c p) n -> p c n", p=128), writes=[sl], slab=True)
                YA = [('yaT', h) for h in range(4)]; YB = [('ybT', h) for h in range(4)]
                for j in range(8):
                    psa, pka = psbig()
                    MM(psa[:, 0:NT], [(sl[:, c, j * 128:(j + 1) * 128], yaT[:, c, 0:NT]) for c in range(4)], [sl] + YA, [pka])
                    psb_, pkb = psbig()
                    MM(psb_[:, 0:NT], [(sl[:, 4 + c, j * 128:(j + 1) * 128], ybT[:, c, 0:NT]) for c in range(4)], [sl] + YB, [pkb])
                    t_ = tmpm[j % 2]
                    TT('dve', t_[:], gab[:, j, :], psa[:, 0:NT], ALU.mult, [('gab', j), pka], [t_])
                    TT('dve', gab[:, 8 + j, :], gab[:, 8 + j, :], psb_[:, 0:NT], ALU.mult, [('gab', 8 + j), pkb], [('gab', 8 + j)])
                    TT('dve', mrg[:, j, :], t_[:], gab[:, 8 + j, :], ALU.add, [t_, ('gab', 8 + j)], [('mrg', j)])
                MR = [('mrg', j) for j in range(8)]
                sl = load_slab(I['w_out'][l], 0, NCH, 0, D)
                for j in range(8):
                    ps, pk = dense_tile(sl, j * 128, 128, NT, rhs=mrg, rkeys=MR)
                    CPalt(acc[:, j, :], ps[:, 0:NT], [pk], [acc])
                rmsnorm_add(acc, NT, l, 'post1', sc)
                rmsnorm_u(xT, NT, l, 'pre2', sc)
                AT_ = [('aT', j) for j in range(8)]
                for q in range(4):
                    slu = load_slab(I['w_ff_up'][l], 0, NCH, q * D, (q + 1) * D)
                    sld = load_slab(I['w_ff_down'][l], q * D, NCH, 0, D)
                    for j in range(8):
                        ps, pk = dense_tile(slu, j * 128, 128, NT)
                        t_ = tmpm[j % 2]
                        ACT(t_[:], ps[:, 0:NT], AF.Relu, [pk], [t_])
                        TT('dve', aT[:, j, :], t_[:], t_[:], ALU.mult, [t_], [('aT', j)])
                    for j in range(8):
                        ps, pk = dense_tile(sld, j * 128, 128, NT, rhs=aT, rkeys=AT_)
                        if q == 0:
                            CPalt(acc[:, j, :], ps[:, 0:NT], [pk], [acc])
                        else:
                            TT('dve', acc[:, j, :], acc[:, j, :], ps[:, 0:NT], ALU.add, [acc, pk], [acc])
                rmsnorm_add(acc, NT, l, 'post2', sc)
                P.barrier()
                P.flush()

        def mkchunks(groups):
            chunks = []
            po = 0
            for (off, nseg, L, kind) in groups:
                for s_ in range(nseg):
                    ch = Chunk()
                    ch.off = off + s_ * L; ch.L = L; ch.kind = kind; ch.seq = s_; ch.padoff = po + s_ * 2 * L
                    chunks.append(ch)
                po += nseg * 2 * L
            return chunks

        try:
          stage('setup')
          for blk_i in range(5):
              if blk_i == 0:
                  NT = NMETA + NS * TSL
                  groups = [(0, 1, NMETA, 'p'), (NMETA, NS, TSL, 's')]
                  x_to_fm(I['meta'], NMETA, 0)
                  x_to_fm(I['xs'], NS * TSL, NMETA)
              else:
                  NT = 512
                  groups = [(0, 8, 64, 'p')]
                  for tt_ in range(4):
                      r0 = (blk_i - 1) * 512 + tt_ * 128
                      x_to_fm(I['xp'][r0:r0 + 128, :], 128, tt_ * 128)
              chunks = mkchunks(groups)
              cgroups = [(0, 1, NT, 'p')] if blk_i > 0 else groups
              for l in range(DEPTH):
                  with ExitStack() as sc0:
                      rmsnorm_u(xT, NT, l, 'pre1', sc0)
                      if debug and l == 0 and blk_i == 0:
                          dump('uT', uT[:, :, 0:NT], [128, NCH, NT], UT_ALL)
                      P.barrier()
                      P.flush()
                  stage('norm1')
                  rwkv_phase(l, NT, cgroups, groups, chunks, blk_i)
                  stage('rwkv')
                  mlstm_phase(l, NT, cgroups, chunks, blk_i)
                  stage('mlstm')
                  tail_phase(l, NT)
                  stage('tail')
              if blk_i == 0:
                  x_from_fm(O['ys'], NS * TSL, NMETA)
              else:
                  for tt_ in range(4):
                      r0 = (blk_i - 1) * 512 + tt_ * 128
                      x_from_fm(O['yp'][r0:r0 + 128, :], 128, tt_ * 128)
              P.barrier()
              P.flush()

          for l in range(DEPTH):
              for hp in range(4):
                  ps, pk = pssmall()
                  TR(ps[:, 0:128], Sbd[l][hp][:], 128, [Sbd[l][hp]], [pk])
                  sg = stg[st['stg']]; st['stg'] ^= 1
                  CPalt(sg[:, 0:128], ps[:, 0:128], [pk], [sg])
                  for h2 in range(2):
                      P.dma('sp', O['pS'][l, hp * 2 + h2], sg[h2 * 64:(h2 + 1) * 64, h2 * 64:(h2 + 1) * 64], reads=[sg])
              from_fm(O['psh'][l], lambda j, l=l: shc[:, l, :], 14, 128, [shc])
              for h in range(4):
                  ps, pk = pssmall()
                  TR(ps[:, 0:128], CTp[l][:, h, 0:128], 128, [CTp[l]], [pk])
                  sg = stg[st['stg']]; st['stg'] ^= 1
                  CPalt(sg[:, 0:128], ps[:, 0:128], [pk], [sg])
                  P.dma('sp', O['pC'][l, h], sg[:, 0:128], reads=[sg])
                  P.dma('sp', O['pn'][l, h:h + 1, :].rearrange("a k -> k a"), CTp[l][:, h, 128:129], reads=[CTp[l]])
              P.dma('sp', O['pm'][l:l + 1, :], mP[0:1, l, :], reads=[('mP', h) for h in range(4)])
              sg = stg[st['stg']]; st['stg'] ^= 1
              for c in range(8):
                  ps, pk = pssmall()
                  TR(ps[0:3, 0:128], cvc[:, l, c, :], 128, [cvc], [pk])
                  CPalt(sg[0:3, c * 128:(c + 1) * 128], ps[0:3, 0:128], [pk], [sg])
              P.dma('sp', O['pcv'][l], sg[0:3, :], reads=[sg])
          stage('end')
        except StopBuild:
          pass
        P.barrier(engines=('pe', 'act', 'dve', 'pool', 'sp'), with_w=True)
        P.flush()
        print("instructions:", P.nins, "waits:", P.nwaits, {k: P.cnt[k] for k in ('pe', 'act', 'dve', 'pool')})
    return nc, dbg


def _consts():
    c = np.zeros((len(CONST_NAMES), 128, 128), np.float32)
    ix = {n: i for i, n in enumerate(CONST_NAMES)}
    c[ix['ident']] = np.eye(128, dtype=np.float32)
    c[ix['ones']] = 1.0
    c[ix['blk'], 0:64, 0:64] = 1.0
    c[ix['blk'], 64:128, 64:128] = 1.0
    for L in (64, 16, 8):
        s = np.arange(L)[:, None]
        t = np.arange(L)[None, :]
        lt = (s < t).astype(np.float32)
        le = (s <= t).astype(np.float32)
        for h in range(2):
            c[ix['mlt%d' % L], h * L:(h + 1) * L, h * L:(h + 1) * L] = lt
            c[ix['mgt%d' % L], h * L:(h + 1) * L, h * L:(h + 1) * L] = lt.T
            c[ix['mle%d' % L], h * L:(h + 1) * L, 0:L] = le
        c[ix['mneg%d' % L], 0:L, 0:L] = np.where(s <= t, 0.0, -30000.0)
    return np.ascontiguousarray(c.transpose(1, 0, 2).reshape(128, -1))


_CACHE = {}


def _prep_inputs(inp, n_cores=8):
    f = lambda a: np.ascontiguousarray(np.asarray(a, dtype=np.float32))
    prm = np.zeros((DEPTH, NPRM, 128), np.float32)
    for l in range(DEPTH):
        rows = [inp['pre1'][l], inp['post1'][l], inp['pre2'][l], inp['post2'][l], inp['rw_mu'][l], inp['rw_w0'][l], inp['rw_a0'][l],
                inp['rw_k_k'][l], inp['rw_k_a'][l], inp['rw_r_k'][l], inp['rw_gn_g'][l], inp['rw_gn_b'][l], inp['ml_conv_w'][l],
                inp['ml_conv_b'][l], inp['ml_gn_g'][l]]
        prm[l] = np.concatenate([np.asarray(r, np.float32).reshape(-1, 128) for r in rows], axis=0)
    gbias = np.concatenate([np.concatenate([np.asarray(inp['ml_i_bias'][l], np.float32), np.asarray(inp['ml_f_bias'][l], np.float32)])
                            for l in range(DEPTH)]).reshape(1, 16)
    shared = {'meta': f(inp['meta_tokens']), 'w_in': f(inp['w_in']), 'w_up': f(inp['rw_w_up']), 'a_up': f(inp['rw_a_up']),
              'g_up': f(inp['rw_g_up']), 'p_a': f(inp['p_a']), 'p_b': f(inp['p_b']), 'w_out': f(inp['w_out']),
              'w_ff_up': f(inp['w_ff_up']), 'w_ff_down': f(inp['w_ff_down']), 'prm': prm, 'gbias': gbias, 'consts': _consts()}
    maps = []
    xp = np.asarray(inp['x_prompt'], np.float32)
    xs = np.asarray(inp['x_sample'], np.float32)
    for i in range(n_cores):
        sl = slice(i * NS, (i + 1) * NS)
        m = dict(shared)
        m['xp'] = f(xp[i])
        m['xs'] = f(xs[sl].reshape(NS * TSL, D))
        m['sS'] = f(np.asarray(inp['state_rwkv_S'])[:, sl])
        m['ssh'] = f(np.asarray(inp['state_rwkv_shift'])[:, sl])
        m['sC'] = f(np.asarray(inp['state_mlstm_C'])[:, sl])
        m['sn'] = f(np.asarray(inp['state_mlstm_n'])[:, sl])
        m['sm'] = f(np.asarray(inp['state_mlstm_m'])[:, sl].reshape(DEPTH, NS * 4))
        m['scv'] = f(np.asarray(inp['state_mlstm_conv'])[:, sl].reshape(DEPTH, NS * 3, D))
        maps.append(m)
    return maps


def _gather(results, n_cores=8):
    R = results
    cat = lambda k, ax: np.concatenate([R[i][k] for i in range(n_cores)], axis=ax)
    y_prompt = np.stack([R[i]['yp'] for i in range(n_cores)])
    y_sample = np.concatenate([R[i]['ys'].reshape(NS, TSL, D) for i in range(n_cores)], axis=0)
    pS = np.stack([R[i]['pS'] for i in range(n_cores)], axis=1)
    psh = np.stack([R[i]['psh'].reshape(DEPTH, 1792) for i in range(n_cores)], axis=1)
    pC = np.stack([R[i]['pC'] for i in range(n_cores)], axis=1)
    pn = np.stack([R[i]['pn'] for i in range(n_cores)], axis=1)
    pm = np.stack([R[i]['pm'] for i in range(n_cores)], axis=1)
    pcv = np.stack([R[i]['pcv'] for i in range(n_cores)], axis=1)
    oS = cat('oS', 1)
    osh = cat('osh', 1)
    oC = cat('oC', 1)
    on = np.concatenate([R[i]['on'].reshape(DEPTH, NS, 4, 128) for i in range(n_cores)], axis=1)
    om = np.concatenate([R[i]['om'].reshape(DEPTH, NS, 4) for i in range(n_cores)], axis=1)
    ocv = np.concatenate([R[i]['ocv'].reshape(DEPTH, NS, 3, D) for i in range(n_cores)], axis=1)
    outs = (y_prompt, y_sample, pS, psh, pC, pn, pm, pcv, oS, osh, oC, on, om, ocv)
    return tuple(np.ascontiguousarray(o.astype(np.float32)) for o in outs)


def kernel(**inputs):
    if 'nc' not in _CACHE:
        _CACHE['nc'] = build(False)[0]
    nc = _CACHE['nc']
    maps = _prep_inputs(inputs)
    res = run_bass_kernel_spmd(nc, maps, core_ids=list(range(8)))
    return _gather(res.results)
```

```python
import numpy as np
from contextlib import ExitStack
import concourse.bass as bass
import concourse.mybir as mybir
from concourse.bass_utils import run_bass_kernel_spmd

F32 = mybir.dt.float32
BF16 = mybir.dt.bfloat16
AF = mybir.ActivationFunctionType
ALU = mybir.AluOpType

NDS = 40
NWS = 6
NGS = 8
RDT = BF16
F32R = mybir.dt.float32r
USE_F32R = True

D = 1024
NCH = 8
NIN = 5896
SEQ = 2048
NMETA = 16
DEPTH = 2
NS = 16
TSL = 8
SLABW = 1032
SLABS_IN = [(0, 1024), (1024, 1792), (1792, 2816), (2816, 3848), (3848, 4872), (4872, 5896)]
RMS_EPS = 1e-6
RW_GN_EPS = 64e-5
ML_GN_EPS = 1e-5
NPRM = 118
PR = {}
_o = 0
for _n, _c in [('pre1', 8), ('post1', 8), ('pre2', 8), ('post2', 8), ('mu', 14), ('w0', 4), ('a0', 4), ('kk', 4),
               ('ka', 4), ('rk', 4), ('gng', 4), ('gnb', 4), ('cw', 32), ('cb', 8), ('mlg', 4)]:
    PR[_n] = _o
    _o += _c
assert _o == NPRM
CONST_NAMES = ['ident', 'ones', 'blk', 'mlt64', 'mgt64', 'mle64', 'mneg64', 'mlt16', 'mgt16', 'mle16', 'mneg16',
               'mlt8', 'mgt8', 'mle8', 'mneg8']


class Prog:
    ENGS = ['pe', 'act', 'dve', 'pool', 'sp']

    def __init__(self, nc, block, es):
        self.nc = nc
        self.bfn = {'pe': block.tensor, 'act': block.scalar, 'dve': block.vector,
                    'pool': block.gpsimd, 'sp': block.sync}
        self.semh = {}
        for e in ['pe', 'act', 'dve', 'pool']:
            self.semh[e] = es.enter_context(nc.semaphore("sem_" + e))
        for i in range(NDS):
            self.semh['d%d' % i] = es.enter_context(nc.semaphore("dsem%d" % i))
        for i in range(NWS):
            self.semh['w%d' % i] = es.enter_context(nc.semaphore("wsem%d" % i))
        for i in range(NGS):
            self.semh['g%d' % i] = es.enter_context(nc.semaphore("gsem%d" % i))
        self.gnext = 0
        self.cnt = {k: 0 for k in self.semh}
        self.dnext = 0
        self.wnext = 0
        self.buf = {e: [] for e in self.ENGS}
        self.lastw = {}
        self.readers = {}
        self.seen = {e: {} for e in self.ENGS}
        self.nins = 0
        self.clock = {}
        self.tokseq = {}
        self.gseq = 0
        self.nwaits = 0

    KEYMAP = {}

    @staticmethod
    def _k(x):
        if isinstance(x, (str, tuple)):
            return x
        return Prog.KEYMAP.get(x.name, x.name)

    def _deps(self, eng, reads, writes):
        need = {}
        for k in reads:
            w = self.lastw.get(k)
            if w is not None and need.get(w[0], 0) < w[1]:
                need[w[0]] = w[1]
        for k in writes:
            w = self.lastw.get(k)
            if w is not None and need.get(w[0], 0) < w[1]:
                need[w[0]] = w[1]
            for sid, v in self.readers.get(k, {}).items():
                if need.get(sid, 0) < v:
                    need[sid] = v
        out = []
        seen = self.seen[eng]
        items = sorted(need.items(), key=lambda kv: -self.tokseq.get(kv, 0))
        for sid, v in items:
            if seen.get(sid, 0) >= v:
                continue
            seen[sid] = v
            out.append((sid, v))
            ck = self.clock.get((sid, v))
            if ck:
                for s2, v2 in ck.items():
                    if seen.get(s2, 0) < v2:
                        seen[s2] = v2
        out.reverse()
        return out

    def _commit(self, tok, reads, writes, eng=None):
        self.gseq += 1
        self.tokseq[tok] = self.gseq
        if eng is not None:
            self.clock[tok] = dict(self.seen[eng])
        for k in reads:
            d = self.readers.setdefault(k, {})
            if d.get(tok[0], 0) < tok[1]:
                d[tok[0]] = tok[1]
        for k in writes:
            self.lastw[k] = tok
            self.readers[k] = {}

    def op(self, eng, fn, reads=(), writes=(), attach=None, multi=False):
        reads = [self._k(x) for x in reads]
        writes = [self._k(x) for x in writes]
        waits = self._deps(eng, reads, writes)
        self.nwaits += len(waits)
        self.cnt[eng] += 1
        tok = (eng, self.cnt[eng])
        self._commit(tok, reads, writes, eng)
        semh = self.semh
        sem = semh[eng]
        if attach is None:
            attach = eng in ('act', 'dve', 'pe')

        def run(e):
            if attach and waits:
                for sid, v in waits[:-1]:
                    e.wait_ge(semh[sid], v)
                if multi:
                    done = [False]

                    def hook(ins):
                        if not done[0]:
                            ins._wait_ge(semh[waits[-1][0]], waits[-1][1])
                            done[0] = True
                        return ins
                    fn(e, hook).then_inc(sem, 1)
                else:
                    ins = fn(e)
                    ins._wait_ge(semh[waits[-1][0]], waits[-1][1])
                    ins.then_inc(sem, 1)
            elif multi:
                for sid, v in waits:
                    e.wait_ge(semh[sid], v)
                fn(e, lambda ins: ins).then_inc(sem, 1)
            else:
                for sid, v in waits:
                    e.wait_ge(semh[sid], v)
                fn(e).then_inc(sem, 1)
        self.buf[eng].append(run)
        self.nins += 1

    def dma(self, q, out, in_, reads=(), writes=(), slab=False, **kw):
        reads = [self._k(x) for x in reads]
        writes = [self._k(x) for x in writes]
        if slab:
            sid = 'w%d' % self.wnext
            self.wnext = (self.wnext + 1) % NWS
        elif q == 'pool':
            sid = 'g%d' % self.gnext
            self.gnext = (self.gnext + 1) % NGS
        else:
            sid = 'd%d' % self.dnext
            self.dnext = (self.dnext + 1) % NDS
        waits = self._deps(q, reads, writes)
        prev = self.cnt[sid]
        if prev > 0 and self.seen[q].get(sid, 0) < prev:
            self.seen[q][sid] = prev
            waits.append((sid, prev))
        self.cnt[sid] = prev + 16
        tok = (sid, prev + 16)
        self._commit(tok, reads, writes, q)
        semh = self.semh

        def run(e):
            for s, v in waits:
                e.wait_ge(semh[s], v)
            e.dma_start(out=out, in_=in_, **kw).then_inc(semh[sid], 16)
        self.buf[q].append(run)
        self.nins += 1

    def barrier(self, engines=('act', 'dve', 'sp'), with_w=False):
        for e in engines:
            waits = []
            for sid, v in self.cnt.items():
                if v == 0 or (sid.startswith('w') and not with_w):
                    continue
                if self.seen[e].get(sid, 0) >= v:
                    continue
                self.seen[e][sid] = v
                waits.append((sid, v))
            if waits:
                semh = self.semh

                def run(en, waits=waits):
                    for s, v in waits:
                        en.wait_ge(semh[s], v)
                self.buf[e].append(run)

    def flush(self):
        for e in self.ENGS:
            fns = self.buf[e]
            if not fns:
                continue
            self.buf[e] = []

            def body(en, fns=fns):
                for f in fns:
                    f(en)
            self.bfn[e](body)


class Chunk:
    pass


class StopBuild(Exception):
    pass


def build(debug=False, stop=None):
    nc = bass.Bass("TRN2", target_bir_lowering=False)
    stage_i = [0]
    I = {}

    def din(name, shape):
        I[name] = nc.dram_tensor(name, list(shape), F32, kind="ExternalInput").ap()
        return I[name]

    O = {}

    def dout(name, shape):
        O[name] = nc.dram_tensor(name, list(shape), F32, kind="ExternalOutput").ap()
        return O[name]

    din('xp', (SEQ, D)); din('xs', (NS * TSL, D))
    din('sS', (DEPTH, NS, 8, 64, 64)); din('ssh', (DEPTH, NS, 1792)); din('sC', (DEPTH, NS, 4, 128, 128))
    din('sn', (DEPTH, NS, 4, 128)); din('sm', (DEPTH, NS * 4)); din('scv', (DEPTH, NS * 3, D))
    din('meta', (NMETA, D)); din('w_in', (DEPTH, D, NIN)); din('w_up', (DEPTH, 64, 512)); din('a_up', (DEPTH, 64, 512))
    din('g_up', (DEPTH, 128, 512)); din('p_a', (DEPTH, 512, D)); din('p_b', (DEPTH, 512, D)); din('w_out', (DEPTH, D, D))
    din('w_ff_up', (DEPTH, D, 4 * D)); din('w_ff_down', (DEPTH, 4 * D, D))
    din('prm', (DEPTH, NPRM, 128)); din('gbias', (1, 16)); din('consts', (128, len(CONST_NAMES) * 128))

    dout('yp', (SEQ, D)); dout('ys', (NS * TSL, D))
    dout('pS', (DEPTH, 8, 64, 64)); dout('psh', (DEPTH, 14, 128)); dout('pC', (DEPTH, 4, 128, 128)); dout('pn', (DEPTH, 4, 128))
    dout('pm', (DEPTH, 4)); dout('pcv', (DEPTH, 3, D))
    dout('oS', (DEPTH, NS, 8, 64, 64)); dout('osh', (DEPTH, NS, 1792)); dout('oC', (DEPTH, NS, 4, 128, 128))
    dout('on', (DEPTH, NS * 4, 128)); dout('om', (DEPTH, NS * 4)); dout('ocv', (DEPTH, NS * 3, D))
    dbg = {}

    with ExitStack() as es:
        used = {}

        def T(name, shape, dt=F32, st=None):
            n = used.get(name, 0)
            used[name] = n + 1
            un = name if n == 0 else "%s_r%d" % (name, n)
            Prog.KEYMAP[un] = name
            return (st if st is not None else es).enter_context(nc.sbuf_tensor(un, list(shape), dt))

        cst = T('cst', [128, len(CONST_NAMES), 128])
        C = {n: cst[:, i, :] for i, n in enumerate(CONST_NAMES)}
        ones_bf = T('ones_bf', [128, 128], BF16)
        blk_bf = T('blk_bf', [128, 128], BF16)
        ident_bf = T('ident_bf', [128, 128], BF16)
        pc = T('pc', [128, DEPTH, 128])
        npc = T('npc', [128, DEPTH, 8])
        gb = T('gb', [128, 16])
        ngb = T('ngb', [128, 16])
        lw_wa = T('lw_wa', [128, DEPTH, 512], BF16)
        lw_g = T('lw_g', [128, DEPTH, 512], BF16)
        slabs = [T('slab%d' % i, [128, NCH, SLABW], BF16) for i in range(2)]
        xT = T('xT', [128, NCH, 512])
        uT = T('uT', [128, NCH, 512], BF16)
        yaT = T('yaT', [128, 4, 512], BF16)
        ybT = T('ybT', [128, 4, 512], BF16)
        Sbd = [[T('Sbd%d_%d' % (l, hp), [128, 128]) for hp in range(4)] for l in range(DEPTH)]
        CTp = [T('CT%d' % l, [128, 4, 132]) for l in range(DEPTH)]
        mP = T('mP', [128, DEPTH, 4])
        shc = T('shc', [128, DEPTH, 14])
        cvc = T('cvc', [128, DEPTH, 8, 3])
        stg = [T('stg%d' % i, [128, D]) for i in range(2)]
        pss = [es.enter_context(nc.psum_tensor('psb%d' % i, [128, 512], F32)) for i in range(8)]

        block = es.enter_context(nc.Block())
        P = Prog(nc, block, es)

        def stage(name):
            stage_i[0] += 1
            if stop is not None and stage_i[0] >= stop:
                print("STOP at stage", stage_i[0], name)
                P.barrier(engines=('pe', 'act', 'dve', 'pool', 'sp'), with_w=True)
                P.flush()
                raise StopBuild()

        st = {'big': 0, 'small': 0, 'slab': 0, 'stg': 0, 'alt': 0}

        def psbig():
            i = st['big']; st['big'] = (i + 1) % 8
            return pss[i], ('psB', i)

        def pssmall():
            return psbig()

        def kn(x):
            return Prog._k(x)

        def TT(eng, out, in0, in1, op, r, w):
            P.op(eng, lambda e: e.tensor_tensor(out=out, in0=in0, in1=in1, op=op), r, w)

        def TS(eng, out, in0, s1, op0, r, w, s2=None, op1=None):
            if op1 is None:
                P.op(eng, lambda e: e.tensor_scalar(out=out, in0=in0, scalar1=s1, scalar2=None, op0=op0), r, w)
            else:
                P.op(eng, lambda e: e.tensor_scalar(out=out, in0=in0, scalar1=s1, scalar2=s2, op0=op0, op1=op1), r, w)

        def STT(eng, out, in0, scalar, in1, op0, op1, r, w, accum_out=None):
            if accum_out is None:
                P.op(eng, lambda e: e.scalar_tensor_tensor(out=out, in0=in0, scalar=scalar, in1=in1, op0=op0, op1=op1), r, w)
            else:
                P.op(eng, lambda e: e.scalar_tensor_tensor(out=out, in0=in0, scalar=scalar, in1=in1, op0=op0, op1=op1,
                                                           accum_out=accum_out), r, w)

        def ACT(out, in_, func, r, w, bias=0.0, scale=1.0):
            P.op('act', lambda e: e.activation(out=out, in_=in_, func=func, bias=bias, scale=scale), r, w)

        def CP(eng, out, in_, r, w):
            if eng == 'act':
                P.op('act', lambda e: e.copy(out=out, in_=in_), r, w)
            else:
                P.op(eng, lambda e: e.tensor_copy(out=out, in_=in_), r, w)

        def CPalt(out, in_, r, w):
            st['alt'] = (st['alt'] + 1) % 4
            CP('act', out, in_, r, w)

        def MSET(eng, ap, val, w):
            P.op(eng, lambda e: e.memset(ap, val), (), w)

        def SCAN(out, d0, d1, init, op0, op1, r, w):
            P.op('dve', lambda e: e.tensor_tensor_scan(out=out, data0=d0, data1=d1, initial=init, op0=op0, op1=op1), r, w)

        def RECIP(out, in_, r, w):
            P.op('dve', lambda e: e.reciprocal(out=out, in_=in_), r, w)

        def MM(out, pairs, r, w):
            def fn(e, hook):
                n = len(pairs)
                ins = None
                for i, (l_, r_) in enumerate(pairs):
                    ins = hook(e.matmul(out, lhsT=l_, rhs=r_, start=(i == 0), stop=(i == n - 1)))
                return ins
            P.op('pe', fn, r, w, multi=True)

        def MMB(trip, r, w):
            def fn(e, hook):
                ins = None
                for (o_, l_, r_) in trip:
                    ins = hook(e.matmul(o_, lhsT=l_, rhs=r_, start=True, stop=True))
                return ins
            P.op('pe', fn, r, w, multi=True)

        def MMG(groups_, r, w):
            def fn(e, hook):
                ins = None
                for (o_, pairs) in groups_:
                    n = len(pairs)
                    for i, (l_, r_) in enumerate(pairs):
                        ins = hook(e.matmul(o_, lhsT=l_, rhs=r_, start=(i == 0), stop=(i == n - 1)))
                return ins
            P.op('pe', fn, r, w, multi=True)

        def TRB(pairs, r, w):
            def fn(e, hook):
                ins = None
                for (o_, i_) in pairs:
                    ins = hook(e.transpose(out=o_, in_=i_, identity=C['ident'][:, :]))
                return ins
            P.op('pe', fn, list(r) + ['cst'], w, multi=True)

        def TR(out, in_, n_in_part, r, w):
            P.op('pe', lambda e: e.transpose(out=out, in_=in_, identity=C['ident'][0:n_in_part, 0:n_in_part]), r + ['cst'], w)

        def dump(name, ap, shape, r):
            if not debug:
                return
            dbg[name] = dout('dbg_' + name, shape)
            P.dma('pool', dbg[name], ap, reads=r)

        def load_slab(src2d, r0, nrow_chunks, c0, c1):
            i = st['slab']; st['slab'] = (i + 1) % 2
            sl = slabs[i]
            src = src2d[r0:r0 + nrow_chunks * 128, c0:c1].rearrange("(c p) n -> p c n", p=128)
            P.dma('pool', sl[:, 0:nrow_chunks, 0:c1 - c0], src, writes=[sl], slab=True)
            return sl

        def to_fm(dst_fn, rows_ap, R, F, dst_keys):
            i = st['stg']; st['stg'] ^= 1
            sg = stg[i]
            P.dma('sp', sg[0:R, 0:F], rows_ap, writes=[sg])
            for j in range(F // 128):
                ps, pk = pssmall()
                TR(ps[:, 0:R], sg[0:R, j * 128:(j + 1) * 128], R, [sg], [pk])
                CPalt(dst_fn(j), ps[:, 0:R], [pk], dst_keys)

        def from_fm(rows_ap, src_fn, R, F, src_keys):
            i = st['stg']; st['stg'] ^= 1
            sg = stg[i]
            for j in range(F // 128):
                ps, pk = pssmall()
                TR(ps[0:R, 0:128], src_fn(j), 128, src_keys, [pk])
                CPalt(sg[0:R, j * 128:(j + 1) * 128], ps[0:R, 0:128], [pk], [sg])
            P.dma('sp', rows_ap, sg[0:R, 0:F], reads=[sg])

        def x_to_fm(rows_ap, R, t0):
            i = st['stg']; st['stg'] ^= 1
            sg = stg[i]
            P.dma('sp', sg[0:R, 0:D], rows_ap, writes=[sg])
            for j0 in (0, 4):
                ps, pk = psbig()
                P.op('pe', lambda e, hook, ps=ps, j0=j0: [hook(e.transpose(out=ps[:, k * R:(k + 1) * R], in_=sg[0:R, (j0 + k) * 128:(j0 + k + 1) * 128],
                                                                      identity=C['ident'][0:R, 0:R])) for k in range(4)][-1], [sg, 'cst'], [pk], multi=True)
                CPalt(xT[:, j0:j0 + 4, t0:t0 + R], ps[:, 0:4 * R].rearrange("p (k r) -> p k r", r=R), [pk], [xT])

        def x_from_fm(rows_ap, R, t0):
            i = st['stg']; st['stg'] ^= 1
            sg = stg[i]
            for j0 in (0, 4):
                ps, pk = psbig()
                P.op('pe', lambda e, hook, ps=ps, j0=j0: [hook(e.transpose(out=ps[0:R, k * 128:(k + 1) * 128], in_=xT[:, j0 + k, t0:t0 + R],
                                                                      identity=C['ident'][:, :])) for k in range(4)][-1], [xT, 'cst'], [pk], multi=True)
                CPalt(sg[0:R, j0 * 128:(j0 + 4) * 128], ps[0:R, 0:512], [pk], [sg])
            P.dma('sp', rows_ap, sg[0:R, 0:D], reads=[sg])

        P.dma('sp', cst[:].rearrange("p a b -> p (a b)"), I['consts'], writes=[cst])
        CP('dve', ones_bf[:], C['ones'], [cst], [ones_bf])
        CP('dve', blk_bf[:], C['blk'], [cst], [blk_bf])
        CP('dve', ident_bf[:], C['ident'], [cst], [ident_bf])
        for l in range(DEPTH):
            to_fm(lambda j, l=l: pc[:, l, 0:NPRM], I['prm'][l], NPRM, 128, [pc])
        P.dma('sp', gb[:], I['gbias'].partition_broadcast(128), writes=[gb])
        TS('dve', ngb[:], gb[:], -1.0, ALU.mult, [gb], [ngb])
        for l in range(DEPTH):
            TS('dve', npc[:, l, 0:8], pc[:, l, PR['w0']:PR['w0'] + 8], -1.0, ALU.mult, [pc], [npc])
            P.dma('pool', lw_wa[0:64, l, :], I['w_up'][l], writes=[lw_wa])
            P.dma('pool', lw_wa[64:128, l, :], I['a_up'][l], writes=[lw_wa])
            P.dma('pool', lw_g[:, l, :], I['g_up'][l], writes=[lw_g])
            for hp in range(4):
                MSET('dve', Sbd[l][hp][:], 0.0, [Sbd[l][hp]])
        MSET('dve', mP[:], 0.0, [('mP', h) for h in range(4)])
        for l in range(DEPTH):
            MSET('dve', CTp[l][:], 0.0, [CTp[l]])
        MSET('dve', shc[:], 0.0, [shc])
        MSET('dve', cvc[:], 0.0, [cvc])

        def pcol(l, name, j):
            return pc[:, l, PR[name] + j:PR[name] + j + 1]

        def rmsnorm_u(src, NT, l, gname, sc):
            sq = T('nsq', [128, NCH, NT], BF16, sc)
            rstd = T('nrstd', [128, NT], F32, sc)
            P.op('act', lambda e: e.activation(out=sq[:], in_=src[:, :, 0:NT], func=AF.Square), [src], [sq])
            ps, pk = psbig()
            MM(ps[:, 0:NT], [(ones_bf[:], sq[:, c, :]) for c in range(NCH)], [ones_bf, sq], [pk])
            ACT(rstd[:], ps[:, 0:NT], AF.Ln, [pk], [rstd], bias=RMS_EPS, scale=1.0 / D)
            ACT(rstd[:], rstd[:], AF.Exp, [rstd], [rstd], scale=-0.5)
            for c in range(NCH):
                STT('dve', uT[:, c, 0:NT], src[:, c, 0:NT], pcol(l, gname, c), rstd[:], ALU.mult, ALU.mult,
                    [src, rstd, pc], [('uT', c)])

        def rmsnorm_add(acc, NT, l, gname, sc):
            sq = T('asq', [128, NCH, NT], BF16, sc)
            rstd = T('arstd', [128, NT], F32, sc)
            P.op('act', lambda e: e.activation(out=sq[:], in_=acc[:, :, 0:NT], func=AF.Square), [acc], [sq])
            ps, pk = psbig()
            MM(ps[:, 0:NT], [(ones_bf[:], sq[:, c, :]) for c in range(NCH)], [ones_bf, sq], [pk])
            ACT(rstd[:], ps[:, 0:NT], AF.Ln, [pk], [rstd], bias=RMS_EPS, scale=1.0 / D)
            ACT(rstd[:], rstd[:], AF.Exp, [rstd], [rstd], scale=-0.5)
            for c in range(NCH):
                STT('dve', acc[:, c, 0:NT], acc[:, c, 0:NT], pcol(l, gname, c), rstd[:], ALU.mult, ALU.mult,
                    [acc, rstd, pc], [acc])
            P.op('dve', lambda e: e.tensor_tensor(out=xT[:, :, 0:NT], in0=xT[:, :, 0:NT], in1=acc[:, :, 0:NT], op=ALU.add),
                 [acc, xT], [xT])

        UT_ALL = [('uT', c) for c in range(NCH)]

        def dense_tile(sl, col0, ncols, NT, nch=NCH, rhs=None, rkeys=None):
            ps, pk = psbig()
            if rhs is None:
                rhs = uT; rkeys = UT_ALL
            MM(ps[0:ncols, 0:NT], [(sl[:, c, col0:col0 + ncols], rhs[:, c, 0:NT]) for c in range(nch)], [sl] + list(rkeys), [pk])
            return ps, pk

        def rwkv_phase(l, NT, sgroups, groups, chunks, blk_i):
            with ExitStack() as sc:
                zs = T('zs', [128, 12, NT], F32, sc)
                lora = T('lora', [128, NT], BF16, sc)
                sgl = T('sgl', [128, NT], BF16, sc)
                zraw = [T('zraw0', [128, NT + 32], F32, sc)] * 2
                dtmp = [T('dtmp0', [128, NT], F32, sc)] * 2
                if blk_i == 0:
                    shin = T('shin', [128, 14, NS], F32, sc)
                    shout = T('shout', [128, 14, NS], F32, sc)
                    to_fm(lambda j: shin[:, j, :], I['ssh'][l][:, 0:1024], NS, 1024, [shin])
                    to_fm(lambda j: shin[:, 8 + j, :], I['ssh'][l][:, 1024:1792], NS, 768, [shin])
                    Sst = T('Sst', [128, 32, 128], F32, sc)
                    Sbs = T('Sbs', [128, NS * 4, 128], F32, sc)
                    MSET('dve', Sst[:], 0.0, [('Sst', i) for i in range(8)])
                    src = I['sS'][l].rearrange("s (hp h2) v k -> h2 v s hp k", h2=2)
                    Sst4 = Sst[:].rearrange("p (s hp) k -> p s hp k", hp=4)
                    for half in range(2):
                        for h2 in range(2):
                            P.dma('sp', Sst4[h2 * 64:(h2 + 1) * 64, :, :, h2 * 64:(h2 + 1) * 64], src[h2][:, half * 8:(half + 1) * 8],
                                  writes=[('Sst', s8) for s8 in range(8)])
                        for s8 in range(8):
                            u0 = (half * 8 + s8) * 4
                            ps, pk = psbig()
                            TRB([(ps[:, i * 128:(i + 1) * 128], Sst[:, s8 * 4 + i, :]) for i in range(4)], [('Sst', s8)], [pk])
                            CPalt(Sbs[:, u0:u0 + 4, :], ps[:, 0:512].rearrange("p (u k) -> p u k", k=128), [pk], [('Sbs', u0 + i) for i in range(4)])
                slA = load_slab(I['w_in'][l], 0, NCH, SLABS_IN[0][0], SLABS_IN[0][1])
                slB = load_slab(I['w_in'][l], 0, NCH, SLABS_IN[1][0], SLABS_IN[1][1])
                for j in range(14):
                    sl = slA if j < 8 else slB
                    ps, pk = dense_tile(sl, (j % 8 if j < 8 else j - 8) * 128, 128, NT)
                    zr = zraw[j % 2]; dt_ = dtmp[j % 2]
                    off_r = 0
                    for (off, nseg, L, kind) in sgroups:
                        zv = zr[:, off_r:off_r + nseg * (L + 1)].rearrange("p (s t) -> p s t", t=L + 1)
                        pv = ps[:, off:off + nseg * L].rearrange("p (s t) -> p s t", t=L)
                        CP('act', zv[:, :, 1:L + 1], pv, [pk], [zr])
                        if kind == 'p':
                            CP('dve', zv[:, 0, 0:1], shc[:, l, j:j + 1], [shc], [zr])
                        else:
                            CP('dve', zv[:, :, 0], shin[:, j, :], [shin], [zr])
                        dv = dt_[:, off:off + nseg * L].rearrange("p (s t) -> p s t", t=L)
                        TT('dve', dv, zv[:, :, 0:L], pv, ALU.subtract, [zr, pk], [dt_])
                        if kind == 'p':
                            CP('dve', shc[:, l, j:j + 1], zv[:, 0, L:L + 1], [zr], [shc])
                        else:
                            CP('dve', shout[:, j, :], zv[:, :, L], [zr], [shout])
                        off_r += nseg * (L + 1)
                    if j < 12:
                        STT('dve', zs[:, j, :], dt_[:, 0:NT], pcol(l, 'mu', j), ps[:, 0:NT], ALU.mult, ALU.add,
                            [dt_, pk, pc], [('zs', j)])
                    elif j == 12:
                        tmpz = dtmp[j % 2]
                        STT('dve', tmpz[:, 0:NT], dt_[:, 0:NT], pcol(l, 'mu', j), ps[:, 0:NT], ALU.mult, ALU.add,
                            [dt_, pk, pc], [dt_])
                        ACT(lora[0:64, :], tmpz[0:64, 0:NT], AF.Tanh, [dt_], [lora])
                        CP('act', lora[64:128, :], tmpz[64:128, 0:NT], [dt_], [lora])
                    else:
                        tmpz = dtmp[j % 2]
                        STT('dve', tmpz[:, 0:NT], dt_[:, 0:NT], pcol(l, 'mu', j), ps[:, 0:NT], ALU.mult, ALU.add,
                            [dt_, pk, pc], [dt_])
                        ACT(sgl[:], tmpz[:, 0:NT], AF.Sigmoid, [dt_], [sgl])
                if blk_i == 0:
                    from_fm(O['osh'][l][:, 0:1024], lambda j: shout[:, j, :], NS, 1024, [shout])
                    from_fm(O['osh'][l][:, 1024:1792], lambda j: shout[:, 8 + j, :], NS, 768, [shout])
                if debug and l == 0 and blk_i == 0:
                    dump('zs', zs[:].rearrange("p a b -> p (a b)"), [128, 12 * NT], [('zs', j) for j in range(12)])

                a_t = T('a_t', [128, NT], F32, sc); kap = T('kap', [128, NT], F32, sc)
                kmod = T('kmod', [128, NT], F32, sc); b_t = T('b_t', [128, NT], F32, sc); ew = T('ew', [128, NT], F32, sc)
                cl = T('cl', [128, NT], F32, sc); E3 = a_t
                sqk = T('sqk', [128, NT], BF16, sc)
                tmpa = dtmp[0]
                npad = sum(nseg * 2 * L for (off, nseg, L, kind) in groups)
                E1p = [T('E1_%d' % i, [128, NT], F32, sc) for i in range(2)]
                E2p = [T('E2_%d' % i, [128, NT], F32, sc) for i in range(2)]
                rtp = [T('rt_%d' % i, [128, NT], F32, sc) for i in range(2)]
                bonp = [T('bon_%d' % i, [128, NT], F32, sc) for i in range(2)]
                g_tp = [T('g_t_%d' % i, [128, NT], F32, sc) for i in range(2)]
                rtbp = [T('rtb_%d' % i, [128, NT], RDT, sc) for i in range(2)]
                padp = [{n: T('pad%s_%d' % (n, i), [128, npad], RDT, sc) for n in 'kbqv'} for i in range(2)]
                for i in range(2):
                    for n in 'kbqv':
                        MSET('dve', padp[i][n][:], 0.0, [padp[i][n]])
                RW = [128, 512]
                wk = {n: T('w' + n, RW, RDT, sc) for n in ['bh', 'kh', 'PT', 'Vbd', 'Bbd', 'Kbd', 'Kh', 'U0', 'Ktm', 'PVs', 'Zb']}
                for n in ['A0', 'A0T', 'A1', 'A1T', 'Z']:
                    wk[n] = T('w' + n, RW, F32, sc)
                wk['QbT'] = T('wQbT', [128, 256], RDT, sc)
                wk['QkT'] = T('wQkT', [128, 256], RDT, sc)
                wk['gI'] = T('wgI', RW, F32, sc)
                kps = [{'Mx': T('kMx0', RW, F32, sc), 'D0': T('kD00', RW, F32, sc),
                        'Rh': T('kRh0', [128, 256], F32, sc), 'Y0': T('kY00', [128, 256], F32, sc)}] * 2
                batches = []
                cur = []
                for ch in chunks:
                    if cur and (cur[0].L != ch.L or len(cur) >= 4):
                        batches.append(cur); cur = []
                    cur.append(ch)
                if cur:
                    batches.append(cur)
                ui = [0]
                bi = [0]
                def pre_gen(hp):
                    E1 = E1p[hp % 2]; E2 = E2p[hp % 2]; rt = rtp[hp % 2]; bon = bonp[hp % 2]; g_t = g_tp[hp % 2]; rtb = rtbp[hp % 2]
                    padk = padp[hp % 2]['k']; padb = padp[hp % 2]['b']; padq = padp[hp % 2]['q']; padv = padp[hp % 2]['v']
                    kk_t = bon
                    r_ = zs[:, hp, :]; k_ = zs[:, 4 + hp, :]; v_ = zs[:, 8 + hp, :]
                    rk_, kk_, vk_ = ('zs', hp), ('zs', 4 + hp), ('zs', 8 + hp)
                    hc = slice(hp * 128, (hp + 1) * 128)
                    ps, pk = psbig()
                    MM(ps[:, 0:NT], [(lw_wa[0:64, l, hc], lora[0:64, :])], [lw_wa, lora], [pk])
                    ACT(ew[:], ps[:, 0:NT], AF.Exp, [pk, npc], [ew], bias=npc[:, l, hp:hp + 1], scale=-1.0)
                    ACT(ew[:], ew[:], AF.Ln, [ew], [ew], bias=1.0)
                    ACT(ew[:], ew[:], AF.Exp, [ew], [ew], bias=-0.5, scale=-1.0)
                    yield
                    ps, pk = psbig()
                    MM(ps[:, 0:NT], [(lw_wa[64:128, l, hc], lora[64:128, :])], [lw_wa, lora], [pk])
                    ACT(a_t[:], ps[:, 0:NT], AF.Exp, [pk, npc], [a_t], bias=npc[:, l, 4 + hp:5 + hp], scale=-1.0)
                    ACT(a_t[:], a_t[:], AF.Ln, [a_t], [a_t], bias=1.0)
                    ACT(a_t[:], a_t[:], AF.Exp, [a_t], [a_t], scale=-1.0)
                    ps, pk = psbig()
                    MM(ps[:, 0:NT], [(lw_g[:, l, hc], sgl[:])], [lw_g, sgl], [pk])
                    CP('act', g_t[:], ps[:, 0:NT], [pk], [g_t])
                    yield
                    TS('dve', kk_t[:], k_, pcol(l, 'kk', hp), ALU.mult, [kk_, pc], [kk_t])
                    ACT(sqk[:], kk_t[:], AF.Square, [kk_t], [sqk])
                    ps, pk = psbig()
                    MM(ps[:, 0:NT], [(blk_bf[:], sqk[:])], [blk_bf, sqk], [pk])
                    yield
                    TS('dve', tmpa[:], ps[:, 0:NT], 1e-18, ALU.max, [pk], [tmpa])
                    ACT(tmpa[:], tmpa[:], AF.Ln, [tmpa], [tmpa])
                    ACT(tmpa[:], tmpa[:], AF.Exp, [tmpa], [tmpa], scale=-0.5)
                    TT('dve', kap[:], kk_t[:], tmpa[:], ALU.mult, [kk_t, tmpa], [kap])
                    yield
                    TS('dve', tmpa[:], a_t[:], -1.0, ALU.add, [a_t, pc], [tmpa], s2=pcol(l, 'ka', hp), op1=ALU.mult)
                    STT('dve', kmod[:], tmpa[:], 1.0, k_, ALU.add, ALU.mult, [tmpa, kk_], [kmod])
                    TT('dve', b_t[:], kap[:], a_t[:], ALU.mult, [kap, a_t], [b_t])
                    yield
                    STT('dve', tmpa[:], r_, pcol(l, 'rk', hp), kmod[:], ALU.mult, ALU.mult, [rk_, pc, kmod], [tmpa])
                    ps, pk = psbig()
                    MM(ps[:, 0:NT], [(C['blk'], tmpa[:])], ['cst', tmpa], [pk])
                    TT('dve', bon[:], ps[:, 0:NT], v_, ALU.mult, [pk, vk_], [bon])
                    yield
                    for ch in chunks:
                        SCAN(cl[:, ch.off:ch.off + ch.L], C['ones'][:, 0:ch.L], ew[:, ch.off:ch.off + ch.L], 0.0, ALU.mult, ALU.subtract,
                             ['cst', ew], [cl])
                    yield
                    ACT(E1[:], cl[:], AF.Exp, [cl], [E1])
                    TT('dve', tmpa[:], cl[:], ew[:], ALU.add, [cl, ew], [tmpa])
                    ACT(E2[:], tmpa[:], AF.Exp, [tmpa], [E2])
                    ACT(E3[:], cl[:], AF.Exp, [cl], [E3], scale=-1.0)
                    yield
                    TT('dve', rt[:], r_, E1[:], ALU.mult, [rk_, E1], [rt])
                    CP('act', rtb[:], rt[:], [rt], [rtb])
                    yield
                    po = 0
                    for (off, nseg, L, kind) in groups:
                        for (pd, src, sk, E_, ek) in [(padk, kap[:], kap, E2, E2), (padb, b_t[:], b_t, E3, E3),
                                                       (padq, kmod[:], kmod, E3, E3), (padv, v_, vk_, None, None)]:
                            pv = pd[:, po:po + nseg * 2 * L].rearrange("p (s t) -> p s t", t=2 * L)
                            for h2 in range(2):
                                prt = slice(h2 * 64, (h2 + 1) * 64)
                                sv = src[prt, off:off + nseg * L].rearrange("p (s t) -> p s t", t=L)
                                if E_ is None:
                                    CP('act', pv[prt, :, h2 * L:(h2 + 1) * L], sv, [sk], [pd])
                                else:
                                    ev = E_[prt, off:off + nseg * L].rearrange("p (s t) -> p s t", t=L)
                                    TT('dve', pv[prt, :, h2 * L:(h2 + 1) * L], sv, ev, ALU.mult, [sk, ek], [pd])
                            yield
                        po += nseg * 2 * L
                def stage(hp, tick):
                    E1 = E1p[hp % 2]; yT = E2p[hp % 2]; rt = rtp[hp % 2]; rtb = rtbp[hp % 2]
                    padk = padp[hp % 2]['k']; padb = padp[hp % 2]['b']; padq = padp[hp % 2]['q']; padv = padp[hp % 2]['v']
                    pending = []
                    IDT = ident_bf if RDT == BF16 else C['ident']
                    IDK = kn(ident_bf) if RDT == BF16 else 'cst'
                    for bt_ in batches:
                        L = bt_[0].L; L2 = 2 * L; nb = len(bt_); po0 = bt_[0].padoff; o0 = bt_[0].off
                        Wd = nb * L2; Wl = nb * L; W8 = nb * 128
                        kp_ = kps[bi[0] % 2]; bi[0] += 1
                        J = {64: 5, 16: 3, 8: 2}[L]

                        def v3(ap, w):
                            return ap.rearrange("p (n t) -> p n t", t=w)
                        mlt = C['mlt%d' % L][0:L2, 0:L2].unsqueeze(1).to_broadcast([L2, nb, L2])
                        mgt = C['mgt%d' % L][0:L2, 0:L2].unsqueeze(1).to_broadcast([L2, nb, L2])
                        mle = C['mle%d' % L][0:L2, 0:L].unsqueeze(1).to_broadcast([L2, nb, L])
                        idb = C['ident'][0:L2, 0:L2].unsqueeze(1).to_broadcast([L2, nb, L2])
                        E1L = v3(E1[:, o0:o0 + Wl], L)[:, :, L - 1:L]
                        TT('dve', v3(wk['bh'][:, 0:Wd], L2), v3(padb[:, po0:po0 + Wd], L2), E1L.to_broadcast([128, nb, L2]), ALU.mult, [padb, E1], [wk['bh']])
                        TT('dve', v3(wk['kh'][:, 0:Wd], L2), v3(padq[:, po0:po0 + Wd], L2), E1L.to_broadcast([128, nb, L2]), ALU.mult, [padq, E1], [wk['kh']])
                        TT('dve', v3(wk['gI'][:, 0:W8], 128), C['ident'].unsqueeze(1).to_broadcast([128, nb, 128]), E1L.to_broadcast([128, nb, 128]),
                           ALU.mult, ['cst', E1], [wk['gI']])

                        def sl2(t, i):
                            return t[:, po0 + i * L2:po0 + (i + 1) * L2]

                        def bl(t, i):
                            return t[0:L2, i * L2:(i + 1) * L2]

                        def b8(t, i):
                            return t[0:L2, i * 128:(i + 1) * 128]
                        RB = range(nb)

                        def fr(ap):
                            return ap.bitcast(F32R) if USE_F32R else ap
                        ps, pk = psbig()
                        MMB([(bl(ps, i), sl2(padb, i), sl2(padk, i)) for i in RB], [padb, padk], [pk])
                        STT('dve', fr(v3(wk['A0T'][0:L2, 0:Wd], L2)), v3(ps[0:L2, 0:Wd], L2), -1.0, mlt, ALU.mult, ALU.mult, [pk, 'cst'], [wk['A0T']])
                        ps, pk = psbig()
                        MMB([(bl(ps, i), sl2(padk, i), sl2(padb, i)) for i in RB], [padb, padk], [pk])
                        STT('dve', fr(v3(wk['A0'][0:L2, 0:Wd], L2)), v3(ps[0:L2, 0:Wd], L2), -1.0, mgt, ALU.mult, ALU.mult, [pk, 'cst'], [wk['A0']])
                        ps, pk = psbig()
                        MMB([(bl(ps, i), sl2(padq, i), sl2(padk, i)) for i in RB], [padq, padk], [pk])
                        TT('dve', v3(wk['PT'][0:L2, 0:Wd], L2), v3(ps[0:L2, 0:Wd], L2), mlt, ALU.mult, [pk, 'cst'], [wk['PT']])
                        ps, pk = psbig()
                        MMB([(ps[0:L2, i * L:(i + 1) * L], sl2(padb, i), rtb[:, o0 + i * L:o0 + (i + 1) * L]) for i in RB], [padb, rtb], [pk])
                        TT('dve', v3(wk['QbT'][0:L2, 0:Wl], L), v3(ps[0:L2, 0:Wl], L), mle, ALU.mult, [pk, 'cst'], [wk['QbT']])
                        ps, pk = psbig()
                        MMB([(ps[0:L2, i * L:(i + 1) * L], sl2(padq, i), rtb[:, o0 + i * L:o0 + (i + 1) * L]) for i in RB], [padq, rtb], [pk])
                        TT('dve', v3(wk['QkT'][0:L2, 0:Wl], L), v3(ps[0:L2, 0:Wl], L), mle, ALU.mult, [pk, 'cst'], [wk['QkT']])
                        tick()
                        for (dst, srct, off_) in [(wk['Vbd'], padv, po0), (wk['Bbd'], wk['bh'], 0), (wk['Kbd'], wk['kh'], 0), (wk['Ktm'], padk, po0)]:
                            ps, pk = psbig()
                            MMB([(b8(ps, i), srct[:, off_ + i * L2:off_ + (i + 1) * L2], IDT[:, :]) for i in RB], [srct, IDK], [pk])
                            CPalt(dst[0:L2, 0:W8], ps[0:L2, 0:W8], [pk], [dst])
                        ps, pk = psbig()
                        MMB([(b8(ps, i), bl(wk['PT'], i), b8(wk['Vbd'], i)) for i in RB], [wk['PT'], wk['Vbd']], [pk])
                        CPalt(wk['PVs'][0:L2, 0:W8], ps[0:L2, 0:W8], [pk], [wk['PVs']])
                        tick()
                        Z = wk['Z']
                        TT('dve', fr(v3(Z[0:L2, 0:Wd], L2)), v3(wk['A0T'][0:L2, 0:Wd], L2), idb, ALU.add, [wk['A0T'], 'cst'], [Z])
                        Ap, ApT = wk['A0'], wk['A0T']
                        An, AnT = wk['A1'], wk['A1T']
                        for jj in range(1, J + 1):
                            ps, pk = psbig()
                            MMB([(bl(ps, i), fr(bl(ApT, i)), fr(bl(Ap, i))) for i in RB], [Ap, ApT], [pk])
                            CP('act', fr(An[0:L2, 0:Wd]), ps[0:L2, 0:Wd], [pk], [An])
                            if jj < J:
                                ps, pk = psbig()
                                MMB([(bl(ps, i), fr(bl(Ap, i)), fr(bl(ApT, i))) for i in RB], [Ap, ApT], [pk])
                                CP('act', fr(AnT[0:L2, 0:Wd]), ps[0:L2, 0:Wd], [pk], [AnT])
                            ps, pk = psbig()
                            MMB([(bl(ps, i), fr(bl(An, i)), fr(bl(Z, i))) for i in RB], [An, Z], [pk])
                            TT('dve', fr(Z[0:L2, 0:Wd]), Z[0:L2, 0:Wd], ps[0:L2, 0:Wd], ALU.add, [Z, pk], [Z])
                            Ap, ApT, An, AnT = An, AnT, Ap, ApT
                            if pending:
                                pending.pop(0)()
                            tick()
                        tick()
                        Zb = wk['Zb']
                        CP('act', Zb[0:L2, 0:Wd], Z[0:L2, 0:Wd], [Z], [Zb])
                        ps, pk = psbig()
                        MMB([(b8(ps, i), bl(Zb, i), b8(wk['Ktm'], i)) for i in RB], [Zb, wk['Ktm']], [pk])
                        CP('act', wk['Kh'][0:L2, 0:W8], ps[0:L2, 0:W8], [pk], [wk['Kh']])
                        ps, pk = psbig()
                        MMB([(b8(ps, i), bl(Zb, i), b8(wk['PVs'], i)) for i in RB], [Zb, wk['PVs']], [pk])
                        P.op('act', lambda e, o_=wk['U0'][0:L2, 0:W8], i_=ps[0:L2, 0:W8]: e.mul(out=o_, in_=i_, mul=-1.0), [pk], [kn(wk['U0'])])
                        tick()
                        while pending:
                            pending.pop(0)()
                        ps, pk = psbig()
                        MMB([(ps[:, i * L:(i + 1) * L], b8(wk['Kh'], i), wk['QbT'][0:L2, i * L:(i + 1) * L]) for i in RB], [wk['Kh'], wk['QbT']], [pk])
                        TT('dve', kp_['Rh'][:, 0:Wl], rt[:, o0:o0 + Wl], ps[:, 0:Wl], ALU.subtract, [rt, pk], [kp_['Rh']])
                        ps, pk = psbig()
                        MMG([(ps[:, i * L:(i + 1) * L], [(b8(wk['U0'], i), wk['QbT'][0:L2, i * L:(i + 1) * L]),
                                                        (b8(wk['Vbd'], i), wk['QkT'][0:L2, i * L:(i + 1) * L])]) for i in RB],
                            [wk['U0'], wk['QbT'], wk['Vbd'], wk['QkT']], [pk])
                        CP('act', kp_['Y0'][:, 0:Wl], ps[:, 0:Wl], [pk], [kp_['Y0']])
                        ps, pk = psbig()
                        MMG([(ps[:, i * 128:(i + 1) * 128], [(b8(wk['Bbd'], i), b8(wk['U0'], i)), (b8(wk['Kbd'], i), b8(wk['Vbd'], i))]) for i in RB],
                            [wk['Bbd'], wk['U0'], wk['Kbd'], wk['Vbd']], [pk])
                        CP('act', kp_['D0'][:, 0:W8], ps[:, 0:W8], [pk], [kp_['D0']])
                        ps, pk = psbig()
                        MMB([(ps[:, i * 128:(i + 1) * 128], b8(wk['Kh'], i), b8(wk['Bbd'], i)) for i in RB], [wk['Kh'], wk['Bbd']], [pk])
                        TT('dve', kp_['Mx'][:, 0:W8], wk['gI'][:, 0:W8], ps[:, 0:W8], ALU.subtract, [wk['gI'], pk], [kp_['Mx']])
                        while pending:
                            pending.pop(0)()

                        def step2(i, ch, kp_=kp_, L=L):
                            o = ch.off
                            if ch.kind == 'p':
                                S_ap = Sbd[l][hp][:]; S_key = kn(Sbd[l][hp])
                            else:
                                S_ap = Sbs[:, ch.seq * 4 + hp, :]; S_key = ('Sbs', ch.seq * 4 + hp)
                            psY, pkY = psbig()
                            MM(psY[:, 0:L], [(S_ap, kp_['Rh'][:, i * L:(i + 1) * L])], [S_key, kp_['Rh']], [pkY])
                            psS, pkS = psbig()
                            MM(psS[:, 0:128], [(kp_['Mx'][:, i * 128:(i + 1) * 128], S_ap)], [S_key, kp_['Mx']], [pkS])
                            TT('dve', S_ap, psS[:, 0:128], kp_['D0'][:, i * 128:(i + 1) * 128], ALU.add, [pkS, kp_['D0']], [S_key])
                            TT('dve', yT[:, o:o + L], psY[:, 0:L], kp_['Y0'][:, i * L:(i + 1) * L], ALU.add, [pkY, kp_['Y0']], [yT])
                        for i, ch in enumerate(bt_):
                            pending.append(lambda i=i, ch=ch, f=step2: f(i, ch))
                    while pending:
                        pending.pop(0)()
                def post(hp):
                    yT = E2p[hp % 2]; bon = bonp[hp % 2]; g_t = g_tp[hp % 2]
                    ps, pk = psbig()
                    MM(ps[:, 0:NT], [(C['blk'], yT[:])], ['cst', yT], [pk])
                    STT('dve', yT[:], ps[:, 0:NT], -1.0 / 64, yT[:], ALU.mult, ALU.add, [pk, yT], [yT])
                    ACT(tmpa[:], yT[:], AF.Square, [yT], [tmpa])
                    ps, pk = psbig()
                    MM(ps[:, 0:NT], [(C['blk'], tmpa[:])], ['cst', tmpa], [pk])
                    ACT(tmpa[:], ps[:, 0:NT], AF.Ln, [pk], [tmpa], bias=RW_GN_EPS, scale=1.0 / 64)
                    ACT(tmpa[:], tmpa[:], AF.Exp, [tmpa], [tmpa], scale=-0.5)
                    TT('dve', yT[:], yT[:], tmpa[:], ALU.mult, [yT, tmpa], [yT])
                    STT('dve', yT[:], yT[:], pcol(l, 'gng', hp), bon[:], ALU.mult, ALU.add, [yT, pc, bon], [yT])
                    STT('dve', yaT[:, hp, 0:NT], yT[:], pcol(l, 'gnb', hp), g_t[:], ALU.add, ALU.mult, [yT, pc, g_t], [('yaT', hp)])
                def advance(g, n=1):
                    if g is None:
                        return
                    for _ in range(n):
                        try:
                            next(g)
                        except StopIteration:
                            return
                for _ in pre_gen(0):
                    pass
                for hp in range(4):
                    nxt = pre_gen(hp + 1) if hp < 3 else None
                    stage(hp, lambda: advance(nxt, 1))
                    if nxt is not None:
                        for _ in nxt:
                            pass
                    post(hp)
                if blk_i == 0:
                    dst = O['oS'][l].rearrange("s (hp h2) v k -> h2 v s hp k", h2=2)
                    for half in range(2):
                        for s8 in range(8):
                            s_ = half * 8 + s8; u0 = s_ * 4
                            ps, pk = psbig()
                            TRB([(ps[:, i * 128:(i + 1) * 128], Sbs[:, u0 + i, :]) for i in range(4)], [('Sbs', u0 + i) for i in range(4)], [pk])
                            CPalt(Sst[:, s8 * 4:s8 * 4 + 4, :], ps[:, 0:512].rearrange("p (u k) -> p u k", k=128), [pk], [('Sst', s8)])
                        for h2 in range(2):
                            P.dma('sp', dst[h2][:, half * 8:(half + 1) * 8], Sst4[h2 * 64:(h2 + 1) * 64, :, :, h2 * 64:(h2 + 1) * 64],
                                  reads=[('Sst', s8) for s8 in range(8)])
                if debug and l == 0 and blk_i == 0:
                    dump('yaT', yaT[:, :, 0:NT], [128, 4, NT], [('yaT', h) for h in range(4)])
                P.barrier()
                P.flush()

        def mlstm_phase(l, NT, groups, chunks, blk_i):
            with ExitStack() as sc:
                nck = len(chunks)
                npd = sum(nseg * (L + 3) for (off, nseg, L, kind) in groups)
                qkp = T('qkp', [128, 8, npd], F32, sc)
                cacc = T('cacc', [128, NT], F32, sc)
                qb = T('qb', [128, 4, NT], BF16, sc); kb = T('kb', [128, 4, NT], BF16, sc); kf = T('kf', [128, 4, NT], F32, sc)
                sgo = T('sgo', [128, 4, NT], BF16, sc)
                vtm = T('vtm', [128, nck, 4, 132], BF16, sc)
                gsb = T('gsb', [8, NT], F32, sc)
                h4 = T('h4', [128, 4, NT], F32, sc)
                igb = T('igb', [128, NT], F32, sc); sp_ = T('sp_', [128, NT], F32, sc)
                cc4 = T('cc4', [128, 4, NT], F32, sc); G4 = T('G4', [128, 4, NT], F32, sc); em4 = T('em4', [128, 4, NT], F32, sc)
                sc4 = T('sc4', [128, 4, NT], F32, sc); tmpb = T('tmpb', [128, NT], F32, sc)
                if blk_i == 0:
                    cvin = T('cvin', [128, 8, NS * 3], F32, sc); cvout = T('cvout', [128, 8, NS * 3], F32, sc)
                    to_fm(lambda j: cvin[:, j, :], I['scv'][l], NS * 3, D, [cvin])
                    Cst = T('Cst', [128, 32, 128], F32, sc)
                    CTs = T('CTs', [128, NS * 4, 132], F32, sc)
                    m_s = T('m_s', [128, NS * 4], F32, sc); mo_s = T('mo_s', [128, NS * 4], F32, sc)
                    nst = T('nst', [128, NS * 4], F32, sc)
                    CTbC = Cst[:].bitcast(BF16).rearrange("p a (b c) -> p (a b) c", b=2)
                    CTbn = T('CTbn', [128, NS * 4], BF16, sc)
                    for g4 in range(4):
                        g2 = g4 % 2
                        P.dma('sp', Cst[:, g2 * 16:(g2 + 1) * 16, :].rearrange("p (s h) k -> p s h k", h=4),
                              I['sC'][l, g4 * 4:(g4 + 1) * 4].rearrange("s h v k -> v s h k"), writes=[('Cst', g2)])
                        for q4 in range(4):
                            u0 = g4 * 16 + q4 * 4
                            ps, pk = psbig()
                            TRB([(ps[:, i * 128:(i + 1) * 128], Cst[:, g2 * 16 + q4 * 4 + i, :]) for i in range(4)], [('Cst', g2)], [pk])
                            CPalt(CTs[:, u0:u0 + 4, 0:128], ps[:, 0:512].rearrange("p (u k) -> p u k", k=128), [pk], [('CTs', u0 // 4)])
                    to_fm(lambda j: nst[:, :], I['sn'][l].rearrange("s h k -> (s h) k"), NS * 4, 128, [nst])
                    CP('dve', CTs[:, :, 128], nst[:], [nst] + [('CTs', u) for u in range(NS)], [('CTs', u) for u in range(NS)])
                    P.dma('sp', m_s[:], I['sm'][l:l + 1, :].partition_broadcast(128), writes=[m_s])
                    CP('act', CTbC, CTs[:, :, 0:128], [('CTs', u) for u in range(NS)], [('Cst', 0), ('Cst', 1)])
                    CP('act', CTbn[:], CTs[:, :, 128], [('CTs', u) for u in range(NS)], [CTbn])
                MSET('dve', vtm[:, :, :, 128:129], 1.0, [vtm])
                slC = load_slab(I['w_in'][l], 0, NCH, SLABS_IN[2][0], SLABS_IN[2][1])
                slD = load_slab(I['w_in'][l], 0, NCH, SLABS_IN[3][0], SLABS_IN[3][1])
                for j in range(8):
                    ps, pk = dense_tile(slC, j * 128, 128, NT)
                    po = 0
                    for (off, nseg, L, kind) in groups:
                        qv = qkp[:, j, po:po + nseg * (L + 3)].rearrange("p (s t) -> p s t", t=L + 3)
                        pv = ps[:, off:off + nseg * L].rearrange("p (s t) -> p s t", t=L)
                        CP('act', qv[:, :, 3:L + 3], pv, [pk], [('qkp', j)])
                        if kind == 'p':
                            CP('dve', qv[:, 0, 0:3], cvc[:, l, j, :], [cvc], [('qkp', j)])
                            CP('dve', cvc[:, l, j, :], qv[:, 0, L:L + 3], [('qkp', j)], [cvc])
                        else:
                            CP('dve', qv[:, :, 0:3], cvin[:, j, :].rearrange("p (s t) -> p s t", t=3), [cvin], [('qkp', j)])
                            CP('dve', cvout[:, j, :].rearrange("p (s t) -> p s t", t=3), qv[:, :, L:L + 3], [('qkp', j)], [cvout])
                        av = cacc[:, off:off + nseg * L].rearrange("p (s t) -> p s t", t=L)
                        TS('dve', av, qv[:, :, 0:L], pcol(l, 'cw', 0 * 8 + j), ALU.mult, [('qkp', j), pc], [cacc],
                           s2=pcol(l, 'cb', j), op1=ALU.add)
                        for tp in range(1, 4):
                            STT('dve', av, qv[:, :, tp:tp + L], pcol(l, 'cw', tp * 8 + j), av, ALU.mult, ALU.add, [('qkp', j), pc, cacc], [cacc])
                        po += nseg * (L + 3)
                    if j < 4:
                        ACT(qb[:, j, :], cacc[:], AF.Silu, [cacc], [('qb', j)])
                    else:
                        ACT(kf[:, j - 4, :], cacc[:], AF.Silu, [cacc], [('kf', j - 4)])
                        TS('dve', kf[:, j - 4, :], kf[:, j - 4, :], 128.0 ** -0.5, ALU.mult, [('kf', j - 4)], [('kf', j - 4)])
                        CP('act', kb[:, j - 4, :], kf[:, j - 4, :], [('kf', j - 4)], [('kb', j - 4)])
                if blk_i == 0:
                    from_fm(O['ocv'][l], lambda j: cvout[:, j, :], NS * 3, D, [cvout])
                for ci, ch in enumerate(chunks):
                    ps, pk = psbig()
                    MM(ps[0:ch.L, 0:512], [(uT[:, c, ch.off:ch.off + ch.L], slD[:, c, 0:512]) for c in range(NCH)], [slD] + UT_ALL, [pk])
                    CPalt(vtm[0:ch.L, ci, :, 0:128], ps[0:ch.L, 0:512].rearrange("p (h v) -> p h v", v=128), [pk], [vtm])
                for j in range(4):
                    ps, pk = dense_tile(slD, 512 + j * 128, 128, NT)
                    ACT(sgo[:, j, :], ps[:, 0:NT], AF.Sigmoid, [pk], [('sgo', j)])
                ps, pk = psbig()
                MM(ps[0:8, 0:NT], [(slD[:, c, 1024:1032], uT[:, c, 0:NT]) for c in range(NCH)], [slD] + UT_ALL, [pk])
                CP('act', gsb[:], ps[0:8, 0:NT], [pk], [gsb])
                if debug and l == 0 and blk_i == 0:
                    dump('qb', qb[:], [128, 4, NT], [('qb', j) for j in range(4)])
                ctm = [{n: T('m%s%d' % (n, i), [128, w_], dt_, sc) for (n, w_, dt_) in
                        [('t3', 512, F32), ('AT', 512, F32), ('qt', 256, F32),
                         ('aqk', 512, BF16), ('qtb', 512, BF16), ('cco', 64, F32), ('wco', 64, F32), ('wtmp', 64, F32)]} for i in range(2)]
                for d_ in ctm:
                    d_['den'] = d_['t3']
                for h in range(4):
                    ps, pk = psbig()
                    MM(ps[:, 0:NT], [(C['ident'][0:8, h:h + 1].to_broadcast([8, 128]), gsb[:])], ['cst', gsb], [pk])
                    TS('dve', igb[:], ps[:, 0:NT], gb[:, l * 8 + h:l * 8 + h + 1], ALU.add, [pk, gb], [igb])
                    ps, pk = psbig()
                    MM(ps[:, 0:NT], [(C['ident'][0:8, 4 + h:5 + h].to_broadcast([8, 128]), gsb[:])], ['cst', gsb], [pk])
                    ACT(sp_[:], ps[:, 0:NT], AF.Exp, [pk, ngb], [sp_], bias=ngb[:, l * 8 + 4 + h:l * 8 + 5 + h], scale=-1.0)
                    ACT(sp_[:], sp_[:], AF.Ln, [sp_], [sp_], bias=1.0)
                    for ch in chunks:
                        SCAN(em4[:, h, ch.off:ch.off + ch.L], C['ones'][:, 0:ch.L], sp_[:, ch.off:ch.off + ch.L], 0.0, ALU.mult, ALU.subtract,
                             ['cst', sp_], [('em4', h)])
                    TT('dve', cc4[:, h, :], igb[:], em4[:, h, :], ALU.subtract, [igb, ('em4', h)], [('cc4', h)])
                for ci, ch in enumerate(chunks):
                    for h in range(4):
                        L = ch.L; o = ch.off
                        if ch.kind == 'p':
                            m_in = mP[:, l, h:h + 1]; m_key = ('mP', h); m_out = m_in; mo_key = ('mP', h)
                        else:
                            u = ch.seq * 4 + h
                            m_in = m_s[:, u:u + 1]; m_key = 'm_s'; m_out = mo_s[:, u:u + 1]; mo_key = ('mo_s', u)
                        SCAN(G4[:, h, o:o + L], cc4[:, h, o:o + L], cc4[:, h, o:o + L], m_in, ALU.max, ALU.max, [('cc4', h), m_key], [('G4', h)])
                        ACT(sc4[:, h, o:o + L], G4[:, h, o:o + L], AF.Exp, [('G4', h), m_key], [('sc4', h)], bias=m_in, scale=-1.0)
                        TT('dve', m_out, em4[:, h, o + L - 1:o + L], G4[:, h, o + L - 1:o + L], ALU.add, [('em4', h), ('G4', h)], [mo_key])
                for h in range(4):
                    TT('dve', tmpb[:], em4[:, h, :], G4[:, h, :], ALU.add, [('em4', h), ('G4', h)], [tmpb])
                    ACT(em4[:, h, :], tmpb[:], AF.Exp, [tmpb], [('em4', h)], scale=-1.0)
                H4 = lambda n: [(n, h) for h in range(4)]

                mgroups = []
                for ci, ch in enumerate(chunks):
                    if mgroups and ch.kind == 's' and chunks[mgroups[-1][0]].kind == 's' and mgroups[-1][1] < 16:
                        mgroups[-1][1] += 1
                    else:
                        mgroups.append([ci, 1])

                def g_ctx(gi):
                    ci0, ns = mgroups[gi]
                    ch0 = chunks[ci0]
                    return ci0, ns, ch0.L, ch0.off, ctm[gi % 2]

                def CT_of(ch):
                    if ch.kind == 'p':
                        return CTp[l][:, :, :], kn(CTp[l])
                    return CTs[:, ch.seq * 4:(ch.seq + 1) * 4, :], ('CTs', ch.seq)

                def ml_A(gi):
                    ci0, ns, L, o0, tm = g_ctx(gi)
                    W = ns * L; W4 = 4 * W

                    def src4(t):
                        return t.rearrange("p h (n t) -> p h n t", t=L)

                    def f4(ap):
                        return ap.rearrange("p (h n t) -> p h n t", h=4, n=ns)
                    mnegb = C['mneg%d' % L][0:L, 0:L].unsqueeze(1).unsqueeze(1).to_broadcast([L, 4, ns, L])
                    idb = C['ident'][0:L, 0:L].unsqueeze(1).unsqueeze(1).to_broadcast([L, 4, ns, L])
                    t3 = f4(tm['t3'][0:L, 0:W4]); cco = tm['cco']; wco = tm['wco']
                    cc3 = cco[0:L, 0:4 * ns].rearrange("p (h n) -> p h n", n=ns)
                    TT('dve', t3, src4(cc4[0:L, :, o0:o0 + W]), idb, ALU.mult, H4('cc4') + ['cst'], [tm['t3']])
                    P.op('dve', lambda e, o_=cc3, i_=t3: e.tensor_reduce(out=o_, in_=i_, axis=mybir.AxisListType.X, op=ALU.add),
                         [kn(tm['t3'])], [kn(cco)])
                    TT('dve', t3, mnegb, src4(G4[0:L, :, o0:o0 + W]), ALU.subtract, H4('G4') + ['cst'], [tm['t3']])
                    TT('dve', t3, t3, cc3.unsqueeze(3).to_broadcast([L, 4, ns, L]), ALU.add, [tm['t3'], cco], [tm['t3']])
                    ACT(tm['AT'][0:L, 0:W4], tm['t3'][0:L, 0:W4], AF.Exp, [tm['t3']], [tm['AT']])
                    ps, pk = psbig()
                    MMB([(ps[0:L, (h * ns + n) * L:(h * ns + n + 1) * L], kb[:, h, o0 + n * L:o0 + (n + 1) * L], qb[:, h, o0 + n * L:o0 + (n + 1) * L])
                         for h in range(4) for n in range(ns)], H4('kb') + H4('qb'), [pk])
                    TT('dve', tm['aqk'][0:L, 0:W4], tm['AT'][0:L, 0:W4], ps[0:L, 0:W4], ALU.mult, [tm['AT'], pk], [tm['aqk']])
                    qtt = tm['qtb'] if chunks[ci0].kind == 's' else tm['qt']
                    TT('dve', f4(qtt[:, 0:W4]), src4(qb[:, :, o0:o0 + W]), src4(sc4[:, :, o0:o0 + W]), ALU.mult, H4('qb') + H4('sc4'), [qtt])
                    TT('dve', tm['wtmp'][0:L, 0:4 * ns].rearrange("p (h n) -> p h n", n=ns), cc3, src4(G4[0:L, :, o0:o0 + W])[:, :, :, L - 1],
                       ALU.subtract, [cco] + H4('G4'), [tm['wtmp']])
                    ACT(wco[0:L, 0:4 * ns], tm['wtmp'][0:L, 0:4 * ns], AF.Exp, [tm['wtmp']], [wco])

                def ml_B(gi):
                    ci0, ns, L, o0, tm = g_ctx(gi)
                    W = ns * L; W4 = 4 * W
                    grpN = []; grpD = []; ckeys = []
                    smp = chunks[ci0].kind == 's'
                    qtt = tm['qtb'] if smp else tm['qt']
                    for h in range(4):
                        for n in range(ns):
                            ch_ = chunks[ci0 + n]
                            if smp:
                                u_ = ch_.seq * 4 + h
                                Cm = CTbC[:, u_, :]; ncol_ = CTbn[:, u_:u_ + 1]
                                for k_ in (('Cst', 0), ('Cst', 1), kn(CTbn)):
                                    if k_ not in ckeys:
                                        ckeys.append(k_)
                            else:
                                CTv, CT_key = CT_of(ch_)
                                Cm = CTv[:, h, 0:128]; ncol_ = CTv[:, h, 128:129]
                                if CT_key not in ckeys:
                                    ckeys.append(CT_key)
                            c0 = (h * ns + n) * L
                            grpN.append((None, c0, [(vtm[0:L, ci0 + n, h, 0:128], tm['aqk'][0:L, c0:c0 + L]), (Cm, qtt[:, c0:c0 + L])]))
                            grpD.append((None, c0, [(ones_bf[0:L, :], tm['aqk'][0:L, c0:c0 + L]),
                                                    (ncol_.to_broadcast([128, 128]), qtt[:, c0:c0 + L])]))
                    psN, pkN = psbig()
                    MMG([(psN[:, c0:c0 + L], prs) for (_, c0, prs) in grpN], [vtm, tm['aqk'], qtt] + ckeys, [pkN])
                    psD, pkD = psbig()
                    MMG([(psD[:, c0:c0 + L], prs) for (_, c0, prs) in grpD], [ones_bf, tm['aqk'], qtt] + ckeys, [pkD])
                    ACT(tm['den'][:, 0:W4], psD[:, 0:W4], AF.Abs, [pkD], [tm['den']])
                    den4 = tm['den'][:, 0:W4].rearrange("p (h n t) -> p h n t", h=4, n=ns)
                    TT('dve', den4, den4, em4[:, :, o0:o0 + W].rearrange("p h (n t) -> p h n t", t=L), ALU.max, [tm['den']] + H4('em4'), [tm['den']])
                    ACT(tm['den'][:, 0:W4], tm['den'][:, 0:W4], AF.Ln, [tm['den']], [tm['den']])
                    ACT(tm['den'][:, 0:W4], tm['den'][:, 0:W4], AF.Exp, [tm['den']], [tm['den']], scale=-1.0)
                    TT('dve', h4[:, :, o0:o0 + W].rearrange("p h (n t) -> p h n t", t=L), psN[:, 0:W4].rearrange("p (h n t) -> p h n t", h=4, n=ns),
                       den4, ALU.mult, [pkN, tm['den']], H4('h4'))

                kwt = [T('kwt%d' % i, [128, 512], BF16, sc) for i in range(2)]

                def ml_K(gi, n):
                    ci0, ns, L, o0, tm = g_ctx(gi)
                    ci = ci0 + n; o = chunks[ci].off
                    kw = kwt[ci % 2]
                    wc3 = tm['wco'][0:L, 0:4 * ns].rearrange("p (h n) -> p h n", n=ns)[:, :, n:n + 1]
                    ps, pk = psbig()
                    TRB([(ps[0:L, h * 128:(h + 1) * 128], kf[:, h, o:o + L]) for h in range(4)], H4('kf'), [pk])
                    TT('dve', kw[0:L, 0:512].rearrange("p (h c) -> p h c", c=128), ps[0:L, 0:512].rearrange("p (h c) -> p h c", c=128),
                       wc3.to_broadcast([L, 4, 128]), ALU.mult, [pk, tm['wco']], [kw])

                def ml_C(gi, n):
                    ci0, ns, L, o0, tm = g_ctx(gi)
                    ci = ci0 + n; ch = chunks[ci]; o = ch.off
                    CTv, CT_key = CT_of(ch)
                    kw = kwt[ci % 2]
                    psC, pkC = psbig()
                    MMB([(psC[:, h * 128:(h + 1) * 128], kw[0:L, h * 128:(h + 1) * 128], vtm[0:L, ci, h, 0:128]) for h in range(4)], [kw, vtm], [pkC])
                    psn, pkn = psbig()
                    MMB([(psn[:, h:h + 1], kw[0:L, h * 128:(h + 1) * 128], vtm[0:L, ci, h, 128:129]) for h in range(4)], [kw, vtm], [pkn])
                    TT('dve', CTv[:, :, 0:129], CTv[:, :, 0:129], sc4[:, :, o + L - 1:o + L].to_broadcast([128, 4, 129]), ALU.mult,
                       [CT_key] + H4('sc4'), [CT_key])
                    TT('dve', CTv[:, :, 0:128], CTv[:, :, 0:128], psC[:, 0:512].rearrange("p (h c) -> p h c", c=128), ALU.add, [CT_key, pkC], [CT_key])
                    TT('dve', CTv[:, :, 128], CTv[:, :, 128], psn[:, 0:4], ALU.add, [CT_key, pkn], [CT_key])

                ml_A(0)
                ml_K(0, 0)
                for gi in range(len(mgroups)):
                    if gi + 1 < len(mgroups):
                        ml_A(gi + 1)
                    ml_B(gi)
                    ns_ = mgroups[gi][1]
                    for n in range(ns_):
                        if n + 1 < ns_:
                            ml_K(gi, n + 1)
                        elif gi + 1 < len(mgroups):
                            ml_K(gi + 1, 0)
                        ml_C(gi, n)
                tq = [tmpb, igb, sp_, cacc]
                pks = []
                for h in range(4):
                    ps, pk = psbig()
                    MM(ps[:, 0:NT], [(C['ones'], h4[:, h, :])], ['cst', ('h4', h)], [pk])
                    pks.append((ps, pk))
                for h in range(4):
                    ps, pk = pks[h]
                    STT('dve', h4[:, h, :], ps[:, 0:NT], -1.0 / 128, h4[:, h, :], ALU.mult, ALU.add, [pk, ('h4', h)], [('h4', h)])
                for h in range(4):
                    ACT(tq[h][:], h4[:, h, :], AF.Square, [('h4', h)], [tq[h]])
                pks = []
                for h in range(4):
                    ps, pk = psbig()
                    MM(ps[:, 0:NT], [(C['ones'], tq[h][:])], ['cst', tq[h]], [pk])
                    pks.append((ps, pk))
                for h in range(4):
                    ps, pk = pks[h]
                    ACT(tq[h][:], ps[:, 0:NT], AF.Ln, [pk], [tq[h]], bias=ML_GN_EPS, scale=1.0 / 128)
                for h in range(4):
                    ACT(tq[h][:], tq[h][:], AF.Exp, [tq[h]], [tq[h]], scale=-0.5)
                for h in range(4):
                    STT('dve', h4[:, h, :], h4[:, h, :], pcol(l, 'mlg', h), tq[h][:], ALU.mult, ALU.mult, [('h4', h), pc, tq[h]], [('h4', h)])
                for h in range(4):
                    TT('dve', ybT[:, h, 0:NT], h4[:, h, :], sgo[:, h, :], ALU.mult, [('h4', h), ('sgo', h)], [('ybT', h)])
                if blk_i == 0:
                    for g4 in range(4):
                        g2 = g4 % 2
                        for q4 in range(4):
                            u0 = g4 * 16 + q4 * 4
                            ps, pk = psbig()
                            TRB([(ps[:, i * 128:(i + 1) * 128], CTs[:, u0 + i, 0:128]) for i in range(4)], [('CTs', u0 // 4)], [pk])
                            CPalt(Cst[:, g2 * 16 + q4 * 4:g2 * 16 + q4 * 4 + 4, :], ps[:, 0:512].rearrange("p (u k) -> p u k", k=128), [pk], [('Cst', g2)])
                        P.dma('sp', O['oC'][l, g4 * 4:(g4 + 1) * 4].rearrange("s h v k -> v s h k"),
                              Cst[:, g2 * 16:(g2 + 1) * 16, :].rearrange("p (s h) k -> p s h k", h=4), reads=[('Cst', g2)])
                    CP('dve', nst[:], CTs[:, :, 128], [('CTs', u) for u in range(NS)], [nst])
                    from_fm(O['on'][l], lambda j: nst[:, :], NS * 4, 128, [nst])
                    P.dma('sp', O['om'][l:l + 1, :], mo_s[0:1, :], reads=[('mo_s', u) for u in range(NS * 4)])
                if debug and l == 0 and blk_i == 0:
                    dump('ybT', ybT[:, :, 0:NT], [128, 4, NT], [('ybT', h) for h in range(4)])
                P.barrier()
                P.flush()

        def tail_phase(l, NT):
            with ExitStack() as sc:
                gab = T('gab', [128, 16, NT], F32, sc)
                mrg = T('mrg', [128, NCH, NT], BF16, sc)
                acc = T('acc', [128, NCH, NT], F32, sc)
                tmpm = [T('tmpm%d' % i, [128, NT], F32, sc) for i in range(2)]
                aT = T('aT', [128, NCH, NT], BF16, sc)
                for si in (4, 5):
                    sl = load_slab(I['w_in'][l], 0, NCH, SLABS_IN[si][0], SLABS_IN[si][1])
                    for j in range(8):
                        ps, pk = dense_tile(sl, j * 128, 128, NT)
                        ACT(gab[:, (si - 4) * 8 + j, :], ps[:, 0:NT], AF.Sigmoid, [pk], [('gab', (si - 4) * 8 + j)])
                i = st['slab']; st['slab'] = (i + 1) % 2
                sl = slabs[i]
                P.dma('pool', sl[:, 0:4, 0:D], I['p_a'][l].rearrange("(c p) n -> p c n", p=128), writes=[sl], slab=True)
                P.dma('pool', sl[:, 4:8, 0:D], I['p_b'][l].rearrange("(c p) n -> p c n", p=128), writes=[sl], slab=True)
                YA = [('yaT', h) for h in range(4)]; YB = [('ybT', h) for h in range(4)]
                for j in range(8):
                    psa, pka = psbig()
                    MM(psa[:, 0:NT], [(sl[:, c, j * 128:(j + 1) * 128], yaT[:, c, 0:NT]) for c in range(4)], [sl] + YA, [pka])
                    psb_, pkb = psbig()
                    MM(psb_[:, 0:NT], [(sl[:, 4 + c, j * 128:(j + 1) * 128], ybT[:, c, 0:NT]) for c in range(4)], [sl] + YB, [pkb])
                    t_ = tmpm[j % 2]
                    TT('dve', t_[:], gab[:, j, :], psa[:, 0:NT], ALU.mult, [('gab', j), pka], [t_])
                    TT('dve', gab[:, 8 + j, :], gab[:, 8 + j, :], psb_[:, 0:NT], ALU.mult, [('gab', 8 + j), pkb], [('gab', 8 + j)])
                    TT('dve', mrg[:, j, :], t_[:], gab[:, 8 + j, :], ALU.add, [t_, ('gab', 8 + j)], [('mrg', j)])
                MR = [('mrg', j) for j in range(8)]
                sl = load_slab(I['w_out'][l], 0, NCH, 0, D)
                for j in range(8):
                    ps, pk = dense_tile(sl, j * 128, 128, NT, rhs=mrg, rkeys=MR)
                    CPalt(acc[:, j, :], ps[:, 0:NT], [pk], [acc])
                rmsnorm_add(acc, NT, l, 'post1', sc)
                rmsnorm_u(xT, NT, l, 'pre2', sc)
                AT_ = [('aT', j) for j in range(8)]
                for q in range(4):
                    slu = load_slab(I['w_ff_up'][l], 0, NCH, q * D, (q + 1) * D)
                    sld = load_slab(I['w_ff_down'][l], q * D, NCH, 0, D)
                    for j in range(8):
                        ps, pk = dense_tile(slu, j * 128, 128, NT)
                        t_ = tmpm[j % 2]
                        ACT(t_[:], ps[:, 0:NT], AF.Relu, [pk], [t_])
                        TT('dve', aT[:, j, :], t_[:], t_[:], ALU.mult, [t_], [('aT', j)])
                    for j in range(8):
                        ps, pk = dense_tile(sld, j * 128, 128, NT, rhs=aT, rkeys=AT_)
                        if q == 0:
                            CPalt(acc[:, j, :], ps[:, 0:NT], [pk], [acc])
                        else:
                            TT('dve', acc[:, j, :], acc[:, j, :], ps[:, 0:NT], ALU.add, [acc, pk], [acc])
                rmsnorm_add(acc, NT, l, 'post2', sc)
                P.barrier()
                P.flush()

        def mkchunks(groups):
            chunks = []
            po = 0
            for (off, nseg, L, kind) in groups:
                for s_ in range(nseg):
                    ch = Chunk()
                    ch.off = off + s_ * L; ch.L = L; ch.kind = kind; ch.seq = s_; ch.padoff = po + s_ * 2 * L
                    chunks.append(ch)
                po += nseg * 2 * L
            return chunks

        try:
          stage('setup')
          for blk_i in range(5):
              if blk_i == 0:
                  NT = NMETA + NS * TSL
                  groups = [(0, 1, NMETA, 'p'), (NMETA, NS, TSL, 's')]
                  x_to_fm(I['meta'], NMETA, 0)
                  x_to_fm(I['xs'], NS * TSL, NMETA)
              else:
                  NT = 512
                  groups = [(0, 8, 64, 'p')]
                  for tt_ in range(4):
                      r0 = (blk_i - 1) * 512 + tt_ * 128
                      x_to_fm(I['xp'][r0:r0 + 128, :], 128, tt_ * 128)
              chunks = mkchunks(groups)
              cgroups = [(0, 1, NT, 'p')] if blk_i > 0 else groups
              for l in range(DEPTH):
                  with ExitStack() as sc0:
                      rmsnorm_u(xT, NT, l, 'pre1', sc0)
                      if debug and l == 0 and blk_i == 0:
                          dump('uT', uT[:, :, 0:NT], [128, NCH, NT], UT_ALL)
                      P.barrier()
                      P.flush()
                  stage('norm1')
                  rwkv_phase(l, NT, cgroups, groups, chunks, blk_i)
                  stage('rwkv')
                  mlstm_phase(l, NT, cgroups, chunks, blk_i)
                  stage('mlstm')
                  tail_phase(l, NT)
                  stage('tail')
              if blk_i == 0:
                  x_from_fm(O['ys'], NS * TSL, NMETA)
              else:
                  for tt_ in range(4):
                      r0 = (blk_i - 1) * 512 + tt_ * 128
                      x_from_fm(O['yp'][r0:r0 + 128, :], 128, tt_ * 128)
              P.barrier()
              P.flush()

          for l in range(DEPTH):
              for hp in range(4):
                  ps, pk = pssmall()
                  TR(ps[:, 0:128], Sbd[l][hp][:], 128, [Sbd[l][hp]], [pk])
                  sg = stg[st['stg']]; st['stg'] ^= 1
                  CPalt(sg[:, 0:128], ps[:, 0:128], [pk], [sg])
                  for h2 in range(2):
                      P.dma('sp', O['pS'][l, hp * 2 + h2], sg[h2 * 64:(h2 + 1) * 64, h2 * 64:(h2 + 1) * 64], reads=[sg])
              from_fm(O['psh'][l], lambda j, l=l: shc[:, l, :], 14, 128, [shc])
              for h in range(4):
                  ps, pk = pssmall()
                  TR(ps[:, 0:128], CTp[l][:, h, 0:128], 128, [CTp[l]], [pk])
                  sg = stg[st['stg']]; st['stg'] ^= 1
                  CPalt(sg[:, 0:128], ps[:, 0:128], [pk], [sg])
                  P.dma('sp', O['pC'][l, h], sg[:, 0:128], reads=[sg])
                  P.dma('sp', O['pn'][l, h:h + 1, :].rearrange("a k -> k a"), CTp[l][:, h, 128:129], reads=[CTp[l]])
              P.dma('sp', O['pm'][l:l + 1, :], mP[0:1, l, :], reads=[('mP', h) for h in range(4)])
              sg = stg[st['stg']]; st['stg'] ^= 1
              for c in range(8):
                  ps, pk = pssmall()
                  TR(ps[0:3, 0:128], cvc[:, l, c, :], 128, [cvc], [pk])
                  CPalt(sg[0:3, c * 128:(c + 1) * 128], ps[0:3, 0:128], [pk], [sg])
              P.dma('sp', O['pcv'][l], sg[0:3, :], reads=[sg])
          stage('end')
        except StopBuild:
          pass
        P.barrier(engines=('pe', 'act', 'dve', 'pool', 'sp'), with_w=True)
        P.flush()
        print("instructions:", P.nins, "waits:", P.nwaits, {k: P.cnt[k] for k in ('pe', 'act', 'dve', 'pool')})
    return nc, dbg


def _consts():
    c = np.zeros((len(CONST_NAMES), 128, 128), np.float32)
    ix = {n: i for i, n in enumerate(CONST_NAMES)}
    c[ix['ident']] = np.eye(128, dtype=np.float32)
    c[ix['ones']] = 1.0
    c[ix['blk'], 0:64, 0:64] = 1.0
    c[ix['blk'], 64:128, 64:128] = 1.0
    for L in (64, 16, 8):
        s = np.arange(L)[:, None]
        t = np.arange(L)[None, :]
        lt = (s < t).astype(np.float32)
        le = (s <= t).astype(np.float32)
        for h in range(2):
            c[ix['mlt%d' % L], h * L:(h + 1) * L, h * L:(h + 1) * L] = lt
            c[ix['mgt%d' % L], h * L:(h + 1) * L, h * L:(h + 1) * L] = lt.T
            c[ix['mle%d' % L], h * L:(h + 1) * L, 0:L] = le
        c[ix['mneg%d' % L], 0:L, 0:L] = np.where(s <= t, 0.0, -30000.0)
    return np.ascontiguousarray(c.transpose(1, 0, 2).reshape(128, -1))


_CACHE = {}


def _prep_inputs(inp, n_cores=8):
    f = lambda a: np.ascontiguousarray(np.asarray(a, dtype=np.float32))
    prm = np.zeros((DEPTH, NPRM, 128), np.float32)
    for l in range(DEPTH):
        rows = [inp['pre1'][l], inp['post1'][l], inp['pre2'][l], inp['post2'][l], inp['rw_mu'][l], inp['rw_w0'][l], inp['rw_a0'][l],
                inp['rw_k_k'][l], inp['rw_k_a'][l], inp['rw_r_k'][l], inp['rw_gn_g'][l], inp['rw_gn_b'][l], inp['ml_conv_w'][l],
                inp['ml_conv_b'][l], inp['ml_gn_g'][l]]
        prm[l] = np.concatenate([np.asarray(r, np.float32).reshape(-1, 128) for r in rows], axis=0)
    gbias = np.concatenate([np.concatenate([np.asarray(inp['ml_i_bias'][l], np.float32), np.asarray(inp['ml_f_bias'][l], np.float32)])
                            for l in range(DEPTH)]).reshape(1, 16)
    shared = {'meta': f(inp['meta_tokens']), 'w_in': f(inp['w_in']), 'w_up': f(inp['rw_w_up']), 'a_up': f(inp['rw_a_up']),
              'g_up': f(inp['rw_g_up']), 'p_a': f(inp['p_a']), 'p_b': f(inp['p_b']), 'w_out': f(inp['w_out']),
              'w_ff_up': f(inp['w_ff_up']), 'w_ff_down': f(inp['w_ff_down']), 'prm': prm, 'gbias': gbias, 'consts': _consts()}
    maps = []
    xp = np.asarray(inp['x_prompt'], np.float32)
    xs = np.asarray(inp['x_sample'], np.float32)
    for i in range(n_cores):
        sl = slice(i * NS, (i + 1) * NS)
        m = dict(shared)
        m['xp'] = f(xp[i])
        m['xs'] = f(xs[sl].reshape(NS * TSL, D))
        m['sS'] = f(np.asarray(inp['state_rwkv_S'])[:, sl])
        m['ssh'] = f(np.asarray(inp['state_rwkv_shift'])[:, sl])
        m['sC'] = f(np.asarray(inp['state_mlstm_C'])[:, sl])
        m['sn'] = f(np.asarray(inp['state_mlstm_n'])[:, sl])
        m['sm'] = f(np.asarray(inp['state_mlstm_m'])[:, sl].reshape(DEPTH, NS * 4))
        m['scv'] = f(np.asarray(inp['state_mlstm_conv'])[:, sl].reshape(DEPTH, NS * 3, D))
        maps.append(m)
    return maps


def _gather(results, n_cores=8):
    R = results
    cat = lambda k, ax: np.concatenate([R[i][k] for i in range(n_cores)], axis=ax)
    y_prompt = np.stack([R[i]['yp'] for i in range(n_cores)])
    y_sample = np.concatenate([R[i]['ys'].reshape(NS, TSL, D) for i in range(n_cores)], axis=0)
    pS = np.stack([R[i]['pS'] for i in range(n_cores)], axis=1)
    psh = np.stack([R[i]['psh'].reshape(DEPTH, 1792) for i in range(n_cores)], axis=1)
    pC = np.stack([R[i]['pC'] for i in range(n_cores)], axis=1)
    pn = np.stack([R[i]['pn'] for i in range(n_cores)], axis=1)
    pm = np.stack([R[i]['pm'] for i in range(n_cores)], axis=1)
    pcv = np.stack([R[i]['pcv'] for i in range(n_cores)], axis=1)
    oS = cat('oS', 1)
    osh = cat('osh', 1)
    oC = cat('oC', 1)
    on = np.concatenate([R[i]['on'].reshape(DEPTH, NS, 4, 128) for i in range(n_cores)], axis=1)
    om = np.concatenate([R[i]['om'].reshape(DEPTH, NS, 4) for i in range(n_cores)], axis=1)
    ocv = np.concatenate([R[i]['ocv'].reshape(DEPTH, NS, 3, D) for i in range(n_cores)], axis=1)
    outs = (y_prompt, y_sample, pS, psh, pC, pn, pm, pcv, oS, osh, oC, on, om, ocv)
    return tuple(np.ascontiguousarray(o.astype(np.float32)) for o in outs)


def kernel(**inputs):
    if 'nc' not in _CACHE:
        _CACHE['nc'] = build(False)[0]
    nc = _CACHE['nc']
    maps = _prep_inputs(inputs)
    res = run_bass_kernel_spmd(nc, maps, core_ids=list(range(8)))
    return _gather(res.results)
```
